# Optimizing a Trainium2 kernel written in Bass

```python
import jax, jax.numpy as jnp
from jax import lax
import numpy as np

D_MODEL = 1024
BATCH = 4
SEQ = 8192
DEPTH = 1

MEM_LEN = 256
MIX_WIDTH = D_MODEL
HGRN_HEADS = 4
HGRN_WIDTH = MIX_WIDTH // 2
HGRN_HEAD_DIM = HGRN_WIDTH // HGRN_HEADS
HGRN_CHUNK = 32
MOBA_HEADS = 4
MOBA_WIDTH = MIX_WIDTH - HGRN_WIDTH
MOBA_HEAD_DIM = MOBA_WIDTH // MOBA_HEADS
MOBA_BLOCK = 256
MOBA_TOP_K = 3
MOBA_QUERY_CHUNK = 64
ROPE_THETA = 500000.0
ROT_DIM = MOBA_HEAD_DIM // 4
XATTN_HEADS = 4
XATTN_HEAD_DIM = D_MODEL // XATTN_HEADS
D_FF = 4 * D_MODEL
NORM_EPS = 1e-6
IN_SPLITS = (HGRN_WIDTH, 2 * HGRN_WIDTH, 3 * HGRN_WIDTH, 4 * HGRN_WIDTH,
             4 * HGRN_WIDTH + MOBA_WIDTH, 4 * HGRN_WIDTH + 2 * MOBA_WIDTH)
IN_COLS = 4 * HGRN_WIDTH + 3 * MOBA_WIDTH

kernel_name = "hymba_hgrn2_moba_xattn_layer"


def rms_norm(x, g):
    xf = x.astype(jnp.float32)
    y = xf * lax.rsqrt(jnp.mean(xf * xf, axis=-1, keepdims=True) + NORM_EPS)
    return (y * g.astype(jnp.float32)).astype(x.dtype)


def split_heads(t, n_heads):
    B, S, W = t.shape
    return t.reshape(B, S, n_heads, W // n_heads).transpose(0, 2, 1, 3)


def merge_heads(t):
    B, H, S, hd = t.shape
    return t.transpose(0, 2, 1, 3).reshape(B, S, H * hd)


def partial_rotary(t, positions):
    half = ROT_DIM // 2
    inv_freq = ROPE_THETA ** (-jnp.arange(half, dtype=jnp.float32) * 2.0 / ROT_DIM)
    ang = positions.astype(jnp.float32)[:, None] * inv_freq[None, :]
    cos, sin = jnp.cos(ang), jnp.sin(ang)
    tf = t.astype(jnp.float32)
    x1, x2 = tf[..., :half], tf[..., half:ROT_DIM]
    out = jnp.concatenate([x1 * cos - x2 * sin, x2 * cos + x1 * sin, tf[..., ROT_DIM:]], axis=-1)
    return out.astype(t.dtype)


def hgrn2_mixer(q, f_logit, inp, gate, lb, norm_g):
    B, S, _ = q.shape
    H, dh, C = HGRN_HEADS, HGRN_HEAD_DIM, HGRN_CHUNK
    n = S // C
    f = lb.astype(jnp.float32) + (1.0 - lb.astype(jnp.float32)) * jax.nn.sigmoid(f_logit.astype(jnp.float32))
    log_f = jnp.log(f)
    k = 1.0 - f

    def chunks(t):
        return split_heads(t.astype(jnp.float32), H).reshape(B, H, n, C, dh)

    qc, kc, vc, gc = chunks(q), chunks(k), chunks(inp), chunks(log_f)
    b = jnp.cumsum(gc, axis=3)
    b_last = b[..., -1:, :]
    q_t = qc * jnp.exp(b)
    k_t = kc * jnp.exp(-b)
    k_end = kc * jnp.exp(b_last - b)
    causal = jnp.tril(jnp.ones((C, C), dtype=bool))
    A = jnp.where(causal, jnp.einsum('bhncd,bhnsd->bhncs', q_t, k_t), 0.0)
    o_intra = jnp.einsum('bhncs,bhnsv->bhncv', A, vc)
    decay = jnp.exp(b_last[..., 0, :])

    def step(state, xs):
        q_n, k_n, v_n, dec_n = xs
        o_n = jnp.einsum('bhcd,bhdv->bhcv', q_n, state)
        new_state = dec_n[..., None] * state + jnp.einsum('bhsd,bhsv->bhdv', k_n, v_n)
        return new_state, o_n

    xs = (jnp.moveaxis(q_t, 2, 0), jnp.moveaxis(k_end, 2, 0),
          jnp.moveaxis(vc, 2, 0), jnp.moveaxis(decay, 2, 0))
    _, o_inter = lax.scan(step, jnp.zeros((B, H, dh, dh), jnp.float32), xs)
    o = (o_intra + jnp.moveaxis(o_inter, 0, 2)).reshape(B, H, S, dh)
    o = o * lax.rsqrt(jnp.mean(o * o, axis=-1, keepdims=True) + NORM_EPS)
    o = merge_heads(o) * norm_g.astype(jnp.float32) * jax.nn.silu(gate.astype(jnp.float32))
    return o.astype(q.dtype)


def moba_attention(q, k, v):
    B, H, S, hd = q.shape
    QC, BLK = MOBA_QUERY_CHUNK, MOBA_BLOCK
    n_blocks = -(-S // BLK)
    pad = n_blocks * BLK - S
    kb = jnp.pad(k, ((0, 0), (0, 0), (0, pad), (0, 0))).reshape(B, H, n_blocks, BLK, hd)
    vb = jnp.pad(v, ((0, 0), (0, 0), (0, pad), (0, 0))).reshape(B, H, n_blocks, BLK, hd)
    k_mean = jnp.mean(kb.astype(jnp.float32), axis=3)
    n_sel = min(MOBA_TOP_K, n_blocks)
    n_chunks = S // QC
    scale = hd ** -0.5
    b_idx = jnp.arange(B)[:, None, None, None]
    h_idx = jnp.arange(H)[None, :, None, None]
    block_ids = jnp.arange(n_blocks)
    slot_ids = jnp.arange(n_sel)
    q_chunks = jnp.moveaxis(q.reshape(B, H, n_chunks, QC, hd), 2, 0)

    def attend_chunk(args):
        qc, c = args
        start = c * QC
        blk = start // BLK
        q_pos = start + jnp.arange(QC)
        gate = jnp.einsum('bhqd,bhnd->bhqn', qc.astype(jnp.float32), k_mean)
        gate = jnp.where(block_ids < blk, gate, -jnp.inf)
        _, sel = lax.top_k(gate, n_sel)
        k_sel = kb[b_idx, h_idx, sel]
        v_sel = vb[b_idx, h_idx, sel]
        k_own = lax.dynamic_index_in_dim(kb, blk, axis=2, keepdims=False)
        v_own = lax.dynamic_index_in_dim(vb, blk, axis=2, keepdims=False)
        s_sel = jnp.einsum('bhqd,bhqnkd->bhqnk', qc, k_sel,
                           preferred_element_type=jnp.float32) * scale
        s_sel = jnp.where((slot_ids < blk)[:, None], s_sel, -jnp.inf)
        s_own = jnp.einsum('bhqd,bhkd->bhqk', qc, k_own,
                           preferred_element_type=jnp.float32) * scale
        k_pos = blk * BLK + jnp.arange(BLK)
        s_own = jnp.where(k_pos[None, :] <= q_pos[:, None], s_own, -jnp.inf)
        scores = jnp.concatenate([s_sel.reshape(B, H, QC, n_sel * BLK), s_own], axis=-1)
        p = jax.nn.softmax(scores, axis=-1).astype(v.dtype)
        p_sel = p[..., :n_sel * BLK].reshape(B, H, QC, n_sel, BLK)
        p_own = p[..., n_sel * BLK:]
        return (jnp.einsum('bhqnk,bhqnkd->bhqd', p_sel, v_sel)
                + jnp.einsum('bhqk,bhkd->bhqd', p_own, v_own))

    out = lax.map(attend_chunk, (q_chunks, jnp.arange(n_chunks)))
    return jnp.moveaxis(out, 0, 2).reshape(B, H, S, hd)


def cross_attention(h, mem_n, w_q, w_k, w_v, w_o):
    B, S, _ = h.shape
    q = split_heads(h @ w_q, XATTN_HEADS)
    k = split_heads(mem_n @ w_k, XATTN_HEADS)
    v = split_heads(mem_n @ w_v, XATTN_HEADS)
    s = jnp.einsum('bhsd,bhmd->bhsm', q, k, preferred_element_type=jnp.float32) * XATTN_HEAD_DIM ** -0.5
    p = jax.nn.softmax(s, axis=-1).astype(v.dtype)
    o = jnp.einsum('bhsm,bhmd->bhsd', p, v)
    return merge_heads(o) @ w_o


def setup_inputs(seed: int = 0) -> dict:
    key = jax.random.key(seed)
    ks = jax.random.split(key, 20)

    def normal(k, shape, scale):
        return jax.random.normal(k, shape, jnp.float32) * scale

    def gain(k, shape):
        return 1.0 + 0.05 * jax.random.normal(k, shape, jnp.float32)

    return {
        "x": normal(ks[0], (BATCH, SEQ, D_MODEL), 1.0),
        "mem": normal(ks[1], (BATCH, MEM_LEN, D_MODEL), 1.0),
        "norm_mix": gain(ks[2], (DEPTH, D_MODEL)),
        "w_in": normal(ks[3], (DEPTH, D_MODEL, IN_COLS), D_MODEL ** -0.5),
        "lb_logits": gain(ks[4], (DEPTH + 1, HGRN_WIDTH)),
        "hgrn_norm": gain(ks[5], (DEPTH, HGRN_WIDTH)),
        "w_out": normal(ks[6], (DEPTH, MIX_WIDTH, D_MODEL), MIX_WIDTH ** -0.5),
        "norm_xattn": gain(ks[7], (DEPTH, D_MODEL)),
        "norm_mem": gain(ks[8], (DEPTH, D_MODEL)),
        "w_xq": normal(ks[9], (DEPTH, D_MODEL, D_MODEL), D_MODEL ** -0.5),
        "w_xk": normal(ks[10], (DEPTH, D_MODEL, D_MODEL), D_MODEL ** -0.5),
        "w_xv": normal(ks[11], (DEPTH, D_MODEL, D_MODEL), D_MODEL ** -0.5),
        "w_xo": normal(ks[12], (DEPTH, D_MODEL, D_MODEL), D_MODEL ** -0.5),
        "norm_mlp": gain(ks[13], (DEPTH, D_MODEL)),
        "w_ff1": normal(ks[14], (DEPTH, D_MODEL, D_FF), D_MODEL ** -0.5),
        "w_ff2": normal(ks[15], (DEPTH, D_FF, D_MODEL), D_FF ** -0.5),
        "norm_final": gain(ks[16], (D_MODEL,)),
    }


def reference(x, mem, norm_mix, w_in, lb_logits, hgrn_norm, w_out, norm_xattn, norm_mem,
              w_xq, w_xk, w_xv, w_xo, norm_mlp, w_ff1, w_ff2, norm_final):
    S = x.shape[1]
    positions = jnp.arange(S, dtype=jnp.int32)
    lb_table = jnp.cumsum(jax.nn.softmax(lb_logits.astype(jnp.float32), axis=0), axis=0)
    for l in range(DEPTH):
        h = rms_norm(x, norm_mix[l])
        proj = h @ w_in[l]
        hq, hf, hi, hg, mq, mk, mv = jnp.split(proj, IN_SPLITS, axis=-1)
        o_hgrn = hgrn2_mixer(hq, hf, hi, hg, lb_table[l], hgrn_norm[l])
        mq = partial_rotary(split_heads(mq, MOBA_HEADS), positions)
        mk = partial_rotary(split_heads(mk, MOBA_HEADS), positions)
        mv = split_heads(mv, MOBA_HEADS)
        o_moba = merge_heads(moba_attention(mq, mk, mv))
        mixed = jnp.concatenate([o_hgrn, o_moba.astype(o_hgrn.dtype)], axis=-1)
        x = x + mixed @ w_out[l]
        h = rms_norm(x, norm_xattn[l])
        mem_n = rms_norm(mem, norm_mem[l])
        x = x + cross_attention(h, mem_n, w_xq[l], w_xk[l], w_xv[l], w_xo[l])
        h = rms_norm(x, norm_mlp[l])
        x = x + jnp.square(jax.nn.relu(h @ w_ff1[l])) @ w_ff2[l]
    return rms_norm(x, norm_final)
```

```python
import numpy as np
from contextlib import ExitStack
import ml_dtypes
import concourse.bass as bass
import concourse.mybir as mybir
from concourse.bass_utils import run_bass_kernel_spmd

F32 = mybir.dt.float32
BF16 = mybir.dt.bfloat16
AF = mybir.ActivationFunctionType
ALU = mybir.AluOpType

ENGS = ("pe", "act", "dve", "pool", "sp")
EPS = 1e-6
NEG = -1.0e4


class Buf:
    __slots__ = ("name", "last_w", "readers", "excl")

    def __init__(self, name=""):
        self.name = name
        self.excl = False
        self.last_w = None
        self.readers = []


class Op:
    __slots__ = ("eng", "fn", "deps", "is_dma", "lane", "lane_val", "pos", "signal", "sigval", "waits", "idx")


class Sched:
    def __init__(self, nc):
        self.nc = nc
        self.ops = []
        self.streams = {e: [] for e in ENGS}
        self.lane_count = {}
        self.lane_last = {}
        self.out_lanes = set()
        self.bar = []
        self.bar_pending = {e: False for e in ENGS}

    def barrier(self):
        bar = []
        for e in ENGS:
            for op in reversed(self.streams[e]):
                if not op.is_dma:
                    bar.append(op.idx)
                    break
        for lane, idx in self.lane_last.items():
            bar.append(idx)
        self.bar = bar
        self.bar_pending = {e: True for e in ENGS}

    def _record(self, eng, fn, reads, writes, is_dma=False, lane=None):
        op = Op()
        op.eng = eng
        op.fn = fn
        op.is_dma = is_dma
        op.lane = lane
        op.signal = False
        op.sigval = 0
        op.waits = []
        op.idx = len(self.ops)
        if any(b.excl for b in reads):
            writes = writes + [b for b in reads if b.excl and b not in writes]
        deps = {}
        for b in reads:
            if b.last_w is not None:
                deps[b.last_w] = True
        for b in writes:
            if b.last_w is not None:
                deps.setdefault(b.last_w, False)
            for r in b.readers:
                deps.setdefault(r, False)
        if self.bar_pending[eng]:
            for i in self.bar:
                deps.setdefault(i, False)
            self.bar_pending[eng] = False
        op.deps = deps
        for b in reads:
            b.readers.append(op.idx)
        for b in writes:
            b.last_w = op.idx
            b.readers = []
        if is_dma:
            c = self.lane_count.get(lane, 0) + 1
            self.lane_count[lane] = c
            op.lane_val = 16 * c
            self.lane_last[lane] = op.idx
        op.pos = len(self.streams[eng])
        self.streams[eng].append(op)
        self.ops.append(op)
        return op

    _cap = None

    def begin(self):
        self._cap = []

    def end(self):
        l = self._cap
        self._cap = None
        return l

    def merged(self, lists):
        items = []
        for li, l in enumerate(lists):
            n = len(l)
            for i, it in enumerate(l):
                items.append(((i + 0.5) / n, li, i, it))
        items.sort(key=lambda x: (x[0], x[1], x[2]))
        for _, _, _, it in items:
            if it[0] == "op":
                self.op(*it[1:])
            else:
                self.dma(*it[1:])

    def op(self, eng, fn, reads=(), writes=()):
        if self._cap is not None:
            self._cap.append(("op", eng, fn, list(reads), list(writes)))
            return None
        return self._record(eng, fn, list(reads), list(writes))

    def dma(self, eng, fn, reads=(), writes=(), lane=None, is_out=False):
        if self._cap is not None:
            self._cap.append(("dma", eng, fn, list(reads), list(writes), lane, is_out))
            return None
        if is_out:
            self.out_lanes.add(lane)
        return self._record(eng, fn, list(reads), list(writes), is_dma=True, lane=lane)

    def resolve(self):
        seen = {e: {} for e in ENGS}
        seen_lane = {e: {} for e in ENGS}
        for op in self.ops:
            E = op.eng
            need = {}
            need_lane = {}
            for di, is_raw in op.deps.items():
                d = self.ops[di]
                if d.is_dma:
                    if seen_lane[E].get(d.lane, 0) < d.lane_val:
                        need_lane[d.lane] = max(need_lane.get(d.lane, 0), d.lane_val)
                else:
                    if d.eng == E and E == "pe":
                        continue
                    if seen[E].get(d.eng, -1) < d.pos:
                        need[d.eng] = max(need.get(d.eng, -1), d.pos)
            for Ed, pos in need.items():
                seen[E][Ed] = pos
                self.streams[Ed][pos].signal = True
                op.waits.append(("c", Ed, pos))
            for lane, val in need_lane.items():
                seen_lane[E][lane] = val
                op.waits.append(("d", lane, val))
        for e in ENGS:
            c = 0
            for op in self.streams[e]:
                if (not op.is_dma) and op.signal:
                    c += 1
                    op.sigval = c

    def emit(self, stack):
        nc = self.nc
        self.resolve()
        esem = {e: stack.enter_context(nc.semaphore("cs_" + e)) for e in ENGS}
        lsem = {}
        for lane in self.lane_count:
            lsem[lane] = stack.enter_context(nc.semaphore("dl_" + lane))
        block = stack.enter_context(nc.Block())
        streams = self.streams
        out_lanes = self.out_lanes
        lane_count = self.lane_count

        def run(eng_name, e):
            for op in streams[eng_name]:
                for w in op.waits:
                    if w[0] == "c":
                        e.wait_ge(esem[w[1]], streams[w[1]][w[2]].sigval)
                    else:
                        e.wait_ge(lsem[w[1]], w[2])
                ins = op.fn(e)
                if op.is_dma:
                    ins.then_inc(lsem[op.lane], 16)
                elif op.signal:
                    ins.then_inc(esem[eng_name], 1)
            if eng_name == "sp":
                for lane in sorted(out_lanes):
                    e.wait_ge(lsem[lane], 16 * lane_count[lane])

        @block.tensor
        def _(e):
            run("pe", e)

        @block.scalar
        def _(e):
            run("act", e)

        @block.vector
        def _(e):
            run("dve", e)

        @block.gpsimd
        def _(e):
            run("pool", e)

        @block.sync
        def _(e):
            run("sp", e)


class T:
    __slots__ = ("ap", "b", "flat")

    def __init__(self, ap, b=None, name=""):
        self.ap = ap
        self.flat = ap
        self.b = b if b is not None else Buf(name)


def _bufs(ts):
    return [t.b if isinstance(t, T) else t for t in ts]


def build_nc(debug=False):
    nc = bass.Bass("TRN2", target_bir_lowering=False)

    def din(name, shape, dt=F32):
        return nc.dram_tensor(name, list(shape), dt, kind="ExternalInput").ap()

    xall = din("xall", [8192, 1024])
    mem = din("mem", [256, 1024])
    w_in = din("w_in", [1024, 3584])
    w_out = din("w_out", [1024, 1024])
    w_xq = din("w_xq", [1024, 1024])
    w_xk = din("w_xk", [1024, 1024])
    w_xv = din("w_xv", [1024, 1024])
    w_xo = din("w_xo", [1024, 1024])
    w_ff1 = din("w_ff1", [1024, 4096])
    w_ff2 = din("w_ff2", [4096, 1024])
    norm_mix = din("norm_mix", [1, 1024])
    norm_xattn = din("norm_xattn", [1, 1024])
    norm_mem = din("norm_mem", [1, 1024])
    norm_mlp = din("norm_mlp", [1, 1024])
    norm_final = din("norm_final", [1, 1024])
    hgrn_norm = din("hgrn_norm", [1, 512])
    lb_logits = din("lb_logits", [2, 512])
    rot = din("rot", [8192, 32])
    vbias = din("vbias", [1, 512])
    cU = din("cU", [128, 128])
    cL = din("cL", [128, 128])
    cMask4 = din("cMask4", [128, 512])
    cTriB = din("cTriB", [128, 128], BF16)
    cIdent = din("cIdent", [128, 128], BF16)
    out = nc.dram_tensor("out", [4096, 1024], F32, kind="ExternalOutput").ap()
    mixT = nc.dram_tensor("mixT", [1024, 4096], BF16, kind="ExternalOutput" if debug else "Internal").ap()
    ff1b = nc.dram_tensor("ff1b", [8, 128, 4096], BF16).ap()
    ff2b = nc.dram_tensor("ff2b", [8, 128, 4096], BF16).ap()

    S = Sched(nc)
    st = ExitStack()
    AR_ELEMS = 104000
    arena = st.enter_context(nc.sbuf_tensor("arena", [128, AR_ELEMS], BF16))
    banks = [st.enter_context(nc.psum_tensor("pb%d" % i, [128, 1024], BF16)) for i in range(8)]

    class PB:
        def __init__(self, t, i):
            self.bf = t[:, :]
            self.f = t[:, :].bitcast(F32)
            self.b = Buf("pb%d" % i)
            self.b.excl = True

    P = [PB(banks[i], i) for i in range(8)]

    class Arena:
        def __init__(self):
            self.off = 0

        def alloc(self, shape, dt, name=""):
            n = 1
            for s_ in shape:
                n *= s_
            nel = n * (2 if dt == F32 else 1)
            nel = (nel + 31) // 32 * 32
            assert self.off + nel <= AR_ELEMS, ("arena overflow", name, self.off, nel)
            v = arena[:, self.off:self.off + nel]
            self.off += nel
            if dt == F32:
                v = v.bitcast(F32)
                v = v[:, 0:n]
            else:
                v = v[:, 0:n]
            flat = v
            if len(shape) == 2:
                v = v.rearrange("p (a b) -> p a b", a=shape[0])
            elif len(shape) == 3:
                v = v.rearrange("p (a b c) -> p a b c", a=shape[0], b=shape[1])
            t_ = T(v, name=name)
            t_.flat = flat
            return t_

    A = Arena()

    def mm(out_ap, lhsT, rhs, start, stop, r, w):
        S.op("pe", lambda e: e.matmul(out_ap, lhsT=lhsT, rhs=rhs, start=start, stop=stop), _bufs(r), _bufs(w))

    def tr(out_ap, in_ap, r, w):
        S.op("pe", lambda e: e.transpose(out=out_ap, in_=in_ap, identity=ident.ap), _bufs(r) + [ident.b], _bufs(w))

    def act(out_ap, in_ap, func, r, w, bias=None, scale=None, accum=None):
        kw = {}
        if bias is not None:
            kw["bias"] = bias
        if scale is not None:
            kw["scale"] = scale
        if accum is not None:
            kw["accum_out"] = accum
        S.op("act", lambda e: e.activation(out=out_ap, in_=in_ap, func=func, **kw), _bufs(r), _bufs(w))

    def tt(eng, out_ap, in0, in1, op, r, w):
        S.op(eng, lambda e: e.tensor_tensor(out=out_ap, in0=in0, in1=in1, op=op), _bufs(r), _bufs(w))

    def ts(eng, out_ap, in0, s1, s2, op0, op1, r, w):
        if op1 is None:
            S.op(eng, lambda e: e.tensor_scalar(out=out_ap, in0=in0, scalar1=s1, scalar2=None, op0=op0), _bufs(r), _bufs(w))
        else:
            S.op(eng, lambda e: e.tensor_scalar(out=out_ap, in0=in0, scalar1=s1, scalar2=s2, op0=op0, op1=op1), _bufs(r), _bufs(w))

    def stt(eng, out_ap, in0, scalar, in1, op0, op1, r, w):
        S.op(eng, lambda e: e.scalar_tensor_tensor(out=out_ap, in0=in0, scalar=scalar, in1=in1, op0=op0, op1=op1), _bufs(r), _bufs(w))

    def cp(eng, out_ap, in_ap, r, w):
        if eng == "act":
            act(out_ap, in_ap, AF.Copy, r, w)
        else:
            S.op(eng, lambda e: e.tensor_copy(out=out_ap, in_=in_ap), _bufs(r), _bufs(w))

    def recip(out_ap, in_ap, r, w):
        S.op("dve", lambda e: e.reciprocal(out=out_ap, in_=in_ap), _bufs(r), _bufs(w))

    def vmax(out_ap, in_ap, r, w):
        S.op("dve", lambda e: e.max(out=out_ap, in_=in_ap), _bufs(r), _bufs(w))

    def memset(eng, ap, val, w):
        S.op(eng, lambda e: e.memset(ap, val), [], _bufs(w))

    def dma(eng, out_ap, in_ap, r, w, lane, is_out=False):
        S.dma(eng, lambda e: e.dma_start(out=out_ap, in_=in_ap), _bufs(r), _bufs(w), lane=lane, is_out=is_out)

    ident = A.alloc([128], BF16, "ident")
    triB = A.alloc([128], BF16, "triB")
    Uc = A.alloc([128], F32, "Uc")
    Lc = A.alloc([128], F32, "Lc")
    mask4 = A.alloc([512], F32, "mask4")
    ones_f = A.alloc([8], F32, "ones_f")
    ones_b = A.alloc([128], BF16, "ones_b")
    epsb = A.alloc([2], F32, "epsb")
    dma("sp", ident.ap, cIdent[:, :], [], [ident], "c_ident")
    dma("sp", triB.ap, cTriB[:, :], [], [triB], "c_trib")
    dma("sp", Uc.ap, cU[:, :], [], [Uc], "c_u")
    dma("sp", Lc.ap, cL[:, :], [], [Lc], "c_l")
    dma("sp", mask4.ap, cMask4[:, :], [], [mask4], "c_m4")
    memset("dve", ones_f.ap, 1.0, [ones_f])
    memset("dve", ones_b.ap, 1.0, [ones_b])
    memset("dve", epsb.ap, EPS, [epsb])
    A_persist = A.off

    def norm_tile(xt, g_bc, junk, ssb, xs, pb, dst_ap, dst_bufs, evac_eng, d_model=1024):
        act(junk.ap, xt.ap, AF.Square, [xt], [junk, ssb], accum=ssb.ap[:, 0:1])
        act(ssb.ap[:, 1:2], ssb.ap[:, 0:1], AF.Ln, [ssb, epsb], [ssb], bias=epsb.ap[:, 0:1], scale=1.0 / d_model)
        act(ssb.ap[:, 2:3], ssb.ap[:, 1:2], AF.Exp, [ssb], [ssb], scale=-0.5)
        stt("dve", xs.ap, xt.ap, ssb.ap[:, 2:3], g_bc.ap, ALU.mult, ALU.mult, [xt, ssb, g_bc], [xs])
        for c in range(8):
            tr(pb.bf[:, c * 128:(c + 1) * 128], xs.ap[:, c * 128:(c + 1) * 128], [xs], [pb.b])
        cp(evac_eng, dst_ap, pb.bf[:, 0:1024].rearrange("p (c t) -> p c t", c=8), [pb.b], dst_bufs)

    stg = [A.alloc([4096], BF16, "stg%d" % i) for i in range(2)]
    ff1b_b = [Buf("ff1b%d" % i) for i in range(8)]
    ff2b_b = [Buf("ff2b%d" % i) for i in range(8)]
    w_ff1_v = w_ff1.rearrange("(c p) n -> p c n", p=128)

    def stage_cast(k):
        s_ = stg[k % 2]
        if k < 8:
            dma("pool", s_.ap.rearrange("p (c n) -> p c n", c=8), w_ff1_v[:, :, k * 512:(k + 1) * 512], [], [s_], "stg%d" % (k % 2))
        else:
            j2 = k - 8
            dma("pool", s_.ap.rearrange("p (a n) -> p a n", a=4),
                w_ff2[j2 * 512:(j2 + 1) * 512, :].rearrange("(a p) n -> p a n", p=128), [], [s_], "stg%d" % (k % 2))

    def stage_store(k):
        s_ = stg[k % 2]
        if k < 8:
            dma("sp", ff1b[k], s_.ap, [s_], [ff1b_b[k]], "stgo%d" % (k % 2))
        else:
            dma("sp", ff2b[k - 8], s_.ap, [s_], [ff2b_b[k - 8]], "stgo%d" % (k % 2))

    mix_b = [Buf("mix%d" % i) for i in range(32)]
    g_mix = A.alloc([1024], F32, "g_mix")
    dma("sp", g_mix.ap, norm_mix.partition_broadcast(128), [], [g_mix], "c_gmix")
    A_hm = A.off

    Wh = A.alloc([8, 2048], BF16, "Wh")
    w_in_v = w_in.rearrange("(c p) n -> p c n", p=128)
    for c in range(8):
        dma("pool", Wh.ap[:, c, :], w_in_v[:, c, 0:2048], [], [Wh], "w_h")
    lb_bc = A.alloc([512], F32, "lb")
    oml_bc = A.alloc([512], F32, "oml")
    hgn_bc = A.alloc([512], F32, "hgn")
    l1_bc = A.alloc([512], F32, "l1")
    dma("sp", lb_bc.ap, lb_logits[0:1, :].partition_broadcast(128), [], [lb_bc], "c_lb0")
    dma("sp", l1_bc.ap, lb_logits[1:2, :].partition_broadcast(128), [], [l1_bc], "c_lb1")
    dma("sp", hgn_bc.ap, hgrn_norm.partition_broadcast(128), [], [hgn_bc], "c_hgn")
    tt("dve", lb_bc.ap, l1_bc.ap, lb_bc.ap, ALU.subtract, [l1_bc, lb_bc], [lb_bc])
    act(lb_bc.ap, lb_bc.ap, AF.Exp, [lb_bc], [lb_bc])
    ts("dve", lb_bc.ap, lb_bc.ap, 1.0, None, ALU.add, None, [lb_bc], [lb_bc])
    recip(lb_bc.ap, lb_bc.ap, [lb_bc], [lb_bc])
    ts("dve", oml_bc.ap, lb_bc.ap, -1.0, 1.0, ALU.mult, ALU.add, [lb_bc], [oml_bc])

    xt = [A.alloc([1024], F32, "xt%d" % i) for i in range(2)]
    junk = A.alloc([1024], BF16, "junk")
    ssb = [A.alloc([4], F32, "ssb%d" % i) for i in range(2)]
    xs = [A.alloc([1024], BF16, "xs%d" % i) for i in range(2)]
    hT = [A.alloc([8, 128], BF16, "hT%d" % i) for i in range(2)]
    ez = [A.alloc([512], F32, "ez%d" % i) for i in range(2)]
    vv = [A.alloc([512], BF16, "v%d" % i) for i in range(4)]
    qs = [A.alloc([512], F32, "qs%d" % i) for i in range(2)]
    eg = [A.alloc([512], F32, "eg%d" % i) for i in range(2)]
    gr = [A.alloc([512], F32, "gr%d" % i) for i in range(2)]
    ff = A.alloc([512], F32, "f")
    logf = A.alloc([512], F32, "logf")
    kk = A.alloc([512], F32, "k")
    eb = A.alloc([512], F32, "eb")
    enb = A.alloc([512], F32, "enb")
    eend = A.alloc([512], F32, "eend")
    qts = [A.alloc([512], BF16, "qt%d" % i) for i in range(2)]
    kts = [A.alloc([512], BF16, "kt%d" % i) for i in range(2)]
    kend = [A.alloc([512], BF16, "kend%d" % i) for i in range(3)]
    qkT = [A.alloc([8, 128], BF16, "qkT%d" % i) for i in range(2)]
    ATm = [A.alloc([512], BF16, "ATm%d" % i) for i in range(2)]
    Sst = A.alloc([512], F32, "Sst")
    Sb = A.alloc([512], BF16, "Sb")
    dec = [A.alloc([4], F32, "dec%d" % i) for i in range(3)]
    gs = [A.alloc([512], F32, "gs%d" % i) for i in range(3)]
    ssq = A.alloc([12], F32, "ssq")
    junkC = A.alloc([128], BF16, "junkC")
    on = A.alloc([512], BF16, "on")
    oT = [A.alloc([4, 128], BF16, "oT%d" % i) for i in range(2)]
    memset("dve", Sst.ap, 0.0, [Sst])
    memset("dve", Sb.ap, 0.0, [Sb])

    def hs(h):
        return slice(h * 128, (h + 1) * 128)

    def stageA(t):
        local = t >= 32
        sl = t % 2
        dma("sp", xt[sl].ap, xall[t * 128:(t + 1) * 128, :], [], [xt[sl]], "xt%d" % sl)
        if t % 4 == 0:
            stage_cast(t // 4)
        if t % 4 == 3:
            stage_store(t // 4)
        norm_tile(xt[sl], g_mix, junk, ssb[sl], xs[sl], P[0], hT[sl].ap, [hT[sl]], "act")
        for c in range(8):
            mm(P[1].f[:, 0:512], hT[sl].ap[:, c, :], Wh.ap[:, c, 512:1024], c == 0, c == 7, [hT[sl], Wh], [P[1].b])
        for c in range(8):
            mm(P[2].f[:, 0:512], hT[sl].ap[:, c, :], Wh.ap[:, c, 1024:1536], c == 0, c == 7, [hT[sl], Wh], [P[2].b])
        act(ez[sl].ap, P[1].f[:, 0:512], AF.Exp, [P[1].b], [ez[sl]], scale=-1.0)
        act(vv[t % 4].ap, P[2].f[:, 0:512], AF.Copy, [P[2].b], [vv[t % 4]])
        if local:
            for c in range(8):
                mm(P[1].f[:, 0:512], hT[sl].ap[:, c, :], Wh.ap[:, c, 0:512], c == 0, c == 7, [hT[sl], Wh], [P[1].b])
            for c in range(8):
                mm(P[2].f[:, 0:512], hT[sl].ap[:, c, :], Wh.ap[:, c, 1536:2048], c == 0, c == 7, [hT[sl], Wh], [P[2].b])
            cp("dve", qs[sl].ap, P[1].f[:, 0:512], [P[1].b], [qs[sl]])
            act(eg[sl].ap, P[2].f[:, 0:512], AF.Exp, [P[2].b], [eg[sl]], scale=-1.0)
            cp("dve", gr[sl].ap, P[2].f[:, 0:512], [P[2].b], [gr[sl]])

    def stageB(t):
        local = t >= 32
        sl = t % 2
        s3 = t % 3
        e_ = ez[sl]
        act(e_.ap, e_.ap, AF.Ln, [e_, ones_f], [e_], bias=ones_f.ap[:, 0:1])
        act(e_.ap, e_.ap, AF.Exp, [e_], [e_], scale=-1.0)
        tt("dve", ff.ap, e_.ap, oml_bc.ap, ALU.mult, [e_, oml_bc], [ff])
        tt("pool", ff.ap, ff.ap, lb_bc.ap, ALU.add, [ff, lb_bc], [ff])
        act(logf.ap, ff.ap, AF.Ln, [ff], [logf])
        ts("pool", kk.ap, ff.ap, -1.0, 1.0, ALU.mult, ALU.add, [ff], [kk])
        if local:
            mm(P[3].f[:, 0:512], Uc.ap, logf.ap, True, True, [Uc, logf], [P[3].b])
        mm(P[4].f[:, 0:512], Lc.ap, logf.ap, True, True, [Lc, logf], [P[4].b])
        act(eend.ap, P[4].f[:, 0:512], AF.Exp, [P[4].b], [eend])
        tt("pool", kend[s3].ap, kk.ap, eend.ap, ALU.mult, [kk, eend], [kend[s3]])
        if local:
            act(eb.ap, P[3].f[:, 0:512], AF.Exp, [P[3].b], [eb])
            act(enb.ap, P[3].f[:, 0:512], AF.Exp, [P[3].b], [enb], scale=-1.0)
            tt("dve", qts[sl].ap, qs[sl].ap, eb.ap, ALU.mult, [qs[sl], eb], [qts[sl]])
            tt("pool", kts[sl].ap, kk.ap, enb.ap, ALU.mult, [kk, enb], [kts[sl]])
            g_ = eg[sl]
            act(g_.ap, g_.ap, AF.Ln, [g_, ones_f], [g_], bias=ones_f.ap[:, 0:1])
            act(g_.ap, g_.ap, AF.Exp, [g_], [g_], scale=-1.0)
            tt("dve", gs[s3].ap, gr[sl].ap, g_.ap, ALU.mult, [gr[sl], g_], [gs[s3]])
            tt("pool", gs[s3].ap, gs[s3].ap, hgn_bc.ap, ALU.mult, [gs[s3], hgn_bc], [gs[s3]])
        for h in range(4):
            mm(P[4].f[:, h:h + 1], logf.ap[:, hs(h)], ones_f.ap[:, 0:1], True, True, [logf, ones_f], [P[4].b])
        act(dec[s3].ap, P[4].f[:, 0:4], AF.Exp, [P[4].b], [dec[s3]])

    def stageB2(t):
        sl = t % 2
        qt, kt = qts[sl], kts[sl]
        for h in range(4):
            tr(P[5].bf[:, h * 128:(h + 1) * 128], qt.ap[:, hs(h)], [qt], [P[5].b])
        for h in range(4):
            tr(P[5].bf[:, 512 + h * 128:512 + (h + 1) * 128], kt.ap[:, hs(h)], [kt], [P[5].b])
        cp("dve", qkT[sl].ap, P[5].bf[:, 0:1024].rearrange("p (c t) -> p c t", c=8), [P[5].b], [qkT[sl]])
        for h in range(4):
            mm(P[5].f[:, hs(h)], qkT[sl].ap[:, 4 + h, :], qkT[sl].ap[:, h, :], True, True, [qkT[sl]], [P[5].b])
        tt("dve", ATm[sl].ap, P[5].f[:, 0:512], mask4.ap, ALU.mult, [P[5].b, mask4], [ATm[sl]])

    def stageC(t):
        local = t >= 32
        sl = t % 2
        v_ = vv[t % 4]
        s3 = t % 3
        if local:
            for h in range(4):
                mm(P[7].f[:, hs(h)], ATm[sl].ap[:, hs(h)], v_.ap[:, hs(h)], True, False, [ATm[sl], v_], [P[7].b])
                mm(P[7].f[:, hs(h)], qkT[sl].ap[:, h, :], Sb.ap[:, hs(h)], False, True, [qkT[sl], Sb], [P[7].b])
        for h in range(4):
            mm(P[6].f[:, hs(h)], kend[s3].ap[:, hs(h)], v_.ap[:, hs(h)], True, True, [kend[s3], v_], [P[6].b])
        for h in range(4):
            stt("dve", Sst.ap[:, hs(h)], Sst.ap[:, hs(h)], dec[s3].ap[:, h:h + 1], P[6].f[:, hs(h)], ALU.mult, ALU.add,
                [Sst, dec[s3], P[6].b], [Sst])
        cp("pool", Sb.ap, Sst.ap, [Sst], [Sb])
        if local:
            i = t - 32
            for h in range(4):
                act(junkC.ap[:, 0:128], P[7].f[:, hs(h)], AF.Square, [P[7].b], [junkC, ssq], accum=ssq.ap[:, h:h + 1])
            act(ssq.ap[:, 4:8], ssq.ap[:, 0:4], AF.Ln, [ssq, epsb], [ssq], bias=epsb.ap[:, 0:1], scale=1.0 / 128)
            act(ssq.ap[:, 8:12], ssq.ap[:, 4:8], AF.Exp, [ssq], [ssq], scale=-0.5)
            for h in range(4):
                stt("dve", on.ap[:, hs(h)], P[7].f[:, hs(h)], ssq.ap[:, 8 + h:9 + h], gs[s3].ap[:, hs(h)], ALU.mult, ALU.mult,
                    [P[7].b, ssq, gs[s3]], [on])
            for h in range(4):
                tr(P[7].bf[:, h * 128:(h + 1) * 128], on.ap[:, hs(h)], [on], [P[7].b])
            cp("act", oT[sl].ap, P[7].bf[:, 0:512].rearrange("p (c t) -> p c t", c=4), [P[7].b], [oT[sl]])
            dma("sp", mixT[0:512, i * 128:(i + 1) * 128].rearrange("(h p) t -> p h t", p=128), oT[sl].ap,
                [oT[sl]], [mix_b[i]], "mixo%d" % sl)

    import os as _os
    for it in range(0 if _os.environ.get('SKIP_H') else 64 + 3):
        lists = []
        if it < 64:
            S.begin(); stageA(it); lists.append(S.end())
        if 0 <= it - 1 < 64:
            S.begin(); stageB(it - 1); lists.append(S.end())
        if 32 <= it - 2 < 64:
            S.begin(); stageB2(it - 2); lists.append(S.end())
        if 0 <= it - 3 < 64:
            S.begin(); stageC(it - 3); lists.append(S.end())
        S.merged(lists)

    SCALE = 128 ** -0.5
    for hp in range(0 if _os.environ.get('SKIP_M') else 2):
        S.barrier()
        A.off = A_hm
        Wm = A.alloc([8, 768], BF16, "Wm")
        for c in range(8):
            dma("pool", Wm.ap[:, c, 0:256], w_in_v[:, c, 2048 + hp * 256:2048 + (hp + 1) * 256], [], [Wm], "w_m")
            dma("pool", Wm.ap[:, c, 256:512], w_in_v[:, c, 2560 + hp * 256:2560 + (hp + 1) * 256], [], [Wm], "w_m")
            dma("pool", Wm.ap[:, c, 512:768], w_in_v[:, c, 3072 + hp * 256:3072 + (hp + 1) * 256], [], [Wm], "w_m")
        KT = A.alloc([2, 8192], BF16, "KT")
        KT_b = [Buf("KT%d" % i) for i in range(64)]
        Vaug = A.alloc([64, 2, 130], BF16, "Vaug")
        V_b = [Buf("V%d" % i) for i in range(64)]
        KmF = A.alloc([2, 32], F32, "KmF")
        KmT = A.alloc([2, 32], BF16, "KmT")
        vb_bc = A.alloc([512], F32, "vb")
        dma("sp", vb_bc.ap, vbias.partition_broadcast(128), [], [vb_bc], "c_vb")
        xt = [A.alloc([1024], F32, "xt%d" % i) for i in range(2)]
        junks = [A.alloc([1024], BF16, "junk%d" % i) for i in range(2)]
        ssb = [A.alloc([4], F32, "ssb%d" % i) for i in range(2)]
        xs = [A.alloc([1024], BF16, "xs%d" % i) for i in range(2)]
        hT = [A.alloc([8, 128], BF16, "hT%d" % i) for i in range(2)]
        rt = [A.alloc([32], F32, "rt%d" % i) for i in range(2)]
        ras = [A.alloc([4, 16], F32, "ra%d" % i) for i in range(2)]
        rbs = [A.alloc([4, 16], F32, "rb%d" % i) for i in range(2)]
        qkrs = [A.alloc([4, 128], BF16, "qkr%d" % i) for i in range(2)]
        QTs = [A.alloc([2, 128], BF16, "QT%d" % i) for i in range(2)]
        Gp = A.alloc([32], F32, "Gp")
        mx8 = A.alloc([8], F32, "mx8")
        bqs = [[A.alloc([32], F32, "bq%d_%d" % (i, k)) for k in range(2)] for i in range(2)]
        Pm = [A.alloc([512], BF16, "Pm%d" % i) for i in range(4)]
        PT = [A.alloc([512], BF16, "PT%d" % i) for i in range(3)]
        rden = A.alloc([2], F32, "rden")
        om = A.alloc([2, 128], BF16, "om")
        omT = [A.alloc([2, 128], BF16, "omT%d" % i) for i in range(2)]
        memset("pool", Vaug.flat, 1.0, V_b)
        memset("dve", KmF.flat, 0.0, [KmF])
        memset("dve", KmT.flat, 0.0, [KmT])

        def front(t, pa, pq):
            local = t >= 32
            sl = t % 2
            nblk = t // 2
            ra, rb, qkr, QT = ras[sl], rbs[sl], qkrs[sl], QTs[sl]
            dma("sp", xt[sl].ap, xall[t * 128:(t + 1) * 128, :], [], [xt[sl]], "xt%d" % sl)
            dma("sp", rt[sl].ap, rot[t * 128:(t + 1) * 128, :], [], [rt[sl]], "rt%d" % sl)
            norm_tile(xt[sl], g_mix, junks[sl], ssb[sl], xs[sl], pa, hT[sl].ap, [hT[sl]], "act")
            if local:
                for c in range(8):
                    mm(pq.f[:, 0:256], hT[sl].ap[:, c, :], Wm.ap[:, c, 0:256], c == 0, False, [hT[sl], Wm], [pq.b])
            for c in range(8):
                mm(pq.f[:, 256:512], hT[sl].ap[:, c, :], Wm.ap[:, c, 256:512], (c == 0 and not local), c == 7,
                   [hT[sl], Wm], [pq.b])
            for c in range(8):
                mm(pa.f[:, 0:256], hT[sl].ap[:, c, :], Wm.ap[:, c, 512:768], c == 0, c == 7, [hT[sl], Wm], [pa.b])
            act(Vaug.ap[:, t, :, 0:128], pa.f[:, 0:256].rearrange("p (h d) -> p h d", h=2), AF.Copy, [pa.b], [V_b[t]])
            h0 = 0 if local else 2
            nh = 4 - h0
            X = pq.f[:, 0:512].rearrange("p (h d) -> p h d", h=4)
            cosb = rt[sl].ap[:, 0:16].unsqueeze(1).to_broadcast([128, nh, 16])
            sinb = rt[sl].ap[:, 16:32].unsqueeze(1).to_broadcast([128, nh, 16])
            x1 = X[:, h0:4, 0:16]
            x2 = X[:, h0:4, 16:32]
            tt("dve", ra.ap[:, h0:4, :], x1, cosb, ALU.mult, [pq.b, rt[sl]], [ra])
            tt("dve", rb.ap[:, h0:4, :], x2, sinb, ALU.mult, [pq.b, rt[sl]], [rb])
            tt("dve", qkr.ap[:, h0:4, 0:16], ra.ap[:, h0:4, :], rb.ap[:, h0:4, :], ALU.subtract, [ra, rb], [qkr])
            tt("dve", ra.ap[:, h0:4, :], x2, cosb, ALU.mult, [pq.b, rt[sl]], [ra])
            tt("dve", rb.ap[:, h0:4, :], x1, sinb, ALU.mult, [pq.b, rt[sl]], [rb])
            tt("dve", qkr.ap[:, h0:4, 16:32], ra.ap[:, h0:4, :], rb.ap[:, h0:4, :], ALU.add, [ra, rb], [qkr])
            act(qkr.ap[:, h0:4, 32:128], X[:, h0:4, 32:128], AF.Copy, [pq.b], [qkr])
            for j in range(h0, 4):
                tr(pa.bf[:, j * 128:(j + 1) * 128], qkr.ap[:, j, :], [qkr], [pa.b])
            cp("dve", KT.ap[:, :, t * 128:(t + 1) * 128], pa.bf[:, 256:512].rearrange("p (h t) -> p h t", h=2),
               [pa.b], [KT_b[t]])
            if local:
                cp("dve", QT.ap, pa.bf[:, 0:256].rearrange("p (h t) -> p h t", h=2), [pa.b], [QT])
            for h in range(2):
                mm(pq.f[:, h:h + 1], qkr.ap[:, 2 + h, :], ones_b.ap[:, 0:1], True, True, [qkr, ones_b], [pq.b])
            stt("dve", KmF.ap[:, :, nblk], pq.f[:, 0:2], 1.0 / 256, KmF.ap[:, :, nblk], ALU.mult, ALU.add,
                [pq.b, KmF], [KmF])
            cp("dve", KmT.ap[:, :, nblk], KmF.ap[:, :, nblk], [KmF], [KmT])
            if local:
                j_loc = (t - 32) // 2
                for h in range(2):
                    bq_ = bqs[sl][h]
                    mm(pq.f[:, 0:32], QT.ap[:, h, :], KmT.ap[:, h, :], True, True, [QT, KmT], [pq.b])
                    tt("dve", Gp.ap, pq.f[:, 0:32], vb_bc.ap[:, j_loc * 32:(j_loc + 1) * 32], ALU.add, [pq.b, vb_bc], [Gp])
                    vmax(mx8.ap, Gp.ap, [Gp], [mx8])
                    ts("dve", bq_.ap, Gp.ap, mx8.ap[:, 2:3], NEG, ALU.is_lt, ALU.mult, [Gp, mx8], [bq_])
                    tt("dve", bq_.ap, bq_.ap, vb_bc.ap[:, j_loc * 32:(j_loc + 1) * 32], ALU.add, [bq_, vb_bc], [bq_])

        def attention(t):
            sl = t % 2
            QT = QTs[sl]
            J = t // 2
            i = t - 32
            j_loc = i // 2
            po = P[7]
            units = []
            totals = []
            for h in range(2):
                hu = []
                n0 = 0
                while n0 < J:
                    nb_ = 2 if n0 + 1 < J else 1
                    hu.append(("past", n0, nb_, h))
                    n0 += nb_
                nk_own = 1 if (t % 2 == 0) else 2
                hu.append(("own", J, nk_own, h))
                totals.append(sum((u[2] * 2 if u[0] == "past" else u[2]) for u in hu))
                units += hu
            kt_done = [0, 0]

            def stage_S(ui, u):
                h = u[3]
                pb = P[2 + ui % 3]
                pm = Pm[ui % 4]
                if u[0] == "past":
                    n0_, nb2 = u[1], u[2]
                    w = nb2 * 256
                    kbufs = [KT_b[2 * n0_ + x] for x in range(2 * nb2)]
                    mm(pb.f[:, 0:w], QT.ap[:, h, :], KT.ap[:, h, n0_ * 256:n0_ * 256 + w], True, True, [QT] + kbufs, [pb.b])
                    for x in range(nb2):
                        act(pm.ap[:, x * 256:(x + 1) * 256], pb.f[:, x * 256:(x + 1) * 256], AF.Exp, [pb.b, bqs[sl][h]], [pm],
                            bias=bqs[sl][h].ap[:, n0_ + x:n0_ + x + 1], scale=SCALE)
                else:
                    nk = u[2]
                    w = nk * 128
                    kbufs = [KT_b[2 * J + x] for x in range(nk)]
                    mm(pb.f[:, 0:w], QT.ap[:, h, :], KT.ap[:, h, J * 256:J * 256 + w], True, False, [QT] + kbufs, [pb.b])
                    mm(pb.f[:, w - 128:w], ident.ap, triB.ap, False, True, [ident, triB], [pb.b])
                    act(pm.ap[:, 0:w], pb.f[:, 0:w], AF.Exp, [pb.b], [pm], scale=SCALE)

            def stage_T(ui, u):
                pb = P[5 + ui % 2]
                pm = Pm[ui % 4]
                pt = PT[ui % 3]
                nkt = u[2] * 2 if u[0] == "past" else u[2]
                for x in range(nkt):
                    tr(pb.bf[:, x * 128:(x + 1) * 128], pm.ap[:, x * 128:(x + 1) * 128], [pm], [pb.b])
                cp("dve", pt.ap[:, 0:nkt * 128], pb.bf[:, 0:nkt * 128], [pb.b], [pt])

            def stage_PV(ui, u):
                h = u[3]
                pt = PT[ui % 3]
                nkt = u[2] * 2 if u[0] == "past" else u[2]
                kt0 = u[1] * 2
                for x in range(nkt):
                    first = kt_done[h] == 0
                    kt_done[h] += 1
                    last = kt_done[h] == totals[h]
                    mm(po.f[:, 128:258], pt.ap[:, x * 128:(x + 1) * 128], Vaug.ap[:, kt0 + x, h, :], first, last,
                       [pt, V_b[kt0 + x]], [po.b])
                if kt_done[h] == totals[h]:
                    recip(rden.ap[:, h:h + 1], po.f[:, 256:257], [po.b], [rden])
                    act(om.ap[:, h, :], po.f[:, 128:256], AF.Identity, [po.b, rden], [om], scale=rden.ap[:, h:h + 1])

            nu = len(units)
            D1, D2 = 2, 4
            for step in range(nu + D2):
                if step < nu:
                    stage_S(step, units[step])
                if 0 <= step - D1 < nu:
                    stage_T(step - D1, units[step - D1])
                if 0 <= step - D2 < nu:
                    stage_PV(step - D2, units[step - D2])
            for h in range(2):
                tr(po.bf[:, h * 128:(h + 1) * 128], om.ap[:, h, :], [om], [po.b])
            cp("dve", omT[sl].ap, po.bf[:, 0:256].rearrange("p (h t) -> p h t", h=2), [po.b], [omT[sl]])
            r0 = 512 + hp * 256
            dma("sp", mixT[r0:r0 + 256, i * 128:(i + 1) * 128].rearrange("(h p) t -> p h t", p=128), omT[sl].ap,
                [omT[sl]], [mix_b[i]], "mixo%d" % sl)

        for k in range(16):
            S.begin(); front(2 * k, P[0], P[1]); la = S.end()
            S.begin(); front(2 * k + 1, P[2], P[3]); lb_ = S.end()
            S.merged([la, lb_])
        S.begin(); front(32, P[0], P[1]); la = S.end()
        S.merged([la])
        for t in range(32, 64):
            lists = []
            S.begin(); attention(t); lists.append(S.end())
            if t + 1 < 64:
                S.begin(); front(t + 1, P[0], P[1]); lists.append(S.end())
            S.merged(lists)

    S.barrier()
    A.off = A_persist
    Wo = A.alloc([8, 1024], BF16, "Wo")
    Wxq = A.alloc([8, 1024], BF16, "Wxq")
    Wxo = A.alloc([8, 1024], BF16, "Wxo")
    for c in range(8):
        dma("pool", Wo.ap[:, c, :], w_out.rearrange("(c p) n -> p c n", p=128)[:, c, :], [], [Wo], "w_o")
        dma("pool", Wxq.ap[:, c, :], w_xq.rearrange("(c p) n -> p c n", p=128)[:, c, :], [], [Wxq], "w_xq")
        dma("pool", Wxo.ap[:, c, :], w_xo.rearrange("(c p) n -> p c n", p=128)[:, c, :], [], [Wxo], "w_xo")
    memKT = A.alloc([8, 256], BF16, "memKT")
    memV = A.alloc([2, 1024], BF16, "memV")
    g_xa = A.alloc([1024], F32, "g_xa")
    g_mlp = A.alloc([1024], F32, "g_mlp")
    g_fin = A.alloc([1024], F32, "g_fin")
    dma("sp", g_xa.ap, norm_xattn.partition_broadcast(128), [], [g_xa], "c_gxa")
    dma("sp", g_mlp.ap, norm_mlp.partition_broadcast(128), [], [g_mlp], "c_gmlp")
    dma("sp", g_fin.ap, norm_final.partition_broadcast(128), [], [g_fin], "c_gfin")
    W1 = [A.alloc([8, 512], BF16, "W1_%d" % i) for i in range(2)]
    W2 = [A.alloc([4, 1024], BF16, "W2_%d" % i) for i in range(2)]
    X = [A.alloc([1024], F32, "X%d" % i) for i in range(4)]
    xs = [A.alloc([1024], BF16, "xsF%d" % i) for i in range(4)]
    junk = A.alloc([1024], BF16, "junkF")
    ssb = [A.alloc([4], F32, "ssbF%d" % i) for i in range(4)]
    hTs = A.alloc([8, 512], BF16, "hTs")
    hTs_b = [Buf("hTs%d" % i) for i in range(4)]

    def norm4(g_bc):
        lists = []
        for j in range(4):
            S.begin()
            norm_tile(X[j], g_bc, junk, ssb[j], xs[j], P[j], hTs.ap[:, :, j * 128:(j + 1) * 128], [hTs_b[j]], "act")
            lists.append(S.end())
        S.merged(lists)
    qxT = A.alloc([8, 512], BF16, "qxT")
    oxT = A.alloc([8, 512], BF16, "oxT")
    PTx = A.alloc([2, 512], BF16, "PTx")
    rd = A.alloc([512], F32, "rd")
    rl = [A.alloc([512], F32, "rl%d" % i) for i in range(2)]
    hid = A.alloc([32, 512], BF16, "hid")
    hid_off = A.off - 32 * 512
    Wxk_ap = arena[:, hid_off:hid_off + 8192].rearrange("p (c n) -> p c n", c=8)
    Wxv_ap = arena[:, hid_off + 8192:hid_off + 16384].rearrange("p (c n) -> p c n", c=8)
    for c in range(8):
        dma("pool", Wxk_ap[:, c, :], w_xk.rearrange("(c p) n -> p c n", p=128)[:, c, :], [], [hid], "w_xk")
        dma("pool", Wxv_ap[:, c, :], w_xv.rearrange("(c p) n -> p c n", p=128)[:, c, :], [], [hid], "w_xk")
    g_mem = X[3]
    dma("sp", g_mem.ap, norm_mem.partition_broadcast(128), [], [g_mem], "c_gmem")
    for m in range(2):
        dma("sp", X[m].ap, mem[m * 128:(m + 1) * 128, :], [], [X[m]], "X%d" % m)
        norm_tile(X[m], g_mem, junk, ssb[m], xs[m], P[0], hTs.ap[:, :, m * 128:(m + 1) * 128], [hTs_b[m]], "act")
    for ch in range(8):
        pb = P[1 + ch % 2]
        for c in range(8):
            mm(pb.f[:, 0:256], Wxk_ap[:, c, ch * 128:(ch + 1) * 128], hTs.ap[:, c, 0:256], c == 0, c == 7, [hid] + hTs_b[0:2], [pb.b])
        cp("dve", memKT.ap[:, ch, :], pb.f[:, 0:256], [pb.b], [memKT])
    for m in range(2):
        for hh in range(2):
            pb = P[3 + hh]
            for c in range(8):
                mm(pb.f[:, 0:512], hTs.ap[:, c, m * 128:(m + 1) * 128], Wxv_ap[:, c, hh * 512:(hh + 1) * 512], c == 0, c == 7,
                   [hTs_b[m], hid], [pb.b])
            cp("act", memV.ap[:, m, hh * 512:(hh + 1) * 512], pb.f[:, 0:512], [pb.b], [memV])

    XSCALE = 256 ** -0.5
    mixT_v = mixT.rearrange("(c p) t -> p c t", p=128)
    for s in range(8):
        dma("sp", oxT.ap, mixT_v[:, :, s * 512:(s + 1) * 512], [mix_b[4 * s + j] for j in range(4)], [oxT], "mixin")
        for j in range(4):
            tok0 = 4096 + (s * 4 + j) * 128
            dma("sp", X[j].ap, xall[tok0:tok0 + 128, :], [], [X[j]], "X%d" % j)
        for j in range(4):
            for hh in range(2):
                pb = P[1 + (2 * j + hh) % 2]
                for c in range(8):
                    mm(pb.f[:, 0:512], oxT.ap[:, c, j * 128:(j + 1) * 128], Wo.ap[:, c, hh * 512:(hh + 1) * 512], c == 0, c == 7,
                       [oxT, Wo], [pb.b])
                tt("dve", X[j].ap[:, hh * 512:(hh + 1) * 512], X[j].ap[:, hh * 512:(hh + 1) * 512], pb.f[:, 0:512], ALU.add,
                   [X[j], pb.b], [X[j]])
        norm4(g_xa)
        for ch in range(8):
            pb = P[1 + ch % 2]
            for c in range(8):
                mm(pb.f[:, 0:512], Wxq.ap[:, c, ch * 128:(ch + 1) * 128], hTs.ap[:, c, :], c == 0, c == 7, [Wxq] + hTs_b, [pb.b])
            cp("act", qxT.ap[:, ch, :], pb.f[:, 0:512], [pb.b], [qxT])
        for h in range(4):
            for m in range(2):
                pb = P[3 + m]
                for dd in range(2):
                    mm(pb.f[:, 0:512], memKT.ap[:, 2 * h + dd, m * 128:(m + 1) * 128], qxT.ap[:, 2 * h + dd, :], dd == 0, dd == 1,
                       [memKT, qxT], [pb.b])
                act(PTx.ap[:, m, :], pb.f[:, 0:512], AF.Exp, [pb.b], [PTx], scale=XSCALE)
            for m in range(2):
                mm(P[5].f[:, 0:512], ones_b.ap, PTx.ap[:, m, :], m == 0, m == 1, [ones_b, PTx], [P[5].b])
            act(rd.ap, P[5].f[:, 0:512], AF.Ln, [P[5].b], [rd])
            act(rd.ap, rd.ap, AF.Exp, [rd], [rd], scale=-1.0)
            for dd in range(2):
                pb = P[6 + dd]
                for m in range(2):
                    mm(pb.f[:, 0:512], memV.ap[:, m, (2 * h + dd) * 128:(2 * h + dd + 1) * 128], PTx.ap[:, m, :], m == 0, m == 1,
                       [memV, PTx], [pb.b])
                tt("dve", oxT.ap[:, 2 * h + dd, :], pb.f[:, 0:512], rd.ap, ALU.mult, [pb.b, rd], [oxT])
        for j in range(4):
            for hh in range(2):
                pb = P[1 + (2 * j + hh) % 2]
                for c in range(8):
                    mm(pb.f[:, 0:512], oxT.ap[:, c, j * 128:(j + 1) * 128], Wxo.ap[:, c, hh * 512:(hh + 1) * 512], c == 0, c == 7,
                       [oxT, Wxo], [pb.b])
                tt("dve", X[j].ap[:, hh * 512:(hh + 1) * 512], X[j].ap[:, hh * 512:(hh + 1) * 512], pb.f[:, 0:512], ALU.add,
                   [X[j], pb.b], [X[j]])
        norm4(g_mlp)
        for fc in range(8):
            w1 = W1[fc % 2]
            dma("sp", w1.ap.rearrange("p c n -> p (c n)"), ff1b[fc], [ff1b_b[fc]], [w1], "W1_%d" % (fc % 2))
            for sub in range(4):
                k_ = fc * 4 + sub
                pb = P[1 + k_ % 2]
                for c in range(8):
                    mm(pb.f[:, 0:512], w1.ap[:, c, sub * 128:(sub + 1) * 128], hTs.ap[:, c, :], c == 0, c == 7, [w1] + hTs_b, [pb.b])
                r_ = rl[k_ % 2]
                act(r_.ap, pb.f[:, 0:512], AF.Relu, [pb.b], [r_])
                tt("pool", hid.ap[:, k_, :], r_.ap, r_.ap, ALU.mult, [r_], [hid])
        for j2 in range(8):
            w2 = W2[j2 % 2]
            dma("sp", w2.ap.rearrange("p a n -> p (a n)"), ff2b[j2], [ff2b_b[j2]], [w2], "W2_%d" % (j2 % 2))
            for j in range(4):
                for hh in range(2):
                    pb = P[j * 2 + hh]
                    for sub in range(4):
                        mm(pb.f[:, 0:512], hid.ap[:, j2 * 4 + sub, j * 128:(j + 1) * 128], w2.ap[:, sub, hh * 512:(hh + 1) * 512],
                           (j2 == 0 and sub == 0), (j2 == 7 and sub == 3), [hid, w2], [pb.b])
        for j in range(4):
            for hh in range(2):
                pb = P[j * 2 + hh]
                tt("dve", X[j].ap[:, hh * 512:(hh + 1) * 512], X[j].ap[:, hh * 512:(hh + 1) * 512], pb.f[:, 0:512], ALU.add,
                   [X[j], pb.b], [X[j]])
        for j in range(4):
            sb_ = ssb[j]
            act(junk.ap, X[j].ap, AF.Square, [X[j]], [junk, sb_], accum=sb_.ap[:, 0:1])
            act(sb_.ap[:, 1:2], sb_.ap[:, 0:1], AF.Ln, [sb_, epsb], [sb_], bias=epsb.ap[:, 0:1], scale=1.0 / 1024)
            act(sb_.ap[:, 2:3], sb_.ap[:, 1:2], AF.Exp, [sb_], [sb_], scale=-0.5)
            stt("dve", X[j].ap, X[j].ap, sb_.ap[:, 2:3], g_fin.ap, ALU.mult, ALU.mult, [X[j], sb_, g_fin], [X[j]])
            r0 = (s * 4 + j) * 128
            dma("sp", out[r0:r0 + 128, :], X[j].ap, [X[j]], [], "X%d" % j, is_out=True)

    S.emit(st)
    st.close()
    return nc


_NC_CACHE = {}


def _host_consts():
    ar = np.arange(128)
    U = (ar[:, None] <= ar[None, :]).astype(np.float32)
    L = (ar[:, None] > ar[None, :]).astype(np.float32)
    mask4 = np.tile(U, (1, 4)).astype(np.float32)
    tri = np.where(ar[None, :] <= ar[:, None], 0.0, NEG).astype(np.float32)
    return {
        "cU": U, "cL": L, "cMask4": mask4,
        "cTriB": tri.astype(ml_dtypes.bfloat16),
        "cIdent": np.eye(128, dtype=np.float32).astype(ml_dtypes.bfloat16),
    }


def _rot_table(positions):
    half = 16
    inv_freq = (np.float32(500000.0) ** (-np.arange(half, dtype=np.float32) * np.float32(2.0) / np.float32(32))).astype(np.float32)
    ang = positions.astype(np.float32)[:, None] * inv_freq[None, :]
    return np.concatenate([np.cos(ang), np.sin(ang)], axis=1).astype(np.float32)


def make_in_maps(inputs):
    x = np.asarray(inputs["x"], dtype=np.float32)
    memv = np.asarray(inputs["mem"], dtype=np.float32)
    consts = _host_consts()
    shared = {
        "w_in": np.ascontiguousarray(inputs["w_in"][0]),
        "w_out": np.ascontiguousarray(inputs["w_out"][0]),
        "w_xq": np.ascontiguousarray(inputs["w_xq"][0]),
        "w_xk": np.ascontiguousarray(inputs["w_xk"][0]),
        "w_xv": np.ascontiguousarray(inputs["w_xv"][0]),
        "w_xo": np.ascontiguousarray(inputs["w_xo"][0]),
        "w_ff1": np.ascontiguousarray(inputs["w_ff1"][0]),
        "w_ff2": np.ascontiguousarray(inputs["w_ff2"][0]),
        "norm_mix": np.ascontiguousarray(inputs["norm_mix"][0:1]),
        "norm_xattn": np.ascontiguousarray(inputs["norm_xattn"][0:1]),
        "norm_mem": np.ascontiguousarray(inputs["norm_mem"][0:1]),
        "norm_mlp": np.ascontiguousarray(inputs["norm_mlp"][0:1]),
        "norm_final": np.ascontiguousarray(np.asarray(inputs["norm_final"]).reshape(1, 1024)),
        "hgrn_norm": np.ascontiguousarray(inputs["hgrn_norm"][0:1]),
        "lb_logits": np.ascontiguousarray(inputs["lb_logits"]),
    }
    shared = {k: np.asarray(v, dtype=np.float32) for k, v in shared.items()}
    shared.update(consts)
    in_maps = []
    for c in range(8):
        b, hf = c // 2, c % 2
        xall = np.zeros((8192, 1024), np.float32)
        if hf == 1:
            xall[:4096] = x[b, :4096]
        xall[4096:] = x[b, hf * 4096:(hf + 1) * 4096]
        pos = np.concatenate([np.arange(4096), hf * 4096 + np.arange(4096)])
        vb = np.full((16, 32), NEG, np.float32)
        for j in range(16):
            lo = 16 if hf == 0 else 0
            vb[j, lo:16 + j] = 0.0
        m = dict(shared)
        m["xall"] = xall
        m["mem"] = np.ascontiguousarray(memv[b])
        m["rot"] = _rot_table(pos)
        m["vbias"] = vb.reshape(1, 512)
        in_maps.append(m)
    return in_maps


def kernel(**inputs):
    if "nc" not in _NC_CACHE:
        _NC_CACHE["nc"] = build_nc()
    nc = _NC_CACHE["nc"]
    in_maps = make_in_maps(inputs)
    res = run_bass_kernel_spmd(nc, in_maps, core_ids=list(range(8)))
    outp = np.zeros((4, 8192, 1024), np.float32)
    for c in range(8):
        b, hf = c // 2, c % 2
        outp[b, hf * 4096:(hf + 1) * 4096] = np.asarray(res.results[c]["out"], dtype=np.float32)
    return outp
```

```python
import numpy as np
from contextlib import ExitStack
import ml_dtypes
import concourse.bass as bass
import concourse.mybir as mybir
from concourse.bass_utils import run_bass_kernel_spmd

F32 = mybir.dt.float32
BF16 = mybir.dt.bfloat16
AF = mybir.ActivationFunctionType
ALU = mybir.AluOpType

ENGS = ("pe", "act", "dve", "pool", "sp")
EPS = 1e-6
NEG = -1.0e4


class Buf:
    __slots__ = ("name", "last_w", "readers", "excl")

    def __init__(self, name=""):
        self.name = name
        self.excl = False
        self.last_w = None
        self.readers = []


class Op:
    __slots__ = ("eng", "fn", "deps", "is_dma", "lane", "lane_val", "pos", "signal", "sigval", "waits", "idx")


class Sched:
    def __init__(self, nc):
        self.nc = nc
        self.ops = []
        self.streams = {e: [] for e in ENGS}
        self.lane_count = {}
        self.lane_last = {}
        self.out_lanes = set()
        self.bar = []
        self.bar_pending = {e: False for e in ENGS}

    def barrier(self):
        bar = []
        for e in ENGS:
            for op in reversed(self.streams[e]):
                if not op.is_dma:
                    bar.append(op.idx)
                    break
        for lane, idx in self.lane_last.items():
            bar.append(idx)
        self.bar = bar
        self.bar_pending = {e: True for e in ENGS}

    def _record(self, eng, fn, reads, writes, is_dma=False, lane=None):
        op = Op()
        op.eng = eng
        op.fn = fn
        op.is_dma = is_dma
        op.lane = lane
        op.signal = False
        op.sigval = 0
        op.waits = []
        op.idx = len(self.ops)
        if any(b.excl for b in reads):
            writes = writes + [b for b in reads if b.excl and b not in writes]
        deps = {}
        for b in reads:
            if b.last_w is not None:
                deps[b.last_w] = True
        for b in writes:
            if b.last_w is not None:
                deps.setdefault(b.last_w, False)
            for r in b.readers:
                deps.setdefault(r, False)
        if self.bar_pending[eng]:
            for i in self.bar:
                deps.setdefault(i, False)
            self.bar_pending[eng] = False
        op.deps = deps
        for b in reads:
            b.readers.append(op.idx)
        for b in writes:
            b.last_w = op.idx
            b.readers = []
        if is_dma:
            c = self.lane_count.get(lane, 0) + 1
            self.lane_count[lane] = c
            op.lane_val = 16 * c
            self.lane_last[lane] = op.idx
        op.pos = len(self.streams[eng])
        self.streams[eng].append(op)
        self.ops.append(op)
        return op

    _cap = None

    def begin(self):
        self._cap = []

    def end(self):
        l = self._cap
        self._cap = None
        return l

    def merged(self, lists):
        items = []
        for li, l in enumerate(lists):
            n = len(l)
            for i, it in enumerate(l):
                items.append(((i + 0.5) / n, li, i, it))
        items.sort(key=lambda x: (x[0], x[1], x[2]))
        for _, _, _, it in items:
            if it[0] == "op":
                self.op(*it[1:])
            else:
                self.dma(*it[1:])

    def op(self, eng, fn, reads=(), writes=()):
        if self._cap is not None:
            self._cap.append(("op", eng, fn, list(reads), list(writes)))
            return None
        return self._record(eng, fn, list(reads), list(writes))

    def dma(self, eng, fn, reads=(), writes=(), lane=None, is_out=False):
        if self._cap is not None:
            self._cap.append(("dma", eng, fn, list(reads), list(writes), lane, is_out))
            return None
        if is_out:
            self.out_lanes.add(lane)
        return self._record(eng, fn, list(reads), list(writes), is_dma=True, lane=lane)

    def resolve(self):
        seen = {e: {} for e in ENGS}
        seen_lane = {e: {} for e in ENGS}
        for op in self.ops:
            E = op.eng
            need = {}
            need_lane = {}
            for di, is_raw in op.deps.items():
                d = self.ops[di]
                if d.is_dma:
                    if seen_lane[E].get(d.lane, 0) < d.lane_val:
                        need_lane[d.lane] = max(need_lane.get(d.lane, 0), d.lane_val)
                else:
                    if d.eng == E and E == "pe":
                        continue
                    if seen[E].get(d.eng, -1) < d.pos:
                        need[d.eng] = max(need.get(d.eng, -1), d.pos)
            for Ed, pos in need.items():
                seen[E][Ed] = pos
                self.streams[Ed][pos].signal = True
                op.waits.append(("c", Ed, pos))
            for lane, val in need_lane.items():
                seen_lane[E][lane] = val
                op.waits.append(("d", lane, val))
        for e in ENGS:
            c = 0
            for op in self.streams[e]:
                if (not op.is_dma) and op.signal:
                    c += 1
                    op.sigval = c

    def emit(self, stack):
        nc = self.nc
        self.resolve()
        esem = {e: stack.enter_context(nc.semaphore("cs_" + e)) for e in ENGS}
        lsem = {}
        for lane in self.lane_count:
            lsem[lane] = stack.enter_context(nc.semaphore("dl_" + lane))
        block = stack.enter_context(nc.Block())
        streams = self.streams
        out_lanes = self.out_lanes
        lane_count = self.lane_count

        def run(eng_name, e):
            for op in streams[eng_name]:
                for w in op.waits:
                    if w[0] == "c":
                        e.wait_ge(esem[w[1]], streams[w[1]][w[2]].sigval)
                    else:
                        e.wait_ge(lsem[w[1]], w[2])
                ins = op.fn(e)
                if op.is_dma:
                    ins.then_inc(lsem[op.lane], 16)
                elif op.signal:
                    ins.then_inc(esem[eng_name], 1)
            if eng_name == "sp":
                for lane in sorted(out_lanes):
                    e.wait_ge(lsem[lane], 16 * lane_count[lane])

        @block.tensor
        def _(e):
            run("pe", e)

        @block.scalar
        def _(e):
            run("act", e)

        @block.vector
        def _(e):
            run("dve", e)

        @block.gpsimd
        def _(e):
            run("pool", e)

        @block.sync
        def _(e):
            run("sp", e)


class T:
    __slots__ = ("ap", "b", "flat")

    def __init__(self, ap, b=None, name=""):
        self.ap = ap
        self.flat = ap
        self.b = b if b is not None else Buf(name)


def _bufs(ts):
    return [t.b if isinstance(t, T) else t for t in ts]


def build_nc(debug=False):
    nc = bass.Bass("TRN2", target_bir_lowering=False)

    def din(name, shape, dt=F32):
        return nc.dram_tensor(name, list(shape), dt, kind="ExternalInput").ap()

    xall = din("xall", [8192, 1024])
    mem = din("mem", [256, 1024])
    w_in = din("w_in", [1024, 3584])
    w_out = din("w_out", [1024, 1024])
    w_xq = din("w_xq", [1024, 1024])
    w_xk = din("w_xk", [1024, 1024])
    w_xv = din("w_xv", [1024, 1024])
    w_xo = din("w_xo", [1024, 1024])
    w_ff1 = din("w_ff1", [1024, 4096])
    w_ff2 = din("w_ff2", [4096, 1024])
    norm_mix = din("norm_mix", [1, 1024])
    norm_xattn = din("norm_xattn", [1, 1024])
    norm_mem = din("norm_mem", [1, 1024])
    norm_mlp = din("norm_mlp", [1, 1024])
    norm_final = din("norm_final", [1, 1024])
    hgrn_norm = din("hgrn_norm", [1, 512])
    lb_logits = din("lb_logits", [2, 512])
    rot = din("rot", [8192, 32])
    vbias = din("vbias", [1, 512])
    cU = din("cU", [128, 128])
    cL = din("cL", [128, 128])
    cMask4 = din("cMask4", [128, 512])
    cTriB = din("cTriB", [128, 128], BF16)
    cIdent = din("cIdent", [128, 128], BF16)
    out = nc.dram_tensor("out", [4096, 1024], F32, kind="ExternalOutput").ap()
    mixT = nc.dram_tensor("mixT", [1024, 4096], BF16, kind="ExternalOutput" if debug else "Internal").ap()
    ff1b = nc.dram_tensor("ff1b", [8, 128, 4096], BF16).ap()
    ff2b = nc.dram_tensor("ff2b", [8, 128, 4096], BF16).ap()

    S = Sched(nc)
    st = ExitStack()
    AR_ELEMS = 104000
    arena = st.enter_context(nc.sbuf_tensor("arena", [128, AR_ELEMS], BF16))
    banks = [st.enter_context(nc.psum_tensor("pb%d" % i, [128, 1024], BF16)) for i in range(8)]

    class PB:
        def __init__(self, t, i):
            self.bf = t[:, :]
            self.f = t[:, :].bitcast(F32)
            self.b = Buf("pb%d" % i)
            self.b.excl = True

    P = [PB(banks[i], i) for i in range(8)]

    class Arena:
        def __init__(self):
            self.off = 0

        def alloc(self, shape, dt, name=""):
            n = 1
            for s_ in shape:
                n *= s_
            nel = n * (2 if dt == F32 else 1)
            nel = (nel + 31) // 32 * 32
            assert self.off + nel <= AR_ELEMS, ("arena overflow", name, self.off, nel)
            v = arena[:, self.off:self.off + nel]
            self.off += nel
            if dt == F32:
                v = v.bitcast(F32)
                v = v[:, 0:n]
            else:
                v = v[:, 0:n]
            flat = v
            if len(shape) == 2:
                v = v.rearrange("p (a b) -> p a b", a=shape[0])
            elif len(shape) == 3:
                v = v.rearrange("p (a b c) -> p a b c", a=shape[0], b=shape[1])
            t_ = T(v, name=name)
            t_.flat = flat
            return t_

    A = Arena()

    def mm(out_ap, lhsT, rhs, start, stop, r, w):
        S.op("pe", lambda e: e.matmul(out_ap, lhsT=lhsT, rhs=rhs, start=start, stop=stop), _bufs(r), _bufs(w))

    def tr(out_ap, in_ap, r, w):
        S.op("pe", lambda e: e.transpose(out=out_ap, in_=in_ap, identity=ident.ap), _bufs(r) + [ident.b], _bufs(w))

    def act(out_ap, in_ap, func, r, w, bias=None, scale=None, accum=None):
        kw = {}
        if bias is not None:
            kw["bias"] = bias
        if scale is not None:
            kw["scale"] = scale
        if accum is not None:
            kw["accum_out"] = accum
        S.op("act", lambda e: e.activation(out=out_ap, in_=in_ap, func=func, **kw), _bufs(r), _bufs(w))

    def tt(eng, out_ap, in0, in1, op, r, w):
        S.op(eng, lambda e: e.tensor_tensor(out=out_ap, in0=in0, in1=in1, op=op), _bufs(r), _bufs(w))

    def ts(eng, out_ap, in0, s1, s2, op0, op1, r, w):
        if op1 is None:
            S.op(eng, lambda e: e.tensor_scalar(out=out_ap, in0=in0, scalar1=s1, scalar2=None, op0=op0), _bufs(r), _bufs(w))
        else:
            S.op(eng, lambda e: e.tensor_scalar(out=out_ap, in0=in0, scalar1=s1, scalar2=s2, op0=op0, op1=op1), _bufs(r), _bufs(w))

    def stt(eng, out_ap, in0, scalar, in1, op0, op1, r, w):
        S.op(eng, lambda e: e.scalar_tensor_tensor(out=out_ap, in0=in0, scalar=scalar, in1=in1, op0=op0, op1=op1), _bufs(r), _bufs(w))

    def cp(eng, out_ap, in_ap, r, w):
        if eng == "act":
            act(out_ap, in_ap, AF.Copy, r, w)
        else:
            S.op(eng, lambda e: e.tensor_copy(out=out_ap, in_=in_ap), _bufs(r), _bufs(w))

    def recip(out_ap, in_ap, r, w):
        S.op("dve", lambda e: e.reciprocal(out=out_ap, in_=in_ap), _bufs(r), _bufs(w))

    def vmax(out_ap, in_ap, r, w):
        S.op("dve", lambda e: e.max(out=out_ap, in_=in_ap), _bufs(r), _bufs(w))

    def memset(eng, ap, val, w):
        S.op(eng, lambda e: e.memset(ap, val), [], _bufs(w))

    def dma(eng, out_ap, in_ap, r, w, lane, is_out=False):
        S.dma(eng, lambda e: e.dma_start(out=out_ap, in_=in_ap), _bufs(r), _bufs(w), lane=lane, is_out=is_out)

    ident = A.alloc([128], BF16, "ident")
    triB = A.alloc([128], BF16, "triB")
    Uc = A.alloc([128], F32, "Uc")
    Lc = A.alloc([128], F32, "Lc")
    mask4 = A.alloc([512], F32, "mask4")
    ones_f = A.alloc([8], F32, "ones_f")
    ones_b = A.alloc([128], BF16, "ones_b")
    epsb = A.alloc([2], F32, "epsb")
    dma("sp", ident.ap, cIdent[:, :], [], [ident], "c_ident")
    dma("sp", triB.ap, cTriB[:, :], [], [triB], "c_trib")
    dma("sp", Uc.ap, cU[:, :], [], [Uc], "c_u")
    dma("sp", Lc.ap, cL[:, :], [], [Lc], "c_l")
    dma("sp", mask4.ap, cMask4[:, :], [], [mask4], "c_m4")
    memset("dve", ones_f.ap, 1.0, [ones_f])
    memset("dve", ones_b.ap, 1.0, [ones_b])
    memset("dve", epsb.ap, EPS, [epsb])
    A_persist = A.off

    def norm_tile(xt, g_bc, junk, ssb, xs, pb, dst_ap, dst_bufs, evac_eng, d_model=1024):
        act(junk.ap, xt.ap, AF.Square, [xt], [junk, ssb], accum=ssb.ap[:, 0:1])
        act(ssb.ap[:, 1:2], ssb.ap[:, 0:1], AF.Ln, [ssb, epsb], [ssb], bias=epsb.ap[:, 0:1], scale=1.0 / d_model)
        act(ssb.ap[:, 2:3], ssb.ap[:, 1:2], AF.Exp, [ssb], [ssb], scale=-0.5)
        stt("dve", xs.ap, xt.ap, ssb.ap[:, 2:3], g_bc.ap, ALU.mult, ALU.mult, [xt, ssb, g_bc], [xs])
        for c in range(8):
            tr(pb.bf[:, c * 128:(c + 1) * 128], xs.ap[:, c * 128:(c + 1) * 128], [xs], [pb.b])
        cp(evac_eng, dst_ap, pb.bf[:, 0:1024].rearrange("p (c t) -> p c t", c=8), [pb.b], dst_bufs)

    stg = [A.alloc([4096], BF16, "stg%d" % i) for i in range(2)]
    ff1b_b = [Buf("ff1b%d" % i) for i in range(8)]
    ff2b_b = [Buf("ff2b%d" % i) for i in range(8)]
    w_ff1_v = w_ff1.rearrange("(c p) n -> p c n", p=128)

    def stage_cast(k):
        s_ = stg[k % 2]
        if k < 8:
            dma("pool", s_.ap.rearrange("p (c n) -> p c n", c=8), w_ff1_v[:, :, k * 512:(k + 1) * 512], [], [s_], "stg%d" % (k % 2))
        else:
            j2 = k - 8
            dma("pool", s_.ap.rearrange("p (a n) -> p a n", a=4),
                w_ff2[j2 * 512:(j2 + 1) * 512, :].rearrange("(a p) n -> p a n", p=128), [], [s_], "stg%d" % (k % 2))

    def stage_store(k):
        s_ = stg[k % 2]
        if k < 8:
            dma("sp", ff1b[k], s_.ap, [s_], [ff1b_b[k]], "stgo%d" % (k % 2))
        else:
            dma("sp", ff2b[k - 8], s_.ap, [s_], [ff2b_b[k - 8]], "stgo%d" % (k % 2))

    mix_b = [Buf("mix%d" % i) for i in range(32)]
    g_mix = A.alloc([1024], F32, "g_mix")
    dma("sp", g_mix.ap, norm_mix.partition_broadcast(128), [], [g_mix], "c_gmix")
    A_hm = A.off

    Wh = A.alloc([8, 2048], BF16, "Wh")
    w_in_v = w_in.rearrange("(c p) n -> p c n", p=128)
    for c in range(8):
        dma("pool", Wh.ap[:, c, :], w_in_v[:, c, 0:2048], [], [Wh], "w_h")
    lb_bc = A.alloc([512], F32, "lb")
    oml_bc = A.alloc([512], F32, "oml")
    hgn_bc = A.alloc([512], F32, "hgn")
    l1_bc = A.alloc([512], F32, "l1")
    dma("sp", lb_bc.ap, lb_logits[0:1, :].partition_broadcast(128), [], [lb_bc], "c_lb0")
    dma("sp", l1_bc.ap, lb_logits[1:2, :].partition_broadcast(128), [], [l1_bc], "c_lb1")
    dma("sp", hgn_bc.ap, hgrn_norm.partition_broadcast(128), [], [hgn_bc], "c_hgn")
    tt("dve", lb_bc.ap, l1_bc.ap, lb_bc.ap, ALU.subtract, [l1_bc, lb_bc], [lb_bc])
    act(lb_bc.ap, lb_bc.ap, AF.Exp, [lb_bc], [lb_bc])
    ts("dve", lb_bc.ap, lb_bc.ap, 1.0, None, ALU.add, None, [lb_bc], [lb_bc])
    recip(lb_bc.ap, lb_bc.ap, [lb_bc], [lb_bc])
    ts("dve", oml_bc.ap, lb_bc.ap, -1.0, 1.0, ALU.mult, ALU.add, [lb_bc], [oml_bc])

    xt = [A.alloc([1024], F32, "xt%d" % i) for i in range(2)]
    junk = A.alloc([1024], BF16, "junk")
    ssb = [A.alloc([4], F32, "ssb%d" % i) for i in range(2)]
    xs = [A.alloc([1024], BF16, "xs%d" % i) for i in range(2)]
    hT = [A.alloc([8, 128], BF16, "hT%d" % i) for i in range(2)]
    ez = [A.alloc([512], F32, "ez%d" % i) for i in range(2)]
    vv = [A.alloc([512], BF16, "v%d" % i) for i in range(4)]
    qs = [A.alloc([512], F32, "qs%d" % i) for i in range(2)]
    eg = [A.alloc([512], F32, "eg%d" % i) for i in range(2)]
    gr = [A.alloc([512], F32, "gr%d" % i) for i in range(2)]
    ff = A.alloc([512], F32, "f")
    logf = A.alloc([512], F32, "logf")
    kk = A.alloc([512], F32, "k")
    eb = A.alloc([512], F32, "eb")
    enb = A.alloc([512], F32, "enb")
    eend = A.alloc([512], F32, "eend")
    qts = [A.alloc([512], BF16, "qt%d" % i) for i in range(2)]
    kts = [A.alloc([512], BF16, "kt%d" % i) for i in range(2)]
    kend = [A.alloc([512], BF16, "kend%d" % i) for i in range(3)]
    qkT = [A.alloc([8, 128], BF16, "qkT%d" % i) for i in range(2)]
    ATm = [A.alloc([512], BF16, "ATm%d" % i) for i in range(2)]
    Sst = A.alloc([512], F32, "Sst")
    Sb = A.alloc([512], BF16, "Sb")
    dec = [A.alloc([4], F32, "dec%d" % i) for i in range(3)]
    gs = [A.alloc([512], F32, "gs%d" % i) for i in range(3)]
    ssq = A.alloc([12], F32, "ssq")
    junkC = A.alloc([128], BF16, "junkC")
    on = A.alloc([512], BF16, "on")
    oT = [A.alloc([4, 128], BF16, "oT%d" % i) for i in range(2)]
    memset("dve", Sst.ap, 0.0, [Sst])
    memset("dve", Sb.ap, 0.0, [Sb])

    def hs(h):
        return slice(h * 128, (h + 1) * 128)

    def stageA(t):
        local = t >= 32
        sl = t % 2
        dma("sp", xt[sl].ap, xall[t * 128:(t + 1) * 128, :], [], [xt[sl]], "xt%d" % sl)
        if t % 4 == 0:
            stage_cast(t // 4)
        if t % 4 == 3:
            stage_store(t // 4)
        norm_tile(xt[sl], g_mix, junk, ssb[sl], xs[sl], P[0], hT[sl].ap, [hT[sl]], "act")
        for c in range(8):
            mm(P[1].f[:, 0:512], hT[sl].ap[:, c, :], Wh.ap[:, c, 512:1024], c == 0, c == 7, [hT[sl], Wh], [P[1].b])
        for c in range(8):
            mm(P[2].f[:, 0:512], hT[sl].ap[:, c, :], Wh.ap[:, c, 1024:1536], c == 0, c == 7, [hT[sl], Wh], [P[2].b])
        act(ez[sl].ap, P[1].f[:, 0:512], AF.Exp, [P[1].b], [ez[sl]], scale=-1.0)
        act(vv[t % 4].ap, P[2].f[:, 0:512], AF.Copy, [P[2].b], [vv[t % 4]])
        if local:
            for c in range(8):
                mm(P[1].f[:, 0:512], hT[sl].ap[:, c, :], Wh.ap[:, c, 0:512], c == 0, c == 7, [hT[sl], Wh], [P[1].b])
            for c in range(8):
                mm(P[2].f[:, 0:512], hT[sl].ap[:, c, :], Wh.ap[:, c, 1536:2048], c == 0, c == 7, [hT[sl], Wh], [P[2].b])
            cp("dve", qs[sl].ap, P[1].f[:, 0:512], [P[1].b], [qs[sl]])
            act(eg[sl].ap, P[2].f[:, 0:512], AF.Exp, [P[2].b], [eg[sl]], scale=-1.0)
            cp("dve", gr[sl].ap, P[2].f[:, 0:512], [P[2].b], [gr[sl]])

    def stageB(t):
        local = t >= 32
        sl = t % 2
        s3 = t % 3
        e_ = ez[sl]
        act(e_.ap, e_.ap, AF.Ln, [e_, ones_f], [e_], bias=ones_f.ap[:, 0:1])
        act(e_.ap, e_.ap, AF.Exp, [e_], [e_], scale=-1.0)
        tt("dve", ff.ap, e_.ap, oml_bc.ap, ALU.mult, [e_, oml_bc], [ff])
        tt("pool", ff.ap, ff.ap, lb_bc.ap, ALU.add, [ff, lb_bc], [ff])
        act(logf.ap, ff.ap, AF.Ln, [ff], [logf])
        ts("pool", kk.ap, ff.ap, -1.0, 1.0, ALU.mult, ALU.add, [ff], [kk])
        if local:
            mm(P[3].f[:, 0:512], Uc.ap, logf.ap, True, True, [Uc, logf], [P[3].b])
        mm(P[4].f[:, 0:512], Lc.ap, logf.ap, True, True, [Lc, logf], [P[4].b])
        act(eend.ap, P[4].f[:, 0:512], AF.Exp, [P[4].b], [eend])
        tt("pool", kend[s3].ap, kk.ap, eend.ap, ALU.mult, [kk, eend], [kend[s3]])
        if local:
            act(eb.ap, P[3].f[:, 0:512], AF.Exp, [P[3].b], [eb])
            act(enb.ap, P[3].f[:, 0:512], AF.Exp, [P[3].b], [enb], scale=-1.0)
            tt("dve", qts[sl].ap, qs[sl].ap, eb.ap, ALU.mult, [qs[sl], eb], [qts[sl]])
            tt("pool", kts[sl].ap, kk.ap, enb.ap, ALU.mult, [kk, enb], [kts[sl]])
            g_ = eg[sl]
            act(g_.ap, g_.ap, AF.Ln, [g_, ones_f], [g_], bias=ones_f.ap[:, 0:1])
            act(g_.ap, g_.ap, AF.Exp, [g_], [g_], scale=-1.0)
            tt("dve", gs[s3].ap, gr[sl].ap, g_.ap, ALU.mult, [gr[sl], g_], [gs[s3]])
            tt("pool", gs[s3].ap, gs[s3].ap, hgn_bc.ap, ALU.mult, [gs[s3], hgn_bc], [gs[s3]])
        for h in range(4):
            mm(P[4].f[:, h:h + 1], logf.ap[:, hs(h)], ones_f.ap[:, 0:1], True, True, [logf, ones_f], [P[4].b])
        act(dec[s3].ap, P[4].f[:, 0:4], AF.Exp, [P[4].b], [dec[s3]])

    def stageB2(t):
        sl = t % 2
        qt, kt = qts[sl], kts[sl]
        for h in range(4):
            tr(P[5].bf[:, h * 128:(h + 1) * 128], qt.ap[:, hs(h)], [qt], [P[5].b])
        for h in range(4):
            tr(P[5].bf[:, 512 + h * 128:512 + (h + 1) * 128], kt.ap[:, hs(h)], [kt], [P[5].b])
        cp("dve", qkT[sl].ap, P[5].bf[:, 0:1024].rearrange("p (c t) -> p c t", c=8), [P[5].b], [qkT[sl]])
        for h in range(4):
            mm(P[5].f[:, hs(h)], qkT[sl].ap[:, 4 + h, :], qkT[sl].ap[:, h, :], True, True, [qkT[sl]], [P[5].b])
        tt("dve", ATm[sl].ap, P[5].f[:, 0:512], mask4.ap, ALU.mult, [P[5].b, mask4], [ATm[sl]])

    def stageC(t):
        local = t >= 32
        sl = t % 2
        v_ = vv[t % 4]
        s3 = t % 3
        if local:
            for h in range(4):
                mm(P[7].f[:, hs(h)], ATm[sl].ap[:, hs(h)], v_.ap[:, hs(h)], True, False, [ATm[sl], v_], [P[7].b])
                mm(P[7].f[:, hs(h)], qkT[sl].ap[:, h, :], Sb.ap[:, hs(h)], False, True, [qkT[sl], Sb], [P[7].b])
        for h in range(4):
            mm(P[6].f[:, hs(h)], kend[s3].ap[:, hs(h)], v_.ap[:, hs(h)], True, True, [kend[s3], v_], [P[6].b])
        for h in range(4):
            stt("dve", Sst.ap[:, hs(h)], Sst.ap[:, hs(h)], dec[s3].ap[:, h:h + 1], P[6].f[:, hs(h)], ALU.mult, ALU.add,
                [Sst, dec[s3], P[6].b], [Sst])
        cp("pool", Sb.ap, Sst.ap, [Sst], [Sb])
        if local:
            i = t - 32
            for h in range(4):
                act(junkC.ap[:, 0:128], P[7].f[:, hs(h)], AF.Square, [P[7].b], [junkC, ssq], accum=ssq.ap[:, h:h + 1])
            act(ssq.ap[:, 4:8], ssq.ap[:, 0:4], AF.Ln, [ssq, epsb], [ssq], bias=epsb.ap[:, 0:1], scale=1.0 / 128)
            act(ssq.ap[:, 8:12], ssq.ap[:, 4:8], AF.Exp, [ssq], [ssq], scale=-0.5)
            for h in range(4):
                stt("dve", on.ap[:, hs(h)], P[7].f[:, hs(h)], ssq.ap[:, 8 + h:9 + h], gs[s3].ap[:, hs(h)], ALU.mult, ALU.mult,
                    [P[7].b, ssq, gs[s3]], [on])
            for h in range(4):
                tr(P[7].bf[:, h * 128:(h + 1) * 128], on.ap[:, hs(h)], [on], [P[7].b])
            cp("act", oT[sl].ap, P[7].bf[:, 0:512].rearrange("p (c t) -> p c t", c=4), [P[7].b], [oT[sl]])
            dma("sp", mixT[0:512, i * 128:(i + 1) * 128].rearrange("(h p) t -> p h t", p=128), oT[sl].ap,
                [oT[sl]], [mix_b[i]], "mixo%d" % sl)

    import os as _os
    for it in range(0 if _os.environ.get('SKIP_H') else 64 + 3):
        lists = []
        if it < 64:
            S.begin(); stageA(it); lists.append(S.end())
        if 0 <= it - 1 < 64:
            S.begin(); stageB(it - 1); lists.append(S.end())
        if 32 <= it - 2 < 64:
            S.begin(); stageB2(it - 2); lists.append(S.end())
        if 0 <= it - 3 < 64:
            S.begin(); stageC(it - 3); lists.append(S.end())
        S.merged(lists)

    SCALE = 128 ** -0.5
    for hp in range(0 if _os.environ.get('SKIP_M') else 2):
        S.barrier()
        A.off = A_hm
        Wm = A.alloc([8, 768], BF16, "Wm")
        for c in range(8):
            dma("pool", Wm.ap[:, c, 0:256], w_in_v[:, c, 2048 + hp * 256:2048 + (hp + 1) * 256], [], [Wm], "w_m")
            dma("pool", Wm.ap[:, c, 256:512], w_in_v[:, c, 2560 + hp * 256:2560 + (hp + 1) * 256], [], [Wm], "w_m")
            dma("pool", Wm.ap[:, c, 512:768], w_in_v[:, c, 3072 + hp * 256:3072 + (hp + 1) * 256], [], [Wm], "w_m")
        KT = A.alloc([2, 8192], BF16, "KT")
        KT_b = [Buf("KT%d" % i) for i in range(64)]
        Vaug = A.alloc([64, 2, 130], BF16, "Vaug")
        V_b = [Buf("V%d" % i) for i in range(64)]
        KmF = A.alloc([2, 32], F32, "KmF")
        KmT = A.alloc([2, 32], BF16, "KmT")
        vb_bc = A.alloc([512], F32, "vb")
        dma("sp", vb_bc.ap, vbias.partition_broadcast(128), [], [vb_bc], "c_vb")
        xt = [A.alloc([1024], F32, "xt%d" % i) for i in range(2)]
        junks = [A.alloc([1024], BF16, "junk%d" % i) for i in range(2)]
        ssb = [A.alloc([4], F32, "ssb%d" % i) for i in range(2)]
        xs = [A.alloc([1024], BF16, "xs%d" % i) for i in range(2)]
        hT = [A.alloc([8, 128], BF16, "hT%d" % i) for i in range(2)]
        rt = [A.alloc([32], F32, "rt%d" % i) for i in range(2)]
        ras = [A.alloc([4, 16], F32, "ra%d" % i) for i in range(2)]
        rbs = [A.alloc([4, 16], F32, "rb%d" % i) for i in range(2)]
        qkrs = [A.alloc([4, 128], BF16, "qkr%d" % i) for i in range(2)]
        QTs = [A.alloc([2, 128], BF16, "QT%d" % i) for i in range(2)]
        Gp = A.alloc([32], F32, "Gp")
        mx8 = A.alloc([8], F32, "mx8")
        bqs = [[A.alloc([32], F32, "bq%d_%d" % (i, k)) for k in range(2)] for i in range(2)]
        Pm = [A.alloc([512], BF16, "Pm%d" % i) for i in range(4)]
        PT = [A.alloc([512], BF16, "PT%d" % i) for i in range(3)]
        rden = A.alloc([2], F32, "rden")
        om = A.alloc([2, 128], BF16, "om")
        omT = [A.alloc([2, 128], BF16, "omT%d" % i) for i in range(2)]
        memset("pool", Vaug.flat, 1.0, V_b)
        memset("dve", KmF.flat, 0.0, [KmF])
        memset("dve", KmT.flat, 0.0, [KmT])

        def front(t, pa, pq):
            local = t >= 32
            sl = t % 2
            nblk = t // 2
            ra, rb, qkr, QT = ras[sl], rbs[sl], qkrs[sl], QTs[sl]
            dma("sp", xt[sl].ap, xall[t * 128:(t + 1) * 128, :], [], [xt[sl]], "xt%d" % sl)
            dma("sp", rt[sl].ap, rot[t * 128:(t + 1) * 128, :], [], [rt[sl]], "rt%d" % sl)
            norm_tile(xt[sl], g_mix, junks[sl], ssb[sl], xs[sl], pa, hT[sl].ap, [hT[sl]], "dve")
            if local:
                for c in range(8):
                    mm(pq.f[:, 0:256], hT[sl].ap[:, c, :], Wm.ap[:, c, 0:256], c == 0, False, [hT[sl], Wm], [pq.b])
            for c in range(8):
                mm(pq.f[:, 256:512], hT[sl].ap[:, c, :], Wm.ap[:, c, 256:512], (c == 0 and not local), c == 7,
                   [hT[sl], Wm], [pq.b])
            for c in range(8):
                mm(pa.f[:, 0:256], hT[sl].ap[:, c, :], Wm.ap[:, c, 512:768], c == 0, c == 7, [hT[sl], Wm], [pa.b])
            cp("dve", Vaug.ap[:, t, :, 0:128], pa.f[:, 0:256].rearrange("p (h d) -> p h d", h=2), [pa.b], [V_b[t]])
            h0 = 0 if local else 2
            nh = 4 - h0
            X = pq.f[:, 0:512].rearrange("p (h d) -> p h d", h=4)
            cosb = rt[sl].ap[:, 0:16].unsqueeze(1).to_broadcast([128, nh, 16])
            sinb = rt[sl].ap[:, 16:32].unsqueeze(1).to_broadcast([128, nh, 16])
            x1 = X[:, h0:4, 0:16]
            x2 = X[:, h0:4, 16:32]
            tt("dve", ra.ap[:, h0:4, :], x1, cosb, ALU.mult, [pq.b, rt[sl]], [ra])
            tt("dve", rb.ap[:, h0:4, :], x2, sinb, ALU.mult, [pq.b, rt[sl]], [rb])
            tt("dve", qkr.ap[:, h0:4, 0:16], ra.ap[:, h0:4, :], rb.ap[:, h0:4, :], ALU.subtract, [ra, rb], [qkr])
            tt("dve", ra.ap[:, h0:4, :], x2, cosb, ALU.mult, [pq.b, rt[sl]], [ra])
            tt("dve", rb.ap[:, h0:4, :], x1, sinb, ALU.mult, [pq.b, rt[sl]], [rb])
            tt("dve", qkr.ap[:, h0:4, 16:32], ra.ap[:, h0:4, :], rb.ap[:, h0:4, :], ALU.add, [ra, rb], [qkr])
            cp("dve", qkr.ap[:, h0:4, 32:128], X[:, h0:4, 32:128], [pq.b], [qkr])
            for j in range(h0, 4):
                tr(pa.bf[:, j * 128:(j + 1) * 128], qkr.ap[:, j, :], [qkr], [pa.b])
            cp("dve", KT.ap[:, :, t * 128:(t + 1) * 128], pa.bf[:, 256:512].rearrange("p (h t) -> p h t", h=2),
               [pa.b], [KT_b[t]])
            if local:
                cp("dve", QT.ap, pa.bf[:, 0:256].rearrange("p (h t) -> p h t", h=2), [pa.b], [QT])
            for h in range(2):
                mm(pq.f[:, h:h + 1], qkr.ap[:, 2 + h, :], ones_b.ap[:, 0:1], True, True, [qkr, ones_b], [pq.b])
            stt("dve", KmF.ap[:, :, nblk], pq.f[:, 0:2], 1.0 / 256, KmF.ap[:, :, nblk], ALU.mult, ALU.add,
                [pq.b, KmF], [KmF])
            cp("dve", KmT.ap[:, :, nblk], KmF.ap[:, :, nblk], [KmF], [KmT])
            if local:
                j_loc = (t - 32) // 2
                for h in range(2):
                    bq_ = bqs[sl][h]
                    mm(pq.f[:, 0:32], QT.ap[:, h, :], KmT.ap[:, h, :], True, True, [QT, KmT], [pq.b])
                    tt("dve", Gp.ap, pq.f[:, 0:32], vb_bc.ap[:, j_loc * 32:(j_loc + 1) * 32], ALU.add, [pq.b, vb_bc], [Gp])
                    vmax(mx8.ap, Gp.ap, [Gp], [mx8])
                    ts("dve", bq_.ap, Gp.ap, mx8.ap[:, 2:3], NEG, ALU.is_lt, ALU.mult, [Gp, mx8], [bq_])
                    tt("dve", bq_.ap, bq_.ap, vb_bc.ap[:, j_loc * 32:(j_loc + 1) * 32], ALU.add, [bq_, vb_bc], [bq_])

        def attention(t):
            sl = t % 2
            QT = QTs[sl]
            J = t // 2
            i = t - 32
            j_loc = i // 2
            po = P[7]
            units = []
            totals = []
            for h in range(2):
                hu = []
                n0 = 0
                while n0 < J:
                    nb_ = 2 if n0 + 1 < J else 1
                    hu.append(("past", n0, nb_, h))
                    n0 += nb_
                nk_own = 1 if (t % 2 == 0) else 2
                hu.append(("own", J, nk_own, h))
                totals.append(sum((u[2] * 2 if u[0] == "past" else u[2]) for u in hu))
                units += hu
            kt_done = [0, 0]

            def stage_S(ui, u):
                h = u[3]
                pb = P[2 + ui % 3]
                pm = Pm[ui % 4]
                if u[0] == "past":
                    n0_, nb2 = u[1], u[2]
                    w = nb2 * 256
                    kbufs = [KT_b[2 * n0_ + x] for x in range(2 * nb2)]
                    mm(pb.f[:, 0:w], QT.ap[:, h, :], KT.ap[:, h, n0_ * 256:n0_ * 256 + w], True, True, [QT] + kbufs, [pb.b])
                    for x in range(nb2):
                        act(pm.ap[:, x * 256:(x + 1) * 256], pb.f[:, x * 256:(x + 1) * 256], AF.Exp, [pb.b, bqs[sl][h]], [pm],
                            bias=bqs[sl][h].ap[:, n0_ + x:n0_ + x + 1], scale=SCALE)
                else:
                    nk = u[2]
                    w = nk * 128
                    kbufs = [KT_b[2 * J + x] for x in range(nk)]
                    mm(pb.f[:, 0:w], QT.ap[:, h, :], KT.ap[:, h, J * 256:J * 256 + w], True, False, [QT] + kbufs, [pb.b])
                    mm(pb.f[:, w - 128:w], ident.ap, triB.ap, False, True, [ident, triB], [pb.b])
                    act(pm.ap[:, 0:w], pb.f[:, 0:w], AF.Exp, [pb.b], [pm], scale=SCALE)

            def stage_T(ui, u):
                pb = P[5 + ui % 2]
                pm = Pm[ui % 4]
                pt = PT[ui % 3]
                nkt = u[2] * 2 if u[0] == "past" else u[2]
                for x in range(nkt):
                    tr(pb.bf[:, x * 128:(x + 1) * 128], pm.ap[:, x * 128:(x + 1) * 128], [pm], [pb.b])
                cp("dve", pt.ap[:, 0:nkt * 128], pb.bf[:, 0:nkt * 128], [pb.b], [pt])

            def stage_PV(ui, u):
                h = u[3]
                pt = PT[ui % 3]
                nkt = u[2] * 2 if u[0] == "past" else u[2]
                kt0 = u[1] * 2
                for x in range(nkt):
                    first = kt_done[h] == 0
                    kt_done[h] += 1
                    last = kt_done[h] == totals[h]
                    mm(po.f[:, 128:258], pt.ap[:, x * 128:(x + 1) * 128], Vaug.ap[:, kt0 + x, h, :], first, last,
                       [pt, V_b[kt0 + x]], [po.b])
                if kt_done[h] == totals[h]:
                    recip(rden.ap[:, h:h + 1], po.f[:, 256:257], [po.b], [rden])
                    act(om.ap[:, h, :], po.f[:, 128:256], AF.Identity, [po.b, rden], [om], scale=rden.ap[:, h:h + 1])

            nu = len(units)
            D1, D2 = 2, 4
            for step in range(nu + D2):
                if step < nu:
                    stage_S(step, units[step])
                if 0 <= step - D1 < nu:
                    stage_T(step - D1, units[step - D1])
                if 0 <= step - D2 < nu:
                    stage_PV(step - D2, units[step - D2])
            for h in range(2):
                tr(po.bf[:, h * 128:(h + 1) * 128], om.ap[:, h, :], [om], [po.b])
            cp("dve", omT[sl].ap, po.bf[:, 0:256].rearrange("p (h t) -> p h t", h=2), [po.b], [omT[sl]])
            r0 = 512 + hp * 256
            dma("sp", mixT[r0:r0 + 256, i * 128:(i + 1) * 128].rearrange("(h p) t -> p h t", p=128), omT[sl].ap,
                [omT[sl]], [mix_b[i]], "mixo%d" % sl)

        for k in range(16):
            S.begin(); front(2 * k, P[0], P[1]); la = S.end()
            S.begin(); front(2 * k + 1, P[2], P[3]); lb_ = S.end()
            S.merged([la, lb_])
        S.begin(); front(32, P[0], P[1]); la = S.end()
        S.merged([la])
        for t in range(32, 64):
            lists = []
            S.begin(); attention(t); lists.append(S.end())
            if t + 1 < 64:
                S.begin(); front(t + 1, P[0], P[1]); lists.append(S.end())
            S.merged(lists)

    S.barrier()
    A.off = A_persist
    Wo = A.alloc([8, 1024], BF16, "Wo")
    Wxq = A.alloc([8, 1024], BF16, "Wxq")
    Wxo = A.alloc([8, 1024], BF16, "Wxo")
    for c in range(8):
        dma("pool", Wo.ap[:, c, :], w_out.rearrange("(c p) n -> p c n", p=128)[:, c, :], [], [Wo], "w_o")
        dma("pool", Wxq.ap[:, c, :], w_xq.rearrange("(c p) n -> p c n", p=128)[:, c, :], [], [Wxq], "w_xq")
        dma("pool", Wxo.ap[:, c, :], w_xo.rearrange("(c p) n -> p c n", p=128)[:, c, :], [], [Wxo], "w_xo")
    memKT = A.alloc([8, 256], BF16, "memKT")
    memV = A.alloc([2, 1024], BF16, "memV")
    g_xa = A.alloc([1024], F32, "g_xa")
    g_mlp = A.alloc([1024], F32, "g_mlp")
    g_fin = A.alloc([1024], F32, "g_fin")
    dma("sp", g_xa.ap, norm_xattn.partition_broadcast(128), [], [g_xa], "c_gxa")
    dma("sp", g_mlp.ap, norm_mlp.partition_broadcast(128), [], [g_mlp], "c_gmlp")
    dma("sp", g_fin.ap, norm_final.partition_broadcast(128), [], [g_fin], "c_gfin")
    W1 = [A.alloc([8, 512], BF16, "W1_%d" % i) for i in range(2)]
    W2 = [A.alloc([4, 1024], BF16, "W2_%d" % i) for i in range(2)]
    X = [A.alloc([1024], F32, "X%d" % i) for i in range(4)]
    xs = [A.alloc([1024], BF16, "xsF%d" % i) for i in range(4)]
    junk = A.alloc([1024], BF16, "junkF")
    ssb = [A.alloc([4], F32, "ssbF%d" % i) for i in range(4)]
    hTs = A.alloc([8, 512], BF16, "hTs")
    hTs_b = [Buf("hTs%d" % i) for i in range(4)]

    def norm4(g_bc):
        lists = []
        for j in range(4):
            S.begin()
            norm_tile(X[j], g_bc, junk, ssb[j], xs[j], P[j], hTs.ap[:, :, j * 128:(j + 1) * 128], [hTs_b[j]], "act")
            lists.append(S.end())
        S.merged(lists)
    qxT = A.alloc([8, 512], BF16, "qxT")
    oxT = A.alloc([8, 512], BF16, "oxT")
    PTxs = [A.alloc([2, 512], BF16, "PTx%d" % i) for i in range(2)]
    rds = [A.alloc([512], F32, "rd%d" % i) for i in range(2)]
    qxT_b = [Buf("qxT%d" % i) for i in range(8)]
    oxT_b = [Buf("oxT%d" % i) for i in range(8)]
    rl = [A.alloc([512], F32, "rl%d" % i) for i in range(2)]
    hid = A.alloc([32, 512], BF16, "hid")
    hid_off = A.off - 32 * 512
    Wxk_ap = arena[:, hid_off:hid_off + 8192].rearrange("p (c n) -> p c n", c=8)
    Wxv_ap = arena[:, hid_off + 8192:hid_off + 16384].rearrange("p (c n) -> p c n", c=8)
    for c in range(8):
        dma("pool", Wxk_ap[:, c, :], w_xk.rearrange("(c p) n -> p c n", p=128)[:, c, :], [], [hid], "w_xk")
        dma("pool", Wxv_ap[:, c, :], w_xv.rearrange("(c p) n -> p c n", p=128)[:, c, :], [], [hid], "w_xk")
    g_mem = X[3]
    dma("sp", g_mem.ap, norm_mem.partition_broadcast(128), [], [g_mem], "c_gmem")
    for m in range(2):
        dma("sp", X[m].ap, mem[m * 128:(m + 1) * 128, :], [], [X[m]], "X%d" % m)
        norm_tile(X[m], g_mem, junk, ssb[m], xs[m], P[0], hTs.ap[:, :, m * 128:(m + 1) * 128], [hTs_b[m]], "act")
    for ch in range(8):
        pb = P[1 + ch % 2]
        for c in range(8):
            mm(pb.f[:, 0:256], Wxk_ap[:, c, ch * 128:(ch + 1) * 128], hTs.ap[:, c, 0:256], c == 0, c == 7, [hid] + hTs_b[0:2], [pb.b])
        cp("dve", memKT.ap[:, ch, :], pb.f[:, 0:256], [pb.b], [memKT])
    for m in range(2):
        for hh in range(2):
            pb = P[3 + hh]
            for c in range(8):
                mm(pb.f[:, 0:512], hTs.ap[:, c, m * 128:(m + 1) * 128], Wxv_ap[:, c, hh * 512:(hh + 1) * 512], c == 0, c == 7,
                   [hTs_b[m], hid], [pb.b])
            cp("act", memV.ap[:, m, hh * 512:(hh + 1) * 512], pb.f[:, 0:512], [pb.b], [memV])

    XSCALE = 256 ** -0.5
    mixT_v = mixT.rearrange("(c p) t -> p c t", p=128)
    for s in range(8):
        dma("sp", oxT.ap, mixT_v[:, :, s * 512:(s + 1) * 512], [mix_b[4 * s + j] for j in range(4)], oxT_b, "mixin")
        for j in range(4):
            tok0 = 4096 + (s * 4 + j) * 128
            dma("sp", X[j].ap, xall[tok0:tok0 + 128, :], [], [X[j]], "X%d" % j)
        for j in range(4):
            for hh in range(2):
                pb = P[1 + (2 * j + hh) % 2]
                for c in range(8):
                    mm(pb.f[:, 0:512], oxT.ap[:, c, j * 128:(j + 1) * 128], Wo.ap[:, c, hh * 512:(hh + 1) * 512], c == 0, c == 7,
                       [oxT_b[c], Wo], [pb.b])
                tt("dve", X[j].ap[:, hh * 512:(hh + 1) * 512], X[j].ap[:, hh * 512:(hh + 1) * 512], pb.f[:, 0:512], ALU.add,
                   [X[j], pb.b], [X[j]])
        norm4(g_xa)
        def xattn_lane(L):
            B = P[4 * L:4 * L + 4]
            ptx, rd_ = PTxs[L], rds[L]
            for h in (L, L + 2):
                for dd in range(2):
                    ch = 2 * h + dd
                    for c in range(8):
                        mm(B[0].f[:, 0:512], Wxq.ap[:, c, ch * 128:(ch + 1) * 128], hTs.ap[:, c, :], c == 0, c == 7, [Wxq] + hTs_b, [B[0].b])
                    cp("act", qxT.ap[:, ch, :], B[0].f[:, 0:512], [B[0].b], [qxT_b[ch]])
                for m in range(2):
                    pb = B[1 + m]
                    for dd in range(2):
                        mm(pb.f[:, 0:512], memKT.ap[:, 2 * h + dd, m * 128:(m + 1) * 128], qxT.ap[:, 2 * h + dd, :], dd == 0, dd == 1,
                           [memKT, qxT_b[2 * h + dd]], [pb.b])
                    act(ptx.ap[:, m, :], pb.f[:, 0:512], AF.Exp, [pb.b], [ptx], scale=XSCALE)
                for m in range(2):
                    mm(B[3].f[:, 0:512], ones_b.ap, ptx.ap[:, m, :], m == 0, m == 1, [ones_b, ptx], [B[3].b])
                act(rd_.ap, B[3].f[:, 0:512], AF.Ln, [B[3].b], [rd_])
                act(rd_.ap, rd_.ap, AF.Exp, [rd_], [rd_], scale=-1.0)
                for dd in range(2):
                    pb = B[1 + dd]
                    for m in range(2):
                        mm(pb.f[:, 0:512], memV.ap[:, m, (2 * h + dd) * 128:(2 * h + dd + 1) * 128], ptx.ap[:, m, :], m == 0, m == 1,
                           [memV, ptx], [pb.b])
                    tt("dve", oxT.ap[:, 2 * h + dd, :], pb.f[:, 0:512], rd_.ap, ALU.mult, [pb.b, rd_], [oxT_b[2 * h + dd]])

        lanes = []
        for L in range(2):
            S.begin(); xattn_lane(L); lanes.append(S.end())
        S.merged(lanes)
        for j in range(4):
            for hh in range(2):
                pb = P[1 + (2 * j + hh) % 2]
                for c in range(8):
                    mm(pb.f[:, 0:512], oxT.ap[:, c, j * 128:(j + 1) * 128], Wxo.ap[:, c, hh * 512:(hh + 1) * 512], c == 0, c == 7,
                       [oxT_b[c], Wxo], [pb.b])
                tt("dve", X[j].ap[:, hh * 512:(hh + 1) * 512], X[j].ap[:, hh * 512:(hh + 1) * 512], pb.f[:, 0:512], ALU.add,
                   [X[j], pb.b], [X[j]])
        norm4(g_mlp)
        for fc in range(8):
            w1 = W1[fc % 2]
            dma("sp", w1.ap.rearrange("p c n -> p (c n)"), ff1b[fc], [ff1b_b[fc]], [w1], "W1_%d" % (fc % 2))
            for sub in range(4):
                k_ = fc * 4 + sub
                pb = P[1 + k_ % 2]
                for c in range(8):
                    mm(pb.f[:, 0:512], w1.ap[:, c, sub * 128:(sub + 1) * 128], hTs.ap[:, c, :], c == 0, c == 7, [w1] + hTs_b, [pb.b])
                r_ = rl[k_ % 2]
                act(r_.ap, pb.f[:, 0:512], AF.Relu, [pb.b], [r_])
                tt("pool", hid.ap[:, k_, :], r_.ap, r_.ap, ALU.mult, [r_], [hid])
        for j2 in range(8):
            w2 = W2[j2 % 2]
            dma("sp", w2.ap.rearrange("p a n -> p (a n)"), ff2b[j2], [ff2b_b[j2]], [w2], "W2_%d" % (j2 % 2))
            for j in range(4):
                for hh in range(2):
                    pb = P[j * 2 + hh]
                    for sub in range(4):
                        mm(pb.f[:, 0:512], hid.ap[:, j2 * 4 + sub, j * 128:(j + 1) * 128], w2.ap[:, sub, hh * 512:(hh + 1) * 512],
                           (j2 == 0 and sub == 0), (j2 == 7 and sub == 3), [hid, w2], [pb.b])
        for j in range(4):
            for hh in range(2):
                pb = P[j * 2 + hh]
                tt("dve", X[j].ap[:, hh * 512:(hh + 1) * 512], X[j].ap[:, hh * 512:(hh + 1) * 512], pb.f[:, 0:512], ALU.add,
                   [X[j], pb.b], [X[j]])
        for j in range(4):
            sb_ = ssb[j]
            act(junk.ap, X[j].ap, AF.Square, [X[j]], [junk, sb_], accum=sb_.ap[:, 0:1])
            act(sb_.ap[:, 1:2], sb_.ap[:, 0:1], AF.Ln, [sb_, epsb], [sb_], bias=epsb.ap[:, 0:1], scale=1.0 / 1024)
            act(sb_.ap[:, 2:3], sb_.ap[:, 1:2], AF.Exp, [sb_], [sb_], scale=-0.5)
            stt("dve", X[j].ap, X[j].ap, sb_.ap[:, 2:3], g_fin.ap, ALU.mult, ALU.mult, [X[j], sb_, g_fin], [X[j]])
            r0 = (s * 4 + j) * 128
            dma("sp", out[r0:r0 + 128, :], X[j].ap, [X[j]], [], "X%d" % j, is_out=True)

    S.emit(st)
    st.close()
    return nc


_NC_CACHE = {}


def _host_consts():
    ar = np.arange(128)
    U = (ar[:, None] <= ar[None, :]).astype(np.float32)
    L = (ar[:, None] > ar[None, :]).astype(np.float32)
    mask4 = np.tile(U, (1, 4)).astype(np.float32)
    tri = np.where(ar[None, :] <= ar[:, None], 0.0, NEG).astype(np.float32)
    return {
        "cU": U, "cL": L, "cMask4": mask4,
        "cTriB": tri.astype(ml_dtypes.bfloat16),
        "cIdent": np.eye(128, dtype=np.float32).astype(ml_dtypes.bfloat16),
    }


def _rot_table(positions):
    half = 16
    inv_freq = (np.float32(500000.0) ** (-np.arange(half, dtype=np.float32) * np.float32(2.0) / np.float32(32))).astype(np.float32)
    ang = positions.astype(np.float32)[:, None] * inv_freq[None, :]
    return np.concatenate([np.cos(ang), np.sin(ang)], axis=1).astype(np.float32)


def make_in_maps(inputs):
    x = np.asarray(inputs["x"], dtype=np.float32)
    memv = np.asarray(inputs["mem"], dtype=np.float32)
    consts = _host_consts()
    shared = {
        "w_in": np.ascontiguousarray(inputs["w_in"][0]),
        "w_out": np.ascontiguousarray(inputs["w_out"][0]),
        "w_xq": np.ascontiguousarray(inputs["w_xq"][0]),
        "w_xk": np.ascontiguousarray(inputs["w_xk"][0]),
        "w_xv": np.ascontiguousarray(inputs["w_xv"][0]),
        "w_xo": np.ascontiguousarray(inputs["w_xo"][0]),
        "w_ff1": np.ascontiguousarray(inputs["w_ff1"][0]),
        "w_ff2": np.ascontiguousarray(inputs["w_ff2"][0]),
        "norm_mix": np.ascontiguousarray(inputs["norm_mix"][0:1]),
        "norm_xattn": np.ascontiguousarray(inputs["norm_xattn"][0:1]),
        "norm_mem": np.ascontiguousarray(inputs["norm_mem"][0:1]),
        "norm_mlp": np.ascontiguousarray(inputs["norm_mlp"][0:1]),
        "norm_final": np.ascontiguousarray(np.asarray(inputs["norm_final"]).reshape(1, 1024)),
        "hgrn_norm": np.ascontiguousarray(inputs["hgrn_norm"][0:1]),
        "lb_logits": np.ascontiguousarray(inputs["lb_logits"]),
    }
    shared = {k: np.asarray(v, dtype=np.float32) for k, v in shared.items()}
    shared.update(consts)
    in_maps = []
    for c in range(8):
        b, hf = c // 2, c % 2
        xall = np.zeros((8192, 1024), np.float32)
        if hf == 1:
            xall[:4096] = x[b, :4096]
        xall[4096:] = x[b, hf * 4096:(hf + 1) * 4096]
        pos = np.concatenate([np.arange(4096), hf * 4096 + np.arange(4096)])
        vb = np.full((16, 32), NEG, np.float32)
        for j in range(16):
            lo = 16 if hf == 0 else 0
            vb[j, lo:16 + j] = 0.0
        m = dict(shared)
        m["xall"] = xall
        m["mem"] = np.ascontiguousarray(memv[b])
        m["rot"] = _rot_table(pos)
        m["vbias"] = vb.reshape(1, 512)
        in_maps.append(m)
    return in_maps


def kernel(**inputs):
    if "nc" not in _NC_CACHE:
        _NC_CACHE["nc"] = build_nc()
    nc = _NC_CACHE["nc"]
    in_maps = make_in_maps(inputs)
    res = run_bass_kernel_spmd(nc, in_maps, core_ids=list(range(8)))
    outp = np.zeros((4, 8192, 1024), np.float32)
    for c in range(8):
        b, hf = c // 2, c % 2
        outp[b, hf * 4096:(hf + 1) * 4096] = np.asarray(res.results[c]["out"], dtype=np.float32)
    return outp
```

```python
import numpy as np
from contextlib import ExitStack
import ml_dtypes
import concourse.bass as bass
import concourse.mybir as mybir
from concourse.bass_utils import run_bass_kernel_spmd

F32 = mybir.dt.float32
BF16 = mybir.dt.bfloat16
AF = mybir.ActivationFunctionType
ALU = mybir.AluOpType

ENGS = ("pe", "act", "dve", "pool", "sp")
EPS = 1e-6
NEG = -1.0e4


class Buf:
    __slots__ = ("name", "last_w", "readers", "excl")

    def __init__(self, name=""):
        self.name = name
        self.excl = False
        self.last_w = None
        self.readers = []


class Op:
    __slots__ = ("eng", "fn", "deps", "is_dma", "lane", "lane_val", "pos", "signal", "sigval", "waits", "idx")


class Sched:
    def __init__(self, nc):
        self.nc = nc
        self.ops = []
        self.streams = {e: [] for e in ENGS}
        self.lane_count = {}
        self.lane_last = {}
        self.out_lanes = set()
        self.bar = []
        self.bar_pending = {e: False for e in ENGS}

    def barrier(self):
        bar = []
        for e in ENGS:
            for op in reversed(self.streams[e]):
                if not op.is_dma:
                    bar.append(op.idx)
                    break
        for lane, idx in self.lane_last.items():
            bar.append(idx)
        self.bar = bar
        self.bar_pending = {e: True for e in ENGS}

    def _record(self, eng, fn, reads, writes, is_dma=False, lane=None):
        op = Op()
        op.eng = eng
        op.fn = fn
        op.is_dma = is_dma
        op.lane = lane
        op.signal = False
        op.sigval = 0
        op.waits = []
        op.idx = len(self.ops)
        if any(b.excl for b in reads):
            writes = writes + [b for b in reads if b.excl and b not in writes]
        deps = {}
        for b in reads:
            if b.last_w is not None:
                deps[b.last_w] = True
        for b in writes:
            if b.last_w is not None:
                deps.setdefault(b.last_w, False)
            for r in b.readers:
                deps.setdefault(r, False)
        if self.bar_pending[eng]:
            for i in self.bar:
                deps.setdefault(i, False)
            self.bar_pending[eng] = False
        op.deps = deps
        for b in reads:
            b.readers.append(op.idx)
        for b in writes:
            b.last_w = op.idx
            b.readers = []
        if is_dma:
            c = self.lane_count.get(lane, 0) + 1
            self.lane_count[lane] = c
            op.lane_val = 16 * c
            self.lane_last[lane] = op.idx
        op.pos = len(self.streams[eng])
        self.streams[eng].append(op)
        self.ops.append(op)
        return op

    _cap = None

    def begin(self):
        self._cap = []

    def end(self):
        l = self._cap
        self._cap = None
        return l

    def merged(self, lists):
        items = []
        for li, l in enumerate(lists):
            n = len(l)
            for i, it in enumerate(l):
                items.append(((i + 0.5) / n, li, i, it))
        items.sort(key=lambda x: (x[0], x[1], x[2]))
        for _, _, _, it in items:
            if it[0] == "op":
                self.op(*it[1:])
            else:
                self.dma(*it[1:])

    def op(self, eng, fn, reads=(), writes=()):
        if self._cap is not None:
            self._cap.append(("op", eng, fn, list(reads), list(writes)))
            return None
        return self._record(eng, fn, list(reads), list(writes))

    def dma(self, eng, fn, reads=(), writes=(), lane=None, is_out=False):
        if self._cap is not None:
            self._cap.append(("dma", eng, fn, list(reads), list(writes), lane, is_out))
            return None
        if is_out:
            self.out_lanes.add(lane)
        return self._record(eng, fn, list(reads), list(writes), is_dma=True, lane=lane)

    def resolve(self):
        seen = {e: {} for e in ENGS}
        seen_lane = {e: {} for e in ENGS}
        for op in self.ops:
            E = op.eng
            need = {}
            need_lane = {}
            for di, is_raw in op.deps.items():
                d = self.ops[di]
                if d.is_dma:
                    if seen_lane[E].get(d.lane, 0) < d.lane_val:
                        need_lane[d.lane] = max(need_lane.get(d.lane, 0), d.lane_val)
                else:
                    if d.eng == E and E == "pe":
                        continue
                    if seen[E].get(d.eng, -1) < d.pos:
                        need[d.eng] = max(need.get(d.eng, -1), d.pos)
            for Ed, pos in need.items():
                seen[E][Ed] = pos
                self.streams[Ed][pos].signal = True
                op.waits.append(("c", Ed, pos))
            for lane, val in need_lane.items():
                seen_lane[E][lane] = val
                op.waits.append(("d", lane, val))
        for e in ENGS:
            c = 0
            for op in self.streams[e]:
                if (not op.is_dma) and op.signal:
                    c += 1
                    op.sigval = c

    def emit(self, stack):
        nc = self.nc
        self.resolve()
        esem = {e: stack.enter_context(nc.semaphore("cs_" + e)) for e in ENGS}
        lsem = {}
        for lane in self.lane_count:
            lsem[lane] = stack.enter_context(nc.semaphore("dl_" + lane))
        block = stack.enter_context(nc.Block())
        streams = self.streams
        out_lanes = self.out_lanes
        lane_count = self.lane_count

        def run(eng_name, e):
            for op in streams[eng_name]:
                for w in op.waits:
                    if w[0] == "c":
                        e.wait_ge(esem[w[1]], streams[w[1]][w[2]].sigval)
                    else:
                        e.wait_ge(lsem[w[1]], w[2])
                ins = op.fn(e)
                if op.is_dma:
                    ins.then_inc(lsem[op.lane], 16)
                elif op.signal:
                    ins.then_inc(esem[eng_name], 1)
            if eng_name == "sp":
                for lane in sorted(out_lanes):
                    e.wait_ge(lsem[lane], 16 * lane_count[lane])

        @block.tensor
        def _(e):
            run("pe", e)

        @block.scalar
        def _(e):
            run("act", e)

        @block.vector
        def _(e):
            run("dve", e)

        @block.gpsimd
        def _(e):
            run("pool", e)

        @block.sync
        def _(e):
            run("sp", e)


class T:
    __slots__ = ("ap", "b", "flat")

    def __init__(self, ap, b=None, name=""):
        self.ap = ap
        self.flat = ap
        self.b = b if b is not None else Buf(name)


def _bufs(ts):
    return [t.b if isinstance(t, T) else t for t in ts]


def build_nc(debug=False):
    nc = bass.Bass("TRN2", target_bir_lowering=False)

    def din(name, shape, dt=F32):
        return nc.dram_tensor(name, list(shape), dt, kind="ExternalInput").ap()

    xall = din("xall", [8192, 1024])
    mem = din("mem", [256, 1024])
    w_in = din("w_in", [1024, 3584])
    w_out = din("w_out", [1024, 1024])
    w_xq = din("w_xq", [1024, 1024])
    w_xk = din("w_xk", [1024, 1024])
    w_xv = din("w_xv", [1024, 1024])
    w_xo = din("w_xo", [1024, 1024])
    w_ff1 = din("w_ff1", [1024, 4096])
    w_ff2 = din("w_ff2", [4096, 1024])
    norm_mix = din("norm_mix", [1, 1024])
    norm_xattn = din("norm_xattn", [1, 1024])
    norm_mem = din("norm_mem", [1, 1024])
    norm_mlp = din("norm_mlp", [1, 1024])
    norm_final = din("norm_final", [1, 1024])
    hgrn_norm = din("hgrn_norm", [1, 512])
    lb_logits = din("lb_logits", [2, 512])
    rot = din("rot", [8192, 32])
    vbias = din("vbias", [1, 512])
    cU = din("cU", [128, 128])
    cL = din("cL", [128, 128])
    cMask4 = din("cMask4", [128, 512])
    cTriB = din("cTriB", [128, 128], BF16)
    cIdent = din("cIdent", [128, 128], BF16)
    out = nc.dram_tensor("out", [4096, 1024], F32, kind="ExternalOutput").ap()
    mixT = nc.dram_tensor("mixT", [1024, 4096], BF16, kind="ExternalOutput" if debug else "Internal").ap()
    ff1b = nc.dram_tensor("ff1b", [8, 128, 4096], BF16).ap()
    ff2b = nc.dram_tensor("ff2b", [8, 128, 4096], BF16).ap()

    S = Sched(nc)
    st = ExitStack()
    AR_ELEMS = 104000
    arena = st.enter_context(nc.sbuf_tensor("arena", [128, AR_ELEMS], BF16))
    banks = [st.enter_context(nc.psum_tensor("pb%d" % i, [128, 1024], BF16)) for i in range(8)]

    class PB:
        def __init__(self, t, i):
            self.bf = t[:, :]
            self.f = t[:, :].bitcast(F32)
            self.b = Buf("pb%d" % i)
            self.b.excl = True

    P = [PB(banks[i], i) for i in range(8)]

    class Arena:
        def __init__(self):
            self.off = 0
            self.top = AR_ELEMS

        def alloc_top(self, shape, dt, name=""):
            n = 1
            for s_ in shape:
                n *= s_
            assert dt == BF16
            nel = (n + 31) // 32 * 32
            self.top -= nel
            assert self.off <= self.top, ("arena overflow(top)", name)
            v = arena[:, self.top:self.top + nel][:, 0:n]
            if len(shape) == 2:
                v = v.rearrange("p (a b) -> p a b", a=shape[0])
            return T(v, name=name)

        def alloc(self, shape, dt, name=""):
            n = 1
            for s_ in shape:
                n *= s_
            nel = n * (2 if dt == F32 else 1)
            nel = (nel + 31) // 32 * 32
            assert self.off + nel <= self.top, ("arena overflow", name, self.off, nel, self.top)
            v = arena[:, self.off:self.off + nel]
            self.off += nel
            if dt == F32:
                v = v.bitcast(F32)
                v = v[:, 0:n]
            else:
                v = v[:, 0:n]
            flat = v
            if len(shape) == 2:
                v = v.rearrange("p (a b) -> p a b", a=shape[0])
            elif len(shape) == 3:
                v = v.rearrange("p (a b c) -> p a b c", a=shape[0], b=shape[1])
            t_ = T(v, name=name)
            t_.flat = flat
            return t_

    A = Arena()

    def mm(out_ap, lhsT, rhs, start, stop, r, w):
        S.op("pe", lambda e: e.matmul(out_ap, lhsT=lhsT, rhs=rhs, start=start, stop=stop), _bufs(r), _bufs(w))

    def tr(out_ap, in_ap, r, w):
        S.op("pe", lambda e: e.transpose(out=out_ap, in_=in_ap, identity=ident.ap), _bufs(r) + [ident.b], _bufs(w))

    def act(out_ap, in_ap, func, r, w, bias=None, scale=None, accum=None):
        kw = {}
        if bias is not None:
            kw["bias"] = bias
        if scale is not None:
            kw["scale"] = scale
        if accum is not None:
            kw["accum_out"] = accum
        S.op("act", lambda e: e.activation(out=out_ap, in_=in_ap, func=func, **kw), _bufs(r), _bufs(w))

    def tt(eng, out_ap, in0, in1, op, r, w):
        S.op(eng, lambda e: e.tensor_tensor(out=out_ap, in0=in0, in1=in1, op=op), _bufs(r), _bufs(w))

    def ts(eng, out_ap, in0, s1, s2, op0, op1, r, w):
        if op1 is None:
            S.op(eng, lambda e: e.tensor_scalar(out=out_ap, in0=in0, scalar1=s1, scalar2=None, op0=op0), _bufs(r), _bufs(w))
        else:
            S.op(eng, lambda e: e.tensor_scalar(out=out_ap, in0=in0, scalar1=s1, scalar2=s2, op0=op0, op1=op1), _bufs(r), _bufs(w))

    def stt(eng, out_ap, in0, scalar, in1, op0, op1, r, w):
        S.op(eng, lambda e: e.scalar_tensor_tensor(out=out_ap, in0=in0, scalar=scalar, in1=in1, op0=op0, op1=op1), _bufs(r), _bufs(w))

    def cp(eng, out_ap, in_ap, r, w):
        if eng == "act":
            act(out_ap, in_ap, AF.Copy, r, w)
        else:
            S.op(eng, lambda e: e.tensor_copy(out=out_ap, in_=in_ap), _bufs(r), _bufs(w))

    def recip(out_ap, in_ap, r, w):
        S.op("dve", lambda e: e.reciprocal(out=out_ap, in_=in_ap), _bufs(r), _bufs(w))

    def vmax(out_ap, in_ap, r, w):
        S.op("dve", lambda e: e.max(out=out_ap, in_=in_ap), _bufs(r), _bufs(w))

    def memset(eng, ap, val, w):
        S.op(eng, lambda e: e.memset(ap, val), [], _bufs(w))

    def dma(eng, out_ap, in_ap, r, w, lane, is_out=False):
        S.dma(eng, lambda e: e.dma_start(out=out_ap, in_=in_ap), _bufs(r), _bufs(w), lane=lane, is_out=is_out)

    ident = A.alloc([128], BF16, "ident")
    triB = A.alloc([128], BF16, "triB")
    Uc = A.alloc([128], F32, "Uc")
    Lc = A.alloc([128], F32, "Lc")
    mask4 = A.alloc([512], F32, "mask4")
    ones_f = A.alloc([8], F32, "ones_f")
    ones_b = A.alloc([128], BF16, "ones_b")
    epsb = A.alloc([2], F32, "epsb")
    dma("sp", ident.ap, cIdent[:, :], [], [ident], "c_ident")
    dma("sp", triB.ap, cTriB[:, :], [], [triB], "c_trib")
    dma("sp", Uc.ap, cU[:, :], [], [Uc], "c_u")
    dma("sp", Lc.ap, cL[:, :], [], [Lc], "c_l")
    dma("sp", mask4.ap, cMask4[:, :], [], [mask4], "c_m4")
    memset("dve", ones_f.ap, 1.0, [ones_f])
    memset("dve", ones_b.ap, 1.0, [ones_b])
    memset("dve", epsb.ap, EPS, [epsb])
    A_persist = A.off

    def norm_tile(xt, g_bc, junk, ssb, xs, pb, dst_ap, dst_bufs, evac_eng, d_model=1024):
        act(junk.ap, xt.ap, AF.Square, [xt], [junk, ssb], accum=ssb.ap[:, 0:1])
        act(ssb.ap[:, 1:2], ssb.ap[:, 0:1], AF.Ln, [ssb, epsb], [ssb], bias=epsb.ap[:, 0:1], scale=1.0 / d_model)
        act(ssb.ap[:, 2:3], ssb.ap[:, 1:2], AF.Exp, [ssb], [ssb], scale=-0.5)
        stt("dve", xs.ap, xt.ap, ssb.ap[:, 2:3], g_bc.ap, ALU.mult, ALU.mult, [xt, ssb, g_bc], [xs])
        for c in range(8):
            tr(pb.bf[:, c * 128:(c + 1) * 128], xs.ap[:, c * 128:(c + 1) * 128], [xs], [pb.b])
        cp(evac_eng, dst_ap, pb.bf[:, 0:1024].rearrange("p (c t) -> p c t", c=8), [pb.b], dst_bufs)

    stg = [A.alloc([4096], BF16, "stg%d" % i) for i in range(2)]
    ff1b_b = [Buf("ff1b%d" % i) for i in range(8)]
    ff2b_b = [Buf("ff2b%d" % i) for i in range(8)]
    w_ff1_v = w_ff1.rearrange("(c p) n -> p c n", p=128)

    def stage_cast(k):
        s_ = stg[k % 2]
        if k < 8:
            dma("pool", s_.ap.rearrange("p (c n) -> p c n", c=8), w_ff1_v[:, :, k * 512:(k + 1) * 512], [], [s_], "stg%d" % (k % 2))
        else:
            j2 = k - 8
            dma("pool", s_.ap.rearrange("p (a n) -> p a n", a=4),
                w_ff2[j2 * 512:(j2 + 1) * 512, :].rearrange("(a p) n -> p a n", p=128), [], [s_], "stg%d" % (k % 2))

    def stage_store(k):
        s_ = stg[k % 2]
        if k < 8:
            dma("sp", ff1b[k], s_.ap, [s_], [ff1b_b[k]], "stgo%d" % (k % 2))
        else:
            dma("sp", ff2b[k - 8], s_.ap, [s_], [ff2b_b[k - 8]], "stgo%d" % (k % 2))

    mix_b = [Buf("mix%d" % i) for i in range(32)]
    g_mix = A.alloc([1024], F32, "g_mix")
    dma("sp", g_mix.ap, norm_mix.partition_broadcast(128), [], [g_mix], "c_gmix")
    A_hm = A.off

    Wh = A.alloc([8, 2048], BF16, "Wh")
    w_in_v = w_in.rearrange("(c p) n -> p c n", p=128)
    for c in range(8):
        dma("pool", Wh.ap[:, c, :], w_in_v[:, c, 0:2048], [], [Wh], "w_h")
    lb_bc = A.alloc([512], F32, "lb")
    oml_bc = A.alloc([512], F32, "oml")
    hgn_bc = A.alloc([512], F32, "hgn")
    l1_bc = A.alloc([512], F32, "l1")
    dma("sp", lb_bc.ap, lb_logits[0:1, :].partition_broadcast(128), [], [lb_bc], "c_lb0")
    dma("sp", l1_bc.ap, lb_logits[1:2, :].partition_broadcast(128), [], [l1_bc], "c_lb1")
    dma("sp", hgn_bc.ap, hgrn_norm.partition_broadcast(128), [], [hgn_bc], "c_hgn")
    tt("dve", lb_bc.ap, l1_bc.ap, lb_bc.ap, ALU.subtract, [l1_bc, lb_bc], [lb_bc])
    act(lb_bc.ap, lb_bc.ap, AF.Exp, [lb_bc], [lb_bc])
    ts("dve", lb_bc.ap, lb_bc.ap, 1.0, None, ALU.add, None, [lb_bc], [lb_bc])
    recip(lb_bc.ap, lb_bc.ap, [lb_bc], [lb_bc])
    ts("dve", oml_bc.ap, lb_bc.ap, -1.0, 1.0, ALU.mult, ALU.add, [lb_bc], [oml_bc])

    xt = [A.alloc([1024], F32, "xt%d" % i) for i in range(2)]
    junk = A.alloc([1024], BF16, "junk")
    ssb = [A.alloc([4], F32, "ssb%d" % i) for i in range(2)]
    xs = [A.alloc([1024], BF16, "xs%d" % i) for i in range(2)]
    hT = [A.alloc([8, 128], BF16, "hT%d" % i) for i in range(2)]
    ez = [A.alloc([512], F32, "ez%d" % i) for i in range(2)]
    vv = [A.alloc([512], BF16, "v%d" % i) for i in range(4)]
    qs = [A.alloc([512], F32, "qs%d" % i) for i in range(2)]
    eg = [A.alloc([512], F32, "eg%d" % i) for i in range(2)]
    gr = [A.alloc([512], F32, "gr%d" % i) for i in range(2)]
    ff = A.alloc([512], F32, "f")
    logf = A.alloc([512], F32, "logf")
    kk = A.alloc([512], F32, "k")
    eb = A.alloc([512], F32, "eb")
    enb = A.alloc([512], F32, "enb")
    eend = A.alloc([512], F32, "eend")
    qts = [A.alloc([512], BF16, "qt%d" % i) for i in range(2)]
    kts = [A.alloc([512], BF16, "kt%d" % i) for i in range(2)]
    kend = [A.alloc([512], BF16, "kend%d" % i) for i in range(3)]
    qkT = [A.alloc([8, 128], BF16, "qkT%d" % i) for i in range(2)]
    ATm = [A.alloc([512], BF16, "ATm%d" % i) for i in range(2)]
    Sst = A.alloc([512], F32, "Sst")
    Sb = A.alloc([512], BF16, "Sb")
    dec = [A.alloc([4], F32, "dec%d" % i) for i in range(3)]
    gs = [A.alloc([512], F32, "gs%d" % i) for i in range(3)]
    ssq = A.alloc([12], F32, "ssq")
    junkC = A.alloc([128], BF16, "junkC")
    on = A.alloc([512], BF16, "on")
    oT = [A.alloc([4, 128], BF16, "oT%d" % i) for i in range(2)]
    memset("dve", Sst.ap, 0.0, [Sst])
    memset("dve", Sb.ap, 0.0, [Sb])

    def hs(h):
        return slice(h * 128, (h + 1) * 128)

    def stageA(t):
        local = t >= 32
        sl = t % 2
        dma("sp", xt[sl].ap, xall[t * 128:(t + 1) * 128, :], [], [xt[sl]], "xt%d" % sl)
        if t % 4 == 0:
            stage_cast(t // 4)
        if t % 4 == 3:
            stage_store(t // 4)
        norm_tile(xt[sl], g_mix, junk, ssb[sl], xs[sl], P[0], hT[sl].ap, [hT[sl]], "act")
        for c in range(8):
            mm(P[1].f[:, 0:512], hT[sl].ap[:, c, :], Wh.ap[:, c, 512:1024], c == 0, c == 7, [hT[sl], Wh], [P[1].b])
        for c in range(8):
            mm(P[2].f[:, 0:512], hT[sl].ap[:, c, :], Wh.ap[:, c, 1024:1536], c == 0, c == 7, [hT[sl], Wh], [P[2].b])
        act(ez[sl].ap, P[1].f[:, 0:512], AF.Exp, [P[1].b], [ez[sl]], scale=-1.0)
        act(vv[t % 4].ap, P[2].f[:, 0:512], AF.Copy, [P[2].b], [vv[t % 4]])
        if local:
            for c in range(8):
                mm(P[1].f[:, 0:512], hT[sl].ap[:, c, :], Wh.ap[:, c, 0:512], c == 0, c == 7, [hT[sl], Wh], [P[1].b])
            for c in range(8):
                mm(P[2].f[:, 0:512], hT[sl].ap[:, c, :], Wh.ap[:, c, 1536:2048], c == 0, c == 7, [hT[sl], Wh], [P[2].b])
            cp("dve", qs[sl].ap, P[1].f[:, 0:512], [P[1].b], [qs[sl]])
            act(eg[sl].ap, P[2].f[:, 0:512], AF.Exp, [P[2].b], [eg[sl]], scale=-1.0)
            cp("dve", gr[sl].ap, P[2].f[:, 0:512], [P[2].b], [gr[sl]])

    def stageB(t):
        local = t >= 32
        sl = t % 2
        s3 = t % 3
        e_ = ez[sl]
        act(e_.ap, e_.ap, AF.Ln, [e_, ones_f], [e_], bias=ones_f.ap[:, 0:1])
        act(e_.ap, e_.ap, AF.Exp, [e_], [e_], scale=-1.0)
        tt("dve", ff.ap, e_.ap, oml_bc.ap, ALU.mult, [e_, oml_bc], [ff])
        tt("pool", ff.ap, ff.ap, lb_bc.ap, ALU.add, [ff, lb_bc], [ff])
        act(logf.ap, ff.ap, AF.Ln, [ff], [logf])
        ts("pool", kk.ap, ff.ap, -1.0, 1.0, ALU.mult, ALU.add, [ff], [kk])
        if local:
            mm(P[3].f[:, 0:512], Uc.ap, logf.ap, True, True, [Uc, logf], [P[3].b])
        mm(P[4].f[:, 0:512], Lc.ap, logf.ap, True, True, [Lc, logf], [P[4].b])
        act(eend.ap, P[4].f[:, 0:512], AF.Exp, [P[4].b], [eend])
        tt("pool", kend[s3].ap, kk.ap, eend.ap, ALU.mult, [kk, eend], [kend[s3]])
        if local:
            act(eb.ap, P[3].f[:, 0:512], AF.Exp, [P[3].b], [eb])
            act(enb.ap, P[3].f[:, 0:512], AF.Exp, [P[3].b], [enb], scale=-1.0)
            tt("dve", qts[sl].ap, qs[sl].ap, eb.ap, ALU.mult, [qs[sl], eb], [qts[sl]])
            tt("pool", kts[sl].ap, kk.ap, enb.ap, ALU.mult, [kk, enb], [kts[sl]])
            g_ = eg[sl]
            act(g_.ap, g_.ap, AF.Ln, [g_, ones_f], [g_], bias=ones_f.ap[:, 0:1])
            act(g_.ap, g_.ap, AF.Exp, [g_], [g_], scale=-1.0)
            tt("dve", gs[s3].ap, gr[sl].ap, g_.ap, ALU.mult, [gr[sl], g_], [gs[s3]])
            tt("pool", gs[s3].ap, gs[s3].ap, hgn_bc.ap, ALU.mult, [gs[s3], hgn_bc], [gs[s3]])
        for h in range(4):
            mm(P[4].f[:, h:h + 1], logf.ap[:, hs(h)], ones_f.ap[:, 0:1], True, True, [logf, ones_f], [P[4].b])
        act(dec[s3].ap, P[4].f[:, 0:4], AF.Exp, [P[4].b], [dec[s3]])

    def stageB2(t):
        sl = t % 2
        qt, kt = qts[sl], kts[sl]
        for h in range(4):
            tr(P[5].bf[:, h * 128:(h + 1) * 128], qt.ap[:, hs(h)], [qt], [P[5].b])
        for h in range(4):
            tr(P[5].bf[:, 512 + h * 128:512 + (h + 1) * 128], kt.ap[:, hs(h)], [kt], [P[5].b])
        cp("dve", qkT[sl].ap, P[5].bf[:, 0:1024].rearrange("p (c t) -> p c t", c=8), [P[5].b], [qkT[sl]])
        for h in range(4):
            mm(P[5].f[:, hs(h)], qkT[sl].ap[:, 4 + h, :], qkT[sl].ap[:, h, :], True, True, [qkT[sl]], [P[5].b])
        tt("dve", ATm[sl].ap, P[5].f[:, 0:512], mask4.ap, ALU.mult, [P[5].b, mask4], [ATm[sl]])

    def stageC(t):
        local = t >= 32
        sl = t % 2
        v_ = vv[t % 4]
        s3 = t % 3
        if local:
            for h in range(4):
                mm(P[7].f[:, hs(h)], ATm[sl].ap[:, hs(h)], v_.ap[:, hs(h)], True, False, [ATm[sl], v_], [P[7].b])
                mm(P[7].f[:, hs(h)], qkT[sl].ap[:, h, :], Sb.ap[:, hs(h)], False, True, [qkT[sl], Sb], [P[7].b])
        for h in range(4):
            mm(P[6].f[:, hs(h)], kend[s3].ap[:, hs(h)], v_.ap[:, hs(h)], True, True, [kend[s3], v_], [P[6].b])
        for h in range(4):
            stt("dve", Sst.ap[:, hs(h)], Sst.ap[:, hs(h)], dec[s3].ap[:, h:h + 1], P[6].f[:, hs(h)], ALU.mult, ALU.add,
                [Sst, dec[s3], P[6].b], [Sst])
        cp("pool", Sb.ap, Sst.ap, [Sst], [Sb])
        if local:
            i = t - 32
            for h in range(4):
                act(junkC.ap[:, 0:128], P[7].f[:, hs(h)], AF.Square, [P[7].b], [junkC, ssq], accum=ssq.ap[:, h:h + 1])
            act(ssq.ap[:, 4:8], ssq.ap[:, 0:4], AF.Ln, [ssq, epsb], [ssq], bias=epsb.ap[:, 0:1], scale=1.0 / 128)
            act(ssq.ap[:, 8:12], ssq.ap[:, 4:8], AF.Exp, [ssq], [ssq], scale=-0.5)
            for h in range(4):
                stt("dve", on.ap[:, hs(h)], P[7].f[:, hs(h)], ssq.ap[:, 8 + h:9 + h], gs[s3].ap[:, hs(h)], ALU.mult, ALU.mult,
                    [P[7].b, ssq, gs[s3]], [on])
            for h in range(4):
                tr(P[7].bf[:, h * 128:(h + 1) * 128], on.ap[:, hs(h)], [on], [P[7].b])
            cp("act", oT[sl].ap, P[7].bf[:, 0:512].rearrange("p (c t) -> p c t", c=4), [P[7].b], [oT[sl]])
            dma("sp", mixT[0:512, i * 128:(i + 1) * 128].rearrange("(h p) t -> p h t", p=128), oT[sl].ap,
                [oT[sl]], [mix_b[i]], "mixo%d" % sl)

    import os as _os
    for it in range(0 if _os.environ.get('SKIP_H') else 64 + 3):
        lists = []
        if it < 64:
            S.begin(); stageA(it); lists.append(S.end())
        if 0 <= it - 1 < 64:
            S.begin(); stageB(it - 1); lists.append(S.end())
        if 32 <= it - 2 < 64:
            S.begin(); stageB2(it - 2); lists.append(S.end())
        if 0 <= it - 3 < 64:
            S.begin(); stageC(it - 3); lists.append(S.end())
        S.merged(lists)

    SCALE = 128 ** -0.5
    Wo = A.alloc_top([8, 1024], BF16, "Wo")
    Wxq = A.alloc_top([8, 1024], BF16, "Wxq")
    Wxo = A.alloc_top([8, 1024], BF16, "Wxo")

    def load_Wm(hp_, Wm_):
        for c in range(8):
            dma("pool", Wm_.ap[:, c, 0:256], w_in_v[:, c, 2048 + hp_ * 256:2048 + (hp_ + 1) * 256], [], [Wm_], "w_m%d" % hp_)
            dma("pool", Wm_.ap[:, c, 256:512], w_in_v[:, c, 2560 + hp_ * 256:2560 + (hp_ + 1) * 256], [], [Wm_], "w_m%d" % hp_)
            dma("pool", Wm_.ap[:, c, 512:768], w_in_v[:, c, 3072 + hp_ * 256:3072 + (hp_ + 1) * 256], [], [Wm_], "w_m%d" % hp_)

    S.barrier()
    A.off = A_hm
    Wms = [A.alloc([8, 768], BF16, "Wm%d" % i) for i in range(2)]
    KT = A.alloc([2, 8192], BF16, "KT")
    KT_b = [Buf("KT%d" % i) for i in range(64)]
    Vaug = A.alloc([64, 2, 130], BF16, "Vaug")
    V_b = [Buf("V%d" % i) for i in range(64)]
    KmF = A.alloc([2, 32], F32, "KmF")
    KmT = A.alloc([2, 32], BF16, "KmT")
    vb_bc = A.alloc([512], F32, "vb")
    dma("sp", vb_bc.ap, vbias.partition_broadcast(128), [], [vb_bc], "c_vb")
    xt = [A.alloc([1024], F32, "xt%d" % i) for i in range(2)]
    junks = [A.alloc([1024], BF16, "junk%d" % i) for i in range(2)]
    ssb = [A.alloc([4], F32, "ssb%d" % i) for i in range(2)]
    xs = [A.alloc([1024], BF16, "xs%d" % i) for i in range(2)]
    hT = [A.alloc([8, 128], BF16, "hT%d" % i) for i in range(2)]
    rt = [A.alloc([32], F32, "rt%d" % i) for i in range(2)]
    ras = [A.alloc([4, 16], F32, "ra%d" % i) for i in range(2)]
    rbs = [A.alloc([4, 16], F32, "rb%d" % i) for i in range(2)]
    qkrs = [A.alloc([4, 128], BF16, "qkr%d" % i) for i in range(2)]
    QTs = [A.alloc([2, 128], BF16, "QT%d" % i) for i in range(2)]
    Gp = A.alloc([32], F32, "Gp")
    mx8 = A.alloc([8], F32, "mx8")
    bqs = [[A.alloc([32], F32, "bq%d_%d" % (i, k)) for k in range(2)] for i in range(2)]
    Pm = [A.alloc([512], BF16, "Pm%d" % i) for i in range(4)]
    PT = [A.alloc([512], BF16, "PT%d" % i) for i in range(3)]
    rden = A.alloc([2], F32, "rden")
    om = A.alloc([2, 128], BF16, "om")
    omT = [A.alloc([2, 128], BF16, "omT%d" % i) for i in range(2)]
    M_ALLOC_MARK = True
    for hp in range(0 if _os.environ.get('SKIP_M') else 2):
        Wm = Wms[hp]
        if hp == 0:
            load_Wm(0, Wms[0])
            load_Wm(1, Wms[1])
        else:
            for c in range(8):
                dma("pool", Wo.ap[:, c, :], w_out.rearrange("(c p) n -> p c n", p=128)[:, c, :], [], [Wo], "w_o")
                dma("pool", Wxq.ap[:, c, :], w_xq.rearrange("(c p) n -> p c n", p=128)[:, c, :], [], [Wxq], "w_xq")
                dma("pool", Wxo.ap[:, c, :], w_xo.rearrange("(c p) n -> p c n", p=128)[:, c, :], [], [Wxo], "w_xo")
        if hp == 0:
            memset("pool", Vaug.flat, 1.0, V_b)
        memset("dve", KmF.flat, 0.0, [KmF])
        memset("dve", KmT.flat, 0.0, [KmT])

        def front(t, pa, pq):
            local = t >= 32
            sl = t % 2
            nblk = t // 2
            ra, rb, qkr, QT = ras[sl], rbs[sl], qkrs[sl], QTs[sl]
            dma("sp", xt[sl].ap, xall[t * 128:(t + 1) * 128, :], [], [xt[sl]], "xt%d" % sl)
            dma("sp", rt[sl].ap, rot[t * 128:(t + 1) * 128, :], [], [rt[sl]], "rt%d" % sl)
            norm_tile(xt[sl], g_mix, junks[sl], ssb[sl], xs[sl], pa, hT[sl].ap, [hT[sl]], "dve")
            if local:
                for c in range(8):
                    mm(pq.f[:, 0:256], hT[sl].ap[:, c, :], Wm.ap[:, c, 0:256], c == 0, False, [hT[sl], Wm], [pq.b])
            for c in range(8):
                mm(pq.f[:, 256:512], hT[sl].ap[:, c, :], Wm.ap[:, c, 256:512], (c == 0 and not local), c == 7,
                   [hT[sl], Wm], [pq.b])
            for c in range(8):
                mm(pa.f[:, 0:256], hT[sl].ap[:, c, :], Wm.ap[:, c, 512:768], c == 0, c == 7, [hT[sl], Wm], [pa.b])
            cp("dve", Vaug.ap[:, t, :, 0:128], pa.f[:, 0:256].rearrange("p (h d) -> p h d", h=2), [pa.b], [V_b[t]])
            h0 = 0 if local else 2
            nh = 4 - h0
            X = pq.f[:, 0:512].rearrange("p (h d) -> p h d", h=4)
            cosb = rt[sl].ap[:, 0:16].unsqueeze(1).to_broadcast([128, nh, 16])
            sinb = rt[sl].ap[:, 16:32].unsqueeze(1).to_broadcast([128, nh, 16])
            x1 = X[:, h0:4, 0:16]
            x2 = X[:, h0:4, 16:32]
            tt("dve", ra.ap[:, h0:4, :], x1, cosb, ALU.mult, [pq.b, rt[sl]], [ra])
            tt("dve", rb.ap[:, h0:4, :], x2, sinb, ALU.mult, [pq.b, rt[sl]], [rb])
            tt("dve", qkr.ap[:, h0:4, 0:16], ra.ap[:, h0:4, :], rb.ap[:, h0:4, :], ALU.subtract, [ra, rb], [qkr])
            tt("dve", ra.ap[:, h0:4, :], x2, cosb, ALU.mult, [pq.b, rt[sl]], [ra])
            tt("dve", rb.ap[:, h0:4, :], x1, sinb, ALU.mult, [pq.b, rt[sl]], [rb])
            tt("dve", qkr.ap[:, h0:4, 16:32], ra.ap[:, h0:4, :], rb.ap[:, h0:4, :], ALU.add, [ra, rb], [qkr])
            cp("dve", qkr.ap[:, h0:4, 32:128], X[:, h0:4, 32:128], [pq.b], [qkr])
            for j in range(h0, 4):
                tr(pa.bf[:, j * 128:(j + 1) * 128], qkr.ap[:, j, :], [qkr], [pa.b])
            cp("dve", KT.ap[:, :, t * 128:(t + 1) * 128], pa.bf[:, 256:512].rearrange("p (h t) -> p h t", h=2),
               [pa.b], [KT_b[t]])
            if local:
                cp("dve", QT.ap, pa.bf[:, 0:256].rearrange("p (h t) -> p h t", h=2), [pa.b], [QT])
            for h in range(2):
                mm(pq.f[:, h:h + 1], qkr.ap[:, 2 + h, :], ones_b.ap[:, 0:1], True, True, [qkr, ones_b], [pq.b])
            stt("dve", KmF.ap[:, :, nblk], pq.f[:, 0:2], 1.0 / 256, KmF.ap[:, :, nblk], ALU.mult, ALU.add,
                [pq.b, KmF], [KmF])
            cp("dve", KmT.ap[:, :, nblk], KmF.ap[:, :, nblk], [KmF], [KmT])
            if local:
                j_loc = (t - 32) // 2
                for h in range(2):
                    bq_ = bqs[sl][h]
                    mm(pq.f[:, 0:32], QT.ap[:, h, :], KmT.ap[:, h, :], True, True, [QT, KmT], [pq.b])
                    tt("dve", Gp.ap, pq.f[:, 0:32], vb_bc.ap[:, j_loc * 32:(j_loc + 1) * 32], ALU.add, [pq.b, vb_bc], [Gp])
                    vmax(mx8.ap, Gp.ap, [Gp], [mx8])
                    ts("dve", bq_.ap, Gp.ap, mx8.ap[:, 2:3], NEG, ALU.is_lt, ALU.mult, [Gp, mx8], [bq_])
                    tt("dve", bq_.ap, bq_.ap, vb_bc.ap[:, j_loc * 32:(j_loc + 1) * 32], ALU.add, [bq_, vb_bc], [bq_])

        def attention(t):
            sl = t % 2
            QT = QTs[sl]
            J = t // 2
            i = t - 32
            j_loc = i // 2
            po = P[7]
            units = []
            totals = []
            for h in range(2):
                hu = []
                n0 = 0
                while n0 < J:
                    nb_ = 2 if n0 + 1 < J else 1
                    hu.append(("past", n0, nb_, h))
                    n0 += nb_
                nk_own = 1 if (t % 2 == 0) else 2
                hu.append(("own", J, nk_own, h))
                totals.append(sum((u[2] * 2 if u[0] == "past" else u[2]) for u in hu))
                units += hu
            kt_done = [0, 0]

            def stage_S(ui, u):
                h = u[3]
                pb = P[2 + ui % 3]
                pm = Pm[ui % 4]
                if u[0] == "past":
                    n0_, nb2 = u[1], u[2]
                    w = nb2 * 256
                    kbufs = [KT_b[2 * n0_ + x] for x in range(2 * nb2)]
                    mm(pb.f[:, 0:w], QT.ap[:, h, :], KT.ap[:, h, n0_ * 256:n0_ * 256 + w], True, True, [QT] + kbufs, [pb.b])
                    for x in range(nb2):
                        act(pm.ap[:, x * 256:(x + 1) * 256], pb.f[:, x * 256:(x + 1) * 256], AF.Exp, [pb.b, bqs[sl][h]], [pm],
                            bias=bqs[sl][h].ap[:, n0_ + x:n0_ + x + 1], scale=SCALE)
                else:
                    nk = u[2]
                    w = nk * 128
                    kbufs = [KT_b[2 * J + x] for x in range(nk)]
                    mm(pb.f[:, 0:w], QT.ap[:, h, :], KT.ap[:, h, J * 256:J * 256 + w], True, False, [QT] + kbufs, [pb.b])
                    mm(pb.f[:, w - 128:w], ident.ap, triB.ap, False, True, [ident, triB], [pb.b])
                    act(pm.ap[:, 0:w], pb.f[:, 0:w], AF.Exp, [pb.b], [pm], scale=SCALE)

            def stage_T(ui, u):
                pb = P[5 + ui % 2]
                pm = Pm[ui % 4]
                pt = PT[ui % 3]
                nkt = u[2] * 2 if u[0] == "past" else u[2]
                for x in range(nkt):
                    tr(pb.bf[:, x * 128:(x + 1) * 128], pm.ap[:, x * 128:(x + 1) * 128], [pm], [pb.b])
                cp("dve", pt.ap[:, 0:nkt * 128], pb.bf[:, 0:nkt * 128], [pb.b], [pt])

            def stage_PV(ui, u):
                h = u[3]
                pt = PT[ui % 3]
                nkt = u[2] * 2 if u[0] == "past" else u[2]
                kt0 = u[1] * 2
                for x in range(nkt):
                    first = kt_done[h] == 0
                    kt_done[h] += 1
                    last = kt_done[h] == totals[h]
                    mm(po.f[:, 128:258], pt.ap[:, x * 128:(x + 1) * 128], Vaug.ap[:, kt0 + x, h, :], first, last,
                       [pt, V_b[kt0 + x]], [po.b])
                if kt_done[h] == totals[h]:
                    recip(rden.ap[:, h:h + 1], po.f[:, 256:257], [po.b], [rden])
                    act(om.ap[:, h, :], po.f[:, 128:256], AF.Identity, [po.b, rden], [om], scale=rden.ap[:, h:h + 1])

            nu = len(units)
            D1, D2 = 2, 4
            for step in range(nu + D2):
                if step < nu:
                    stage_S(step, units[step])
                if 0 <= step - D1 < nu:
                    stage_T(step - D1, units[step - D1])
                if 0 <= step - D2 < nu:
                    stage_PV(step - D2, units[step - D2])
            for h in range(2):
                tr(po.bf[:, h * 128:(h + 1) * 128], om.ap[:, h, :], [om], [po.b])
            cp("dve", omT[sl].ap, po.bf[:, 0:256].rearrange("p (h t) -> p h t", h=2), [po.b], [omT[sl]])
            r0 = 512 + hp * 256
            dma("sp", mixT[r0:r0 + 256, i * 128:(i + 1) * 128].rearrange("(h p) t -> p h t", p=128), omT[sl].ap,
                [omT[sl]], [mix_b[i]], "mixo%d" % sl)

        for k in range(16):
            S.begin(); front(2 * k, P[0], P[1]); la = S.end()
            S.begin(); front(2 * k + 1, P[2], P[3]); lb_ = S.end()
            S.merged([la, lb_])
        S.begin(); front(32, P[0], P[1]); la = S.end()
        S.merged([la])
        for t in range(32, 64):
            lists = []
            S.begin(); attention(t); lists.append(S.end())
            if t + 1 < 64:
                S.begin(); front(t + 1, P[0], P[1]); lists.append(S.end())
            S.merged(lists)

    S.barrier()
    A.off = A_persist
    memKT = A.alloc([8, 256], BF16, "memKT")
    memV = A.alloc([2, 1024], BF16, "memV")
    g_xa = A.alloc([1024], F32, "g_xa")
    g_mlp = A.alloc([1024], F32, "g_mlp")
    g_fin = A.alloc([1024], F32, "g_fin")
    dma("sp", g_xa.ap, norm_xattn.partition_broadcast(128), [], [g_xa], "c_gxa")
    dma("sp", g_mlp.ap, norm_mlp.partition_broadcast(128), [], [g_mlp], "c_gmlp")
    dma("sp", g_fin.ap, norm_final.partition_broadcast(128), [], [g_fin], "c_gfin")
    W1 = [A.alloc([8, 512], BF16, "W1_%d" % i) for i in range(2)]
    W2 = [A.alloc([4, 1024], BF16, "W2_%d" % i) for i in range(2)]
    X = [A.alloc([1024], F32, "X%d" % i) for i in range(4)]
    xs = [A.alloc([1024], BF16, "xsF%d" % i) for i in range(4)]
    junk = A.alloc([1024], BF16, "junkF")
    ssb = [A.alloc([4], F32, "ssbF%d" % i) for i in range(4)]
    hTs = A.alloc([8, 512], BF16, "hTs")
    hTs_b = [Buf("hTs%d" % i) for i in range(4)]

    def norm4(g_bc):
        lists = []
        for j in range(4):
            S.begin()
            norm_tile(X[j], g_bc, junk, ssb[j], xs[j], P[j], hTs.ap[:, :, j * 128:(j + 1) * 128], [hTs_b[j]], "act")
            lists.append(S.end())
        S.merged(lists)
    qxT = A.alloc([8, 512], BF16, "qxT")
    oxT = A.alloc([8, 512], BF16, "oxT")
    PTxs = [A.alloc([2, 512], BF16, "PTx%d" % i) for i in range(2)]
    rds = [A.alloc([512], F32, "rd%d" % i) for i in range(2)]
    qxT_b = [Buf("qxT%d" % i) for i in range(8)]
    oxT_b = [Buf("oxT%d" % i) for i in range(8)]
    rl = [A.alloc([512], F32, "rl%d" % i) for i in range(2)]
    hid = A.alloc([32, 512], BF16, "hid")
    hid_off = A.off - 32 * 512
    Wxk_ap = arena[:, hid_off:hid_off + 8192].rearrange("p (c n) -> p c n", c=8)
    Wxv_ap = arena[:, hid_off + 8192:hid_off + 16384].rearrange("p (c n) -> p c n", c=8)
    for c in range(8):
        dma("pool", Wxk_ap[:, c, :], w_xk.rearrange("(c p) n -> p c n", p=128)[:, c, :], [], [hid], "w_xk")
        dma("pool", Wxv_ap[:, c, :], w_xv.rearrange("(c p) n -> p c n", p=128)[:, c, :], [], [hid], "w_xk")
    g_mem = X[3]
    dma("sp", g_mem.ap, norm_mem.partition_broadcast(128), [], [g_mem], "c_gmem")
    for m in range(2):
        dma("sp", X[m].ap, mem[m * 128:(m + 1) * 128, :], [], [X[m]], "X%d" % m)
        norm_tile(X[m], g_mem, junk, ssb[m], xs[m], P[0], hTs.ap[:, :, m * 128:(m + 1) * 128], [hTs_b[m]], "act")
    for ch in range(8):
        pb = P[1 + ch % 2]
        for c in range(8):
            mm(pb.f[:, 0:256], Wxk_ap[:, c, ch * 128:(ch + 1) * 128], hTs.ap[:, c, 0:256], c == 0, c == 7, [hid] + hTs_b[0:2], [pb.b])
        cp("dve", memKT.ap[:, ch, :], pb.f[:, 0:256], [pb.b], [memKT])
    for m in range(2):
        for hh in range(2):
            pb = P[3 + hh]
            for c in range(8):
                mm(pb.f[:, 0:512], hTs.ap[:, c, m * 128:(m + 1) * 128], Wxv_ap[:, c, hh * 512:(hh + 1) * 512], c == 0, c == 7,
                   [hTs_b[m], hid], [pb.b])
            cp("act", memV.ap[:, m, hh * 512:(hh + 1) * 512], pb.f[:, 0:512], [pb.b], [memV])

    XSCALE = 256 ** -0.5
    mixT_v = mixT.rearrange("(c p) t -> p c t", p=128)
    for s in range(8):
        dma("sp", oxT.ap, mixT_v[:, :, s * 512:(s + 1) * 512], [mix_b[4 * s + j] for j in range(4)], oxT_b, "mixin")
        for j in range(4):
            tok0 = 4096 + (s * 4 + j) * 128
            dma("sp", X[j].ap, xall[tok0:tok0 + 128, :], [], [X[j]], "X%d" % j)
        for j in range(4):
            for hh in range(2):
                pb = P[1 + (2 * j + hh) % 2]
                for c in range(8):
                    mm(pb.f[:, 0:512], oxT.ap[:, c, j * 128:(j + 1) * 128], Wo.ap[:, c, hh * 512:(hh + 1) * 512], c == 0, c == 7,
                       [oxT_b[c], Wo], [pb.b])
                tt("dve", X[j].ap[:, hh * 512:(hh + 1) * 512], X[j].ap[:, hh * 512:(hh + 1) * 512], pb.f[:, 0:512], ALU.add,
                   [X[j], pb.b], [X[j]])
        norm4(g_xa)
        def xattn_lane(L):
            B = P[4 * L:4 * L + 4]
            ptx, rd_ = PTxs[L], rds[L]
            for h in (L, L + 2):
                for dd in range(2):
                    ch = 2 * h + dd
                    for c in range(8):
                        mm(B[0].f[:, 0:512], Wxq.ap[:, c, ch * 128:(ch + 1) * 128], hTs.ap[:, c, :], c == 0, c == 7, [Wxq] + hTs_b, [B[0].b])
                    cp("act", qxT.ap[:, ch, :], B[0].f[:, 0:512], [B[0].b], [qxT_b[ch]])
                for m in range(2):
                    pb = B[1 + m]
                    for dd in range(2):
                        mm(pb.f[:, 0:512], memKT.ap[:, 2 * h + dd, m * 128:(m + 1) * 128], qxT.ap[:, 2 * h + dd, :], dd == 0, dd == 1,
                           [memKT, qxT_b[2 * h + dd]], [pb.b])
                    act(ptx.ap[:, m, :], pb.f[:, 0:512], AF.Exp, [pb.b], [ptx], scale=XSCALE)
                for m in range(2):
                    mm(B[3].f[:, 0:512], ones_b.ap, ptx.ap[:, m, :], m == 0, m == 1, [ones_b, ptx], [B[3].b])
                act(rd_.ap, B[3].f[:, 0:512], AF.Ln, [B[3].b], [rd_])
                act(rd_.ap, rd_.ap, AF.Exp, [rd_], [rd_], scale=-1.0)
                for dd in range(2):
                    pb = B[1 + dd]
                    for m in range(2):
                        mm(pb.f[:, 0:512], memV.ap[:, m, (2 * h + dd) * 128:(2 * h + dd + 1) * 128], ptx.ap[:, m, :], m == 0, m == 1,
                           [memV, ptx], [pb.b])
                    tt("dve", oxT.ap[:, 2 * h + dd, :], pb.f[:, 0:512], rd_.ap, ALU.mult, [pb.b, rd_], [oxT_b[2 * h + dd]])

        lanes = []
        for L in range(2):
            S.begin(); xattn_lane(L); lanes.append(S.end())
        S.merged(lanes)
        for j in range(4):
            for hh in range(2):
                pb = P[1 + (2 * j + hh) % 2]
                for c in range(8):
                    mm(pb.f[:, 0:512], oxT.ap[:, c, j * 128:(j + 1) * 128], Wxo.ap[:, c, hh * 512:(hh + 1) * 512], c == 0, c == 7,
                       [oxT_b[c], Wxo], [pb.b])
                tt("dve", X[j].ap[:, hh * 512:(hh + 1) * 512], X[j].ap[:, hh * 512:(hh + 1) * 512], pb.f[:, 0:512], ALU.add,
                   [X[j], pb.b], [X[j]])
        norm4(g_mlp)
        for fc in range(8):
            w1 = W1[fc % 2]
            dma("sp", w1.ap.rearrange("p c n -> p (c n)"), ff1b[fc], [ff1b_b[fc]], [w1], "W1_%d" % (fc % 2))
            for sub in range(4):
                k_ = fc * 4 + sub
                pb = P[1 + k_ % 2]
                for c in range(8):
                    mm(pb.f[:, 0:512], w1.ap[:, c, sub * 128:(sub + 1) * 128], hTs.ap[:, c, :], c == 0, c == 7, [w1] + hTs_b, [pb.b])
                r_ = rl[k_ % 2]
                act(r_.ap, pb.f[:, 0:512], AF.Relu, [pb.b], [r_])
                tt("pool", hid.ap[:, k_, :], r_.ap, r_.ap, ALU.mult, [r_], [hid])
        for j2 in range(8):
            w2 = W2[j2 % 2]
            dma("sp", w2.ap.rearrange("p a n -> p (a n)"), ff2b[j2], [ff2b_b[j2]], [w2], "W2_%d" % (j2 % 2))
            for j in range(4):
                for hh in range(2):
                    pb = P[j * 2 + hh]
                    for sub in range(4):
                        mm(pb.f[:, 0:512], hid.ap[:, j2 * 4 + sub, j * 128:(j + 1) * 128], w2.ap[:, sub, hh * 512:(hh + 1) * 512],
                           (j2 == 0 and sub == 0), (j2 == 7 and sub == 3), [hid, w2], [pb.b])
        for j in range(4):
            for hh in range(2):
                pb = P[j * 2 + hh]
                tt("dve", X[j].ap[:, hh * 512:(hh + 1) * 512], X[j].ap[:, hh * 512:(hh + 1) * 512], pb.f[:, 0:512], ALU.add,
                   [X[j], pb.b], [X[j]])
        for j in range(4):
            sb_ = ssb[j]
            act(junk.ap, X[j].ap, AF.Square, [X[j]], [junk, sb_], accum=sb_.ap[:, 0:1])
            act(sb_.ap[:, 1:2], sb_.ap[:, 0:1], AF.Ln, [sb_, epsb], [sb_], bias=epsb.ap[:, 0:1], scale=1.0 / 1024)
            act(sb_.ap[:, 2:3], sb_.ap[:, 1:2], AF.Exp, [sb_], [sb_], scale=-0.5)
            stt("dve", X[j].ap, X[j].ap, sb_.ap[:, 2:3], g_fin.ap, ALU.mult, ALU.mult, [X[j], sb_, g_fin], [X[j]])
            r0 = (s * 4 + j) * 128
            dma("sp", out[r0:r0 + 128, :], X[j].ap, [X[j]], [], "X%d" % j, is_out=True)

    S.emit(st)
    st.close()
    return nc


_NC_CACHE = {}


def _host_consts():
    ar = np.arange(128)
    U = (ar[:, None] <= ar[None, :]).astype(np.float32)
    L = (ar[:, None] > ar[None, :]).astype(np.float32)
    mask4 = np.tile(U, (1, 4)).astype(np.float32)
    tri = np.where(ar[None, :] <= ar[:, None], 0.0, NEG).astype(np.float32)
    return {
        "cU": U, "cL": L, "cMask4": mask4,
        "cTriB": tri.astype(ml_dtypes.bfloat16),
        "cIdent": np.eye(128, dtype=np.float32).astype(ml_dtypes.bfloat16),
    }


def _rot_table(positions):
    half = 16
    inv_freq = (np.float32(500000.0) ** (-np.arange(half, dtype=np.float32) * np.float32(2.0) / np.float32(32))).astype(np.float32)
    ang = positions.astype(np.float32)[:, None] * inv_freq[None, :]
    return np.concatenate([np.cos(ang), np.sin(ang)], axis=1).astype(np.float32)


def make_in_maps(inputs):
    x = np.asarray(inputs["x"], dtype=np.float32)
    memv = np.asarray(inputs["mem"], dtype=np.float32)
    consts = _host_consts()
    shared = {
        "w_in": np.ascontiguousarray(inputs["w_in"][0]),
        "w_out": np.ascontiguousarray(inputs["w_out"][0]),
        "w_xq": np.ascontiguousarray(inputs["w_xq"][0]),
        "w_xk": np.ascontiguousarray(inputs["w_xk"][0]),
        "w_xv": np.ascontiguousarray(inputs["w_xv"][0]),
        "w_xo": np.ascontiguousarray(inputs["w_xo"][0]),
        "w_ff1": np.ascontiguousarray(inputs["w_ff1"][0]),
        "w_ff2": np.ascontiguousarray(inputs["w_ff2"][0]),
        "norm_mix": np.ascontiguousarray(inputs["norm_mix"][0:1]),
        "norm_xattn": np.ascontiguousarray(inputs["norm_xattn"][0:1]),
        "norm_mem": np.ascontiguousarray(inputs["norm_mem"][0:1]),
        "norm_mlp": np.ascontiguousarray(inputs["norm_mlp"][0:1]),
        "norm_final": np.ascontiguousarray(np.asarray(inputs["norm_final"]).reshape(1, 1024)),
        "hgrn_norm": np.ascontiguousarray(inputs["hgrn_norm"][0:1]),
        "lb_logits": np.ascontiguousarray(inputs["lb_logits"]),
    }
    shared = {k: np.asarray(v, dtype=np.float32) for k, v in shared.items()}
    shared.update(consts)
    in_maps = []
    for c in range(8):
        b, hf = c // 2, c % 2
        xall = np.zeros((8192, 1024), np.float32)
        if hf == 1:
            xall[:4096] = x[b, :4096]
        xall[4096:] = x[b, hf * 4096:(hf + 1) * 4096]
        pos = np.concatenate([np.arange(4096), hf * 4096 + np.arange(4096)])
        vb = np.full((16, 32), NEG, np.float32)
        for j in range(16):
            lo = 16 if hf == 0 else 0
            vb[j, lo:16 + j] = 0.0
        m = dict(shared)
        m["xall"] = xall
        m["mem"] = np.ascontiguousarray(memv[b])
        m["rot"] = _rot_table(pos)
        m["vbias"] = vb.reshape(1, 512)
        in_maps.append(m)
    return in_maps


def kernel(**inputs):
    if "nc" not in _NC_CACHE:
        _NC_CACHE["nc"] = build_nc()
    nc = _NC_CACHE["nc"]
    in_maps = make_in_maps(inputs)
    res = run_bass_kernel_spmd(nc, in_maps, core_ids=list(range(8)))
    outp = np.zeros((4, 8192, 1024), np.float32)
    for c in range(8):
        b, hf = c // 2, c % 2
        outp[b, hf * 4096:(hf + 1) * 4096] = np.asarray(res.results[c]["out"], dtype=np.float32)
    return outp
```

```python
import numpy as np
from contextlib import ExitStack
import ml_dtypes
import concourse.bass as bass
import concourse.mybir as mybir
from concourse.bass_utils import run_bass_kernel_spmd

F32 = mybir.dt.float32
BF16 = mybir.dt.bfloat16
AF = mybir.ActivationFunctionType
ALU = mybir.AluOpType

ENGS = ("pe", "act", "dve", "pool", "sp")
EPS = 1e-6
NEG = -1.0e4


class Buf:
    __slots__ = ("name", "last_w", "readers", "excl")

    def __init__(self, name=""):
        self.name = name
        self.excl = False
        self.last_w = None
        self.readers = []


class Op:
    __slots__ = ("eng", "fn", "deps", "is_dma", "lane", "lane_val", "pos", "signal", "sigval", "waits", "idx")


class Sched:
    def __init__(self, nc):
        self.nc = nc
        self.ops = []
        self.streams = {e: [] for e in ENGS}
        self.lane_count = {}
        self.lane_last = {}
        self.out_lanes = set()
        self.bar = []
        self.bar_pending = {e: False for e in ENGS}

    def barrier(self):
        bar = []
        for e in ENGS:
            for op in reversed(self.streams[e]):
                if not op.is_dma:
                    bar.append(op.idx)
                    break
        for lane, idx in self.lane_last.items():
            bar.append(idx)
        self.bar = bar
        self.bar_pending = {e: True for e in ENGS}

    def _record(self, eng, fn, reads, writes, is_dma=False, lane=None):
        op = Op()
        op.eng = eng
        op.fn = fn
        op.is_dma = is_dma
        op.lane = lane
        op.signal = False
        op.sigval = 0
        op.waits = []
        op.idx = len(self.ops)
        if any(b.excl for b in reads):
            writes = writes + [b for b in reads if b.excl and b not in writes]
        deps = {}
        for b in reads:
            if b.last_w is not None:
                deps[b.last_w] = True
        for b in writes:
            if b.last_w is not None:
                deps.setdefault(b.last_w, False)
            for r in b.readers:
                deps.setdefault(r, False)
        if self.bar_pending[eng]:
            for i in self.bar:
                deps.setdefault(i, False)
            self.bar_pending[eng] = False
        op.deps = deps
        for b in reads:
            b.readers.append(op.idx)
        for b in writes:
            b.last_w = op.idx
            b.readers = []
        if is_dma:
            c = self.lane_count.get(lane, 0) + 1
            self.lane_count[lane] = c
            op.lane_val = 16 * c
            self.lane_last[lane] = op.idx
        op.pos = len(self.streams[eng])
        self.streams[eng].append(op)
        self.ops.append(op)
        return op

    _cap = None

    def begin(self):
        self._cap = []

    def end(self):
        l = self._cap
        self._cap = None
        return l

    def merged(self, lists):
        items = []
        for li, l in enumerate(lists):
            n = len(l)
            for i, it in enumerate(l):
                items.append(((i + 0.5) / n, li, i, it))
        items.sort(key=lambda x: (x[0], x[1], x[2]))
        for _, _, _, it in items:
            if it[0] == "op":
                self.op(*it[1:])
            else:
                self.dma(*it[1:])

    def op(self, eng, fn, reads=(), writes=()):
        if self._cap is not None:
            self._cap.append(("op", eng, fn, list(reads), list(writes)))
            return None
        return self._record(eng, fn, list(reads), list(writes))

    def dma(self, eng, fn, reads=(), writes=(), lane=None, is_out=False):
        if self._cap is not None:
            self._cap.append(("dma", eng, fn, list(reads), list(writes), lane, is_out))
            return None
        if is_out:
            self.out_lanes.add(lane)
        return self._record(eng, fn, list(reads), list(writes), is_dma=True, lane=lane)

    def resolve(self):
        seen = {e: {} for e in ENGS}
        seen_lane = {e: {} for e in ENGS}
        for op in self.ops:
            E = op.eng
            need = {}
            need_lane = {}
            for di, is_raw in op.deps.items():
                d = self.ops[di]
                if d.is_dma:
                    if seen_lane[E].get(d.lane, 0) < d.lane_val:
                        need_lane[d.lane] = max(need_lane.get(d.lane, 0), d.lane_val)
                else:
                    if d.eng == E and E == "pe":
                        continue
                    if seen[E].get(d.eng, -1) < d.pos:
                        need[d.eng] = max(need.get(d.eng, -1), d.pos)
            for Ed, pos in need.items():
                seen[E][Ed] = pos
                self.streams[Ed][pos].signal = True
                op.waits.append(("c", Ed, pos))
            for lane, val in need_lane.items():
                seen_lane[E][lane] = val
                op.waits.append(("d", lane, val))
        for e in ENGS:
            c = 0
            for op in self.streams[e]:
                if (not op.is_dma) and op.signal:
                    c += 1
                    op.sigval = c

    def emit(self, stack):
        nc = self.nc
        self.resolve()
        esem = {e: stack.enter_context(nc.semaphore("cs_" + e)) for e in ENGS}
        lsem = {}
        for lane in self.lane_count:
            lsem[lane] = stack.enter_context(nc.semaphore("dl_" + lane))
        block = stack.enter_context(nc.Block())
        streams = self.streams
        out_lanes = self.out_lanes
        lane_count = self.lane_count

        def run(eng_name, e):
            for op in streams[eng_name]:
                for w in op.waits:
                    if w[0] == "c":
                        e.wait_ge(esem[w[1]], streams[w[1]][w[2]].sigval)
                    else:
                        e.wait_ge(lsem[w[1]], w[2])
                ins = op.fn(e)
                if op.is_dma:
                    ins.then_inc(lsem[op.lane], 16)
                elif op.signal:
                    ins.then_inc(esem[eng_name], 1)
            if eng_name == "sp":
                for lane in sorted(out_lanes):
                    e.wait_ge(lsem[lane], 16 * lane_count[lane])

        @block.tensor
        def _(e):
            run("pe", e)

        @block.scalar
        def _(e):
            run("act", e)

        @block.vector
        def _(e):
            run("dve", e)

        @block.gpsimd
        def _(e):
            run("pool", e)

        @block.sync
        def _(e):
            run("sp", e)


class T:
    __slots__ = ("ap", "b", "flat")

    def __init__(self, ap, b=None, name=""):
        self.ap = ap
        self.flat = ap
        self.b = b if b is not None else Buf(name)


def _bufs(ts):
    return [t.b if isinstance(t, T) else t for t in ts]


def build_nc(debug=False):
    nc = bass.Bass("TRN2", target_bir_lowering=False)

    def din(name, shape, dt=F32):
        return nc.dram_tensor(name, list(shape), dt, kind="ExternalInput").ap()

    xall = din("xall", [8192, 1024])
    mem = din("mem", [256, 1024])
    w_in = din("w_in", [1024, 3584])
    w_out = din("w_out", [1024, 1024])
    w_xq = din("w_xq", [1024, 1024])
    w_xk = din("w_xk", [1024, 1024])
    w_xv = din("w_xv", [1024, 1024])
    w_xo = din("w_xo", [1024, 1024])
    w_ff1 = din("w_ff1", [1024, 4096])
    w_ff2 = din("w_ff2", [4096, 1024])
    norm_mix = din("norm_mix", [1, 1024])
    norm_xattn = din("norm_xattn", [1, 1024])
    norm_mem = din("norm_mem", [1, 1024])
    norm_mlp = din("norm_mlp", [1, 1024])
    norm_final = din("norm_final", [1, 1024])
    hgrn_norm = din("hgrn_norm", [1, 512])
    lb_logits = din("lb_logits", [2, 512])
    rot = din("rot", [8192, 32])
    vbias = din("vbias", [1, 512])
    cU = din("cU", [128, 128])
    cL = din("cL", [128, 128])
    cMask4 = din("cMask4", [128, 512])
    cTriB = din("cTriB", [128, 128], BF16)
    cIdent = din("cIdent", [128, 128], BF16)
    out = nc.dram_tensor("out", [4096, 1024], F32, kind="ExternalOutput").ap()
    mixT = nc.dram_tensor("mixT", [1024, 4096], BF16, kind="ExternalOutput" if debug else "Internal").ap()
    ff1b = nc.dram_tensor("ff1b", [8, 128, 4096], BF16).ap()
    ff2b = nc.dram_tensor("ff2b", [8, 128, 4096], BF16).ap()

    S = Sched(nc)
    st = ExitStack()
    AR_ELEMS = 104000
    arena = st.enter_context(nc.sbuf_tensor("arena", [128, AR_ELEMS], BF16))
    banks = [st.enter_context(nc.psum_tensor("pb%d" % i, [128, 1024], BF16)) for i in range(8)]

    class PB:
        def __init__(self, t, i):
            self.bf = t[:, :]
            self.f = t[:, :].bitcast(F32)
            self.b = Buf("pb%d" % i)
            self.b.excl = True

    P = [PB(banks[i], i) for i in range(8)]

    class Arena:
        def __init__(self):
            self.off = 0
            self.top = AR_ELEMS

        def alloc_top(self, shape, dt, name=""):
            n = 1
            for s_ in shape:
                n *= s_
            assert dt == BF16
            nel = (n + 31) // 32 * 32
            self.top -= nel
            assert self.off <= self.top, ("arena overflow(top)", name)
            v = arena[:, self.top:self.top + nel][:, 0:n]
            if len(shape) == 2:
                v = v.rearrange("p (a b) -> p a b", a=shape[0])
            return T(v, name=name)

        def alloc(self, shape, dt, name=""):
            n = 1
            for s_ in shape:
                n *= s_
            nel = n * (2 if dt == F32 else 1)
            nel = (nel + 31) // 32 * 32
            assert self.off + nel <= self.top, ("arena overflow", name, self.off, nel, self.top)
            v = arena[:, self.off:self.off + nel]
            self.off += nel
            if dt == F32:
                v = v.bitcast(F32)
                v = v[:, 0:n]
            else:
                v = v[:, 0:n]
            flat = v
            if len(shape) == 2:
                v = v.rearrange("p (a b) -> p a b", a=shape[0])
            elif len(shape) == 3:
                v = v.rearrange("p (a b c) -> p a b c", a=shape[0], b=shape[1])
            t_ = T(v, name=name)
            t_.flat = flat
            return t_

    A = Arena()

    def mm(out_ap, lhsT, rhs, start, stop, r, w):
        S.op("pe", lambda e: e.matmul(out_ap, lhsT=lhsT, rhs=rhs, start=start, stop=stop), _bufs(r), _bufs(w))

    def tr(out_ap, in_ap, r, w):
        S.op("pe", lambda e: e.transpose(out=out_ap, in_=in_ap, identity=ident.ap), _bufs(r) + [ident.b], _bufs(w))

    def act(out_ap, in_ap, func, r, w, bias=None, scale=None, accum=None):
        kw = {}
        if bias is not None:
            kw["bias"] = bias
        if scale is not None:
            kw["scale"] = scale
        if accum is not None:
            kw["accum_out"] = accum
        S.op("act", lambda e: e.activation(out=out_ap, in_=in_ap, func=func, **kw), _bufs(r), _bufs(w))

    def tt(eng, out_ap, in0, in1, op, r, w):
        S.op(eng, lambda e: e.tensor_tensor(out=out_ap, in0=in0, in1=in1, op=op), _bufs(r), _bufs(w))

    def ts(eng, out_ap, in0, s1, s2, op0, op1, r, w):
        if op1 is None:
            S.op(eng, lambda e: e.tensor_scalar(out=out_ap, in0=in0, scalar1=s1, scalar2=None, op0=op0), _bufs(r), _bufs(w))
        else:
            S.op(eng, lambda e: e.tensor_scalar(out=out_ap, in0=in0, scalar1=s1, scalar2=s2, op0=op0, op1=op1), _bufs(r), _bufs(w))

    def stt(eng, out_ap, in0, scalar, in1, op0, op1, r, w):
        S.op(eng, lambda e: e.scalar_tensor_tensor(out=out_ap, in0=in0, scalar=scalar, in1=in1, op0=op0, op1=op1), _bufs(r), _bufs(w))

    def cp(eng, out_ap, in_ap, r, w):
        if eng == "act":
            act(out_ap, in_ap, AF.Copy, r, w)
        else:
            S.op(eng, lambda e: e.tensor_copy(out=out_ap, in_=in_ap), _bufs(r), _bufs(w))

    def recip(out_ap, in_ap, r, w):
        S.op("dve", lambda e: e.reciprocal(out=out_ap, in_=in_ap), _bufs(r), _bufs(w))

    def vmax(out_ap, in_ap, r, w):
        S.op("dve", lambda e: e.max(out=out_ap, in_=in_ap), _bufs(r), _bufs(w))

    def memset(eng, ap, val, w):
        S.op(eng, lambda e: e.memset(ap, val), [], _bufs(w))

    def dma(eng, out_ap, in_ap, r, w, lane, is_out=False):
        S.dma(eng, lambda e: e.dma_start(out=out_ap, in_=in_ap), _bufs(r), _bufs(w), lane=lane, is_out=is_out)

    ident = A.alloc([128], BF16, "ident")
    triB = A.alloc([128], BF16, "triB")
    Uc = A.alloc([128], F32, "Uc")
    Lc = A.alloc([128], F32, "Lc")
    mask4 = A.alloc([512], F32, "mask4")
    ones_f = A.alloc([8], F32, "ones_f")
    ones_b = A.alloc([128], BF16, "ones_b")
    epsb = A.alloc([2], F32, "epsb")
    dma("sp", ident.ap, cIdent[:, :], [], [ident], "c_ident")
    dma("sp", triB.ap, cTriB[:, :], [], [triB], "c_trib")
    dma("sp", Uc.ap, cU[:, :], [], [Uc], "c_u")
    dma("sp", Lc.ap, cL[:, :], [], [Lc], "c_l")
    dma("sp", mask4.ap, cMask4[:, :], [], [mask4], "c_m4")
    memset("dve", ones_f.ap, 1.0, [ones_f])
    memset("dve", ones_b.ap, 1.0, [ones_b])
    memset("dve", epsb.ap, EPS, [epsb])
    A_persist = A.off

    def norm_tile(xt, g_bc, junk, ssb, xs, pb, dst_ap, dst_bufs, evac_eng, d_model=1024):
        act(junk.ap, xt.ap, AF.Square, [xt], [junk, ssb], accum=ssb.ap[:, 0:1])
        act(ssb.ap[:, 1:2], ssb.ap[:, 0:1], AF.Ln, [ssb, epsb], [ssb], bias=epsb.ap[:, 0:1], scale=1.0 / d_model)
        act(ssb.ap[:, 2:3], ssb.ap[:, 1:2], AF.Exp, [ssb], [ssb], scale=-0.5)
        stt("dve", xs.ap, xt.ap, ssb.ap[:, 2:3], g_bc.ap, ALU.mult, ALU.mult, [xt, ssb, g_bc], [xs])
        for c in range(8):
            tr(pb.bf[:, c * 128:(c + 1) * 128], xs.ap[:, c * 128:(c + 1) * 128], [xs], [pb.b])
        cp(evac_eng, dst_ap, pb.bf[:, 0:1024].rearrange("p (c t) -> p c t", c=8), [pb.b], dst_bufs)

    stg = [A.alloc([4096], BF16, "stg%d" % i) for i in range(2)]
    ff1b_b = [Buf("ff1b%d" % i) for i in range(8)]
    ff2b_b = [Buf("ff2b%d" % i) for i in range(8)]
    w_ff1_v = w_ff1.rearrange("(c p) n -> p c n", p=128)

    def stage_cast(k):
        s_ = stg[k % 2]
        if k < 8:
            dma("pool", s_.ap.rearrange("p (c n) -> p c n", c=8), w_ff1_v[:, :, k * 512:(k + 1) * 512], [], [s_], "stg%d" % (k % 2))
        else:
            j2 = k - 8
            dma("pool", s_.ap.rearrange("p (a n) -> p a n", a=4),
                w_ff2[j2 * 512:(j2 + 1) * 512, :].rearrange("(a p) n -> p a n", p=128), [], [s_], "stg%d" % (k % 2))

    def stage_store(k):
        s_ = stg[k % 2]
        if k < 8:
            dma("sp", ff1b[k], s_.ap, [s_], [ff1b_b[k]], "stgo%d" % (k % 2))
        else:
            dma("sp", ff2b[k - 8], s_.ap, [s_], [ff2b_b[k - 8]], "stgo%d" % (k % 2))

    mix_b = [Buf("mix%d" % i) for i in range(32)]
    g_mix = A.alloc([1024], F32, "g_mix")
    dma("sp", g_mix.ap, norm_mix.partition_broadcast(128), [], [g_mix], "c_gmix")
    A_hm = A.off
    Wo = A.alloc_top([8, 1024], BF16, "Wo")
    Wxq = A.alloc_top([8, 1024], BF16, "Wxq")
    Wxo = A.alloc_top([8, 1024], BF16, "Wxo")
    A_top_F = A.top
    Wm0 = A.alloc_top([8, 768], BF16, "Wm0")

    Wh = A.alloc([8, 2048], BF16, "Wh")
    w_in_v = w_in.rearrange("(c p) n -> p c n", p=128)
    for c in range(8):
        dma("pool", Wh.ap[:, c, :], w_in_v[:, c, 0:2048], [], [Wh], "w_h")
    for c in range(8):
        dma("pool", Wm0.ap[:, c, 0:256], w_in_v[:, c, 2048:2304], [], [Wm0], "w_m0")
        dma("pool", Wm0.ap[:, c, 256:512], w_in_v[:, c, 2560:2816], [], [Wm0], "w_m0")
        dma("pool", Wm0.ap[:, c, 512:768], w_in_v[:, c, 3072:3328], [], [Wm0], "w_m0")
    lb_bc = A.alloc([512], F32, "lb")
    oml_bc = A.alloc([512], F32, "oml")
    hgn_bc = A.alloc([512], F32, "hgn")
    l1_bc = A.alloc([512], F32, "l1")
    dma("sp", lb_bc.ap, lb_logits[0:1, :].partition_broadcast(128), [], [lb_bc], "c_lb0")
    dma("sp", l1_bc.ap, lb_logits[1:2, :].partition_broadcast(128), [], [l1_bc], "c_lb1")
    dma("sp", hgn_bc.ap, hgrn_norm.partition_broadcast(128), [], [hgn_bc], "c_hgn")
    tt("dve", lb_bc.ap, l1_bc.ap, lb_bc.ap, ALU.subtract, [l1_bc, lb_bc], [lb_bc])
    act(lb_bc.ap, lb_bc.ap, AF.Exp, [lb_bc], [lb_bc])
    ts("dve", lb_bc.ap, lb_bc.ap, 1.0, None, ALU.add, None, [lb_bc], [lb_bc])
    recip(lb_bc.ap, lb_bc.ap, [lb_bc], [lb_bc])
    ts("dve", oml_bc.ap, lb_bc.ap, -1.0, 1.0, ALU.mult, ALU.add, [lb_bc], [oml_bc])

    xt = [A.alloc([1024], F32, "xt%d" % i) for i in range(2)]
    junk = A.alloc([1024], BF16, "junk")
    ssb = [A.alloc([4], F32, "ssb%d" % i) for i in range(2)]
    xs = [A.alloc([1024], BF16, "xs%d" % i) for i in range(2)]
    hT = [A.alloc([8, 128], BF16, "hT%d" % i) for i in range(2)]
    ez = [A.alloc([512], F32, "ez%d" % i) for i in range(2)]
    vv = [A.alloc([512], BF16, "v%d" % i) for i in range(4)]
    qs = [A.alloc([512], F32, "qs%d" % i) for i in range(2)]
    eg = [A.alloc([512], F32, "eg%d" % i) for i in range(2)]
    gr = [A.alloc([512], F32, "gr%d" % i) for i in range(2)]
    ff = A.alloc([512], F32, "f")
    logf = A.alloc([512], F32, "logf")
    kk = A.alloc([512], F32, "k")
    eb = A.alloc([512], F32, "eb")
    enb = A.alloc([512], F32, "enb")
    eend = A.alloc([512], F32, "eend")
    qts = [A.alloc([512], BF16, "qt%d" % i) for i in range(2)]
    kts = [A.alloc([512], BF16, "kt%d" % i) for i in range(2)]
    kend = [A.alloc([512], BF16, "kend%d" % i) for i in range(3)]
    qkT = [A.alloc([8, 128], BF16, "qkT%d" % i) for i in range(2)]
    ATm = [A.alloc([512], BF16, "ATm%d" % i) for i in range(2)]
    Sst = A.alloc([512], F32, "Sst")
    Sb = A.alloc([512], BF16, "Sb")
    dec = [A.alloc([4], F32, "dec%d" % i) for i in range(3)]
    gs = [A.alloc([512], F32, "gs%d" % i) for i in range(3)]
    ssq = A.alloc([12], F32, "ssq")
    junkC = A.alloc([128], BF16, "junkC")
    on = A.alloc([512], BF16, "on")
    oT = [A.alloc([4, 128], BF16, "oT%d" % i) for i in range(2)]
    memset("dve", Sst.ap, 0.0, [Sst])
    memset("dve", Sb.ap, 0.0, [Sb])

    def hs(h):
        return slice(h * 128, (h + 1) * 128)

    def stageA(t):
        local = t >= 32
        sl = t % 2
        dma("sp", xt[sl].ap, xall[t * 128:(t + 1) * 128, :], [], [xt[sl]], "xt%d" % sl)
        if t % 4 == 0:
            stage_cast(t // 4)
        if t % 4 == 3:
            stage_store(t // 4)
        norm_tile(xt[sl], g_mix, junk, ssb[sl], xs[sl], P[0], hT[sl].ap, [hT[sl]], "act")
        for c in range(8):
            mm(P[1].f[:, 0:512], hT[sl].ap[:, c, :], Wh.ap[:, c, 512:1024], c == 0, c == 7, [hT[sl], Wh], [P[1].b])
        for c in range(8):
            mm(P[2].f[:, 0:512], hT[sl].ap[:, c, :], Wh.ap[:, c, 1024:1536], c == 0, c == 7, [hT[sl], Wh], [P[2].b])
        act(ez[sl].ap, P[1].f[:, 0:512], AF.Exp, [P[1].b], [ez[sl]], scale=-1.0)
        act(vv[t % 4].ap, P[2].f[:, 0:512], AF.Copy, [P[2].b], [vv[t % 4]])
        if local:
            for c in range(8):
                mm(P[1].f[:, 0:512], hT[sl].ap[:, c, :], Wh.ap[:, c, 0:512], c == 0, c == 7, [hT[sl], Wh], [P[1].b])
            for c in range(8):
                mm(P[2].f[:, 0:512], hT[sl].ap[:, c, :], Wh.ap[:, c, 1536:2048], c == 0, c == 7, [hT[sl], Wh], [P[2].b])
            cp("dve", qs[sl].ap, P[1].f[:, 0:512], [P[1].b], [qs[sl]])
            act(eg[sl].ap, P[2].f[:, 0:512], AF.Exp, [P[2].b], [eg[sl]], scale=-1.0)
            cp("dve", gr[sl].ap, P[2].f[:, 0:512], [P[2].b], [gr[sl]])

    def stageB(t):
        local = t >= 32
        sl = t % 2
        s3 = t % 3
        e_ = ez[sl]
        act(e_.ap, e_.ap, AF.Ln, [e_, ones_f], [e_], bias=ones_f.ap[:, 0:1])
        act(e_.ap, e_.ap, AF.Exp, [e_], [e_], scale=-1.0)
        tt("dve", ff.ap, e_.ap, oml_bc.ap, ALU.mult, [e_, oml_bc], [ff])
        tt("pool", ff.ap, ff.ap, lb_bc.ap, ALU.add, [ff, lb_bc], [ff])
        act(logf.ap, ff.ap, AF.Ln, [ff], [logf])
        ts("pool", kk.ap, ff.ap, -1.0, 1.0, ALU.mult, ALU.add, [ff], [kk])
        if local:
            mm(P[3].f[:, 0:512], Uc.ap, logf.ap, True, True, [Uc, logf], [P[3].b])
        mm(P[4].f[:, 0:512], Lc.ap, logf.ap, True, True, [Lc, logf], [P[4].b])
        act(eend.ap, P[4].f[:, 0:512], AF.Exp, [P[4].b], [eend])
        tt("pool", kend[s3].ap, kk.ap, eend.ap, ALU.mult, [kk, eend], [kend[s3]])
        if local:
            act(eb.ap, P[3].f[:, 0:512], AF.Exp, [P[3].b], [eb])
            act(enb.ap, P[3].f[:, 0:512], AF.Exp, [P[3].b], [enb], scale=-1.0)
            tt("dve", qts[sl].ap, qs[sl].ap, eb.ap, ALU.mult, [qs[sl], eb], [qts[sl]])
            tt("pool", kts[sl].ap, kk.ap, enb.ap, ALU.mult, [kk, enb], [kts[sl]])
            g_ = eg[sl]
            act(g_.ap, g_.ap, AF.Ln, [g_, ones_f], [g_], bias=ones_f.ap[:, 0:1])
            act(g_.ap, g_.ap, AF.Exp, [g_], [g_], scale=-1.0)
            tt("dve", gs[s3].ap, gr[sl].ap, g_.ap, ALU.mult, [gr[sl], g_], [gs[s3]])
            tt("pool", gs[s3].ap, gs[s3].ap, hgn_bc.ap, ALU.mult, [gs[s3], hgn_bc], [gs[s3]])
        for h in range(4):
            mm(P[4].f[:, h:h + 1], logf.ap[:, hs(h)], ones_f.ap[:, 0:1], True, True, [logf, ones_f], [P[4].b])
        act(dec[s3].ap, P[4].f[:, 0:4], AF.Exp, [P[4].b], [dec[s3]])

    def stageB2(t):
        sl = t % 2
        qt, kt = qts[sl], kts[sl]
        for h in range(4):
            tr(P[5].bf[:, h * 128:(h + 1) * 128], qt.ap[:, hs(h)], [qt], [P[5].b])
        for h in range(4):
            tr(P[5].bf[:, 512 + h * 128:512 + (h + 1) * 128], kt.ap[:, hs(h)], [kt], [P[5].b])
        cp("dve", qkT[sl].ap, P[5].bf[:, 0:1024].rearrange("p (c t) -> p c t", c=8), [P[5].b], [qkT[sl]])
        for h in range(4):
            mm(P[5].f[:, hs(h)], qkT[sl].ap[:, 4 + h, :], qkT[sl].ap[:, h, :], True, True, [qkT[sl]], [P[5].b])
        tt("dve", ATm[sl].ap, P[5].f[:, 0:512], mask4.ap, ALU.mult, [P[5].b, mask4], [ATm[sl]])

    def stageC(t):
        local = t >= 32
        sl = t % 2
        v_ = vv[t % 4]
        s3 = t % 3
        if local:
            for h in range(4):
                mm(P[7].f[:, hs(h)], ATm[sl].ap[:, hs(h)], v_.ap[:, hs(h)], True, False, [ATm[sl], v_], [P[7].b])
                mm(P[7].f[:, hs(h)], qkT[sl].ap[:, h, :], Sb.ap[:, hs(h)], False, True, [qkT[sl], Sb], [P[7].b])
        for h in range(4):
            mm(P[6].f[:, hs(h)], kend[s3].ap[:, hs(h)], v_.ap[:, hs(h)], True, True, [kend[s3], v_], [P[6].b])
        for h in range(4):
            stt("dve", Sst.ap[:, hs(h)], Sst.ap[:, hs(h)], dec[s3].ap[:, h:h + 1], P[6].f[:, hs(h)], ALU.mult, ALU.add,
                [Sst, dec[s3], P[6].b], [Sst])
        cp("pool", Sb.ap, Sst.ap, [Sst], [Sb])
        if local:
            i = t - 32
            for h in range(4):
                act(junkC.ap[:, 0:128], P[7].f[:, hs(h)], AF.Square, [P[7].b], [junkC, ssq], accum=ssq.ap[:, h:h + 1])
            act(ssq.ap[:, 4:8], ssq.ap[:, 0:4], AF.Ln, [ssq, epsb], [ssq], bias=epsb.ap[:, 0:1], scale=1.0 / 128)
            act(ssq.ap[:, 8:12], ssq.ap[:, 4:8], AF.Exp, [ssq], [ssq], scale=-0.5)
            for h in range(4):
                stt("dve", on.ap[:, hs(h)], P[7].f[:, hs(h)], ssq.ap[:, 8 + h:9 + h], gs[s3].ap[:, hs(h)], ALU.mult, ALU.mult,
                    [P[7].b, ssq, gs[s3]], [on])
            for h in range(4):
                tr(P[7].bf[:, h * 128:(h + 1) * 128], on.ap[:, hs(h)], [on], [P[7].b])
            cp("act", oT[sl].ap, P[7].bf[:, 0:512].rearrange("p (c t) -> p c t", c=4), [P[7].b], [oT[sl]])
            dma("sp", mixT[0:512, i * 128:(i + 1) * 128].rearrange("(h p) t -> p h t", p=128), oT[sl].ap,
                [oT[sl]], [mix_b[i]], "mixo%d" % sl)

    import os as _os
    for it in range(0 if _os.environ.get('SKIP_H') else 64 + 3):
        lists = []
        if it < 64:
            S.begin(); stageA(it); lists.append(S.end())
        if 0 <= it - 1 < 64:
            S.begin(); stageB(it - 1); lists.append(S.end())
        if 32 <= it - 2 < 64:
            S.begin(); stageB2(it - 2); lists.append(S.end())
        if 0 <= it - 3 < 64:
            S.begin(); stageC(it - 3); lists.append(S.end())
        S.merged(lists)

    SCALE = 128 ** -0.5
    def load_Wm(hp_, Wm_):
        for c in range(8):
            dma("pool", Wm_.ap[:, c, 0:256], w_in_v[:, c, 2048 + hp_ * 256:2048 + (hp_ + 1) * 256], [], [Wm_], "w_m%d" % hp_)
            dma("pool", Wm_.ap[:, c, 256:512], w_in_v[:, c, 2560 + hp_ * 256:2560 + (hp_ + 1) * 256], [], [Wm_], "w_m%d" % hp_)
            dma("pool", Wm_.ap[:, c, 512:768], w_in_v[:, c, 3072 + hp_ * 256:3072 + (hp_ + 1) * 256], [], [Wm_], "w_m%d" % hp_)

    S.barrier()
    A.off = A_hm
    Wms = [Wm0, A.alloc([8, 768], BF16, "Wm1")]
    KT = A.alloc([2, 8192], BF16, "KT")
    KT_b = [Buf("KT%d" % i) for i in range(64)]
    Vaug = A.alloc([64, 2, 130], BF16, "Vaug")
    V_b = [Buf("V%d" % i) for i in range(64)]
    KmF = A.alloc([2, 32], F32, "KmF")
    KmT = A.alloc([2, 32], BF16, "KmT")
    vb_bc = A.alloc([512], F32, "vb")
    dma("sp", vb_bc.ap, vbias.partition_broadcast(128), [], [vb_bc], "c_vb")
    xt = [A.alloc([1024], F32, "xt%d" % i) for i in range(2)]
    junks = [A.alloc([1024], BF16, "junk%d" % i) for i in range(2)]
    ssb = [A.alloc([4], F32, "ssb%d" % i) for i in range(2)]
    xs = [A.alloc([1024], BF16, "xs%d" % i) for i in range(2)]
    hT = [A.alloc([8, 128], BF16, "hT%d" % i) for i in range(2)]
    rt = [A.alloc([32], F32, "rt%d" % i) for i in range(2)]
    ras = [A.alloc([4, 16], F32, "ra%d" % i) for i in range(2)]
    rbs = [A.alloc([4, 16], F32, "rb%d" % i) for i in range(2)]
    qkrs = [A.alloc([4, 128], BF16, "qkr%d" % i) for i in range(2)]
    QTs = [A.alloc([2, 128], BF16, "QT%d" % i) for i in range(2)]
    Gp = A.alloc([32], F32, "Gp")
    mx8 = A.alloc([8], F32, "mx8")
    bqs = [[A.alloc([32], F32, "bq%d_%d" % (i, k)) for k in range(2)] for i in range(2)]
    Pm = [A.alloc([512], BF16, "Pm%d" % i) for i in range(4)]
    PT = [A.alloc([512], BF16, "PT%d" % i) for i in range(3)]
    rden = A.alloc([2], F32, "rden")
    om = A.alloc([2, 128], BF16, "om")
    omT = [A.alloc([2, 128], BF16, "omT%d" % i) for i in range(2)]
    M_ALLOC_MARK = True
    for hp in range(0 if _os.environ.get('SKIP_M') else 2):
        Wm = Wms[hp]
        if hp == 0:
            load_Wm(1, Wms[1])
        else:
            for c in range(8):
                dma("pool", Wo.ap[:, c, :], w_out.rearrange("(c p) n -> p c n", p=128)[:, c, :], [], [Wo], "w_o")
                dma("pool", Wxq.ap[:, c, :], w_xq.rearrange("(c p) n -> p c n", p=128)[:, c, :], [], [Wxq], "w_xq")
                dma("pool", Wxo.ap[:, c, :], w_xo.rearrange("(c p) n -> p c n", p=128)[:, c, :], [], [Wxo], "w_xo")
        if hp == 0:
            memset("pool", Vaug.flat, 1.0, V_b)
        memset("dve", KmF.flat, 0.0, [KmF])
        memset("dve", KmT.flat, 0.0, [KmT])

        def front(t, pa, pq):
            local = t >= 32
            sl = t % 2
            nblk = t // 2
            ra, rb, qkr, QT = ras[sl], rbs[sl], qkrs[sl], QTs[sl]
            dma("sp", xt[sl].ap, xall[t * 128:(t + 1) * 128, :], [], [xt[sl]], "xt%d" % sl)
            dma("sp", rt[sl].ap, rot[t * 128:(t + 1) * 128, :], [], [rt[sl]], "rt%d" % sl)
            norm_tile(xt[sl], g_mix, junks[sl], ssb[sl], xs[sl], pa, hT[sl].ap, [hT[sl]], "dve")
            if local:
                for c in range(8):
                    mm(pq.f[:, 0:256], hT[sl].ap[:, c, :], Wm.ap[:, c, 0:256], c == 0, False, [hT[sl], Wm], [pq.b])
            for c in range(8):
                mm(pq.f[:, 256:512], hT[sl].ap[:, c, :], Wm.ap[:, c, 256:512], (c == 0 and not local), c == 7,
                   [hT[sl], Wm], [pq.b])
            for c in range(8):
                mm(pa.f[:, 0:256], hT[sl].ap[:, c, :], Wm.ap[:, c, 512:768], c == 0, c == 7, [hT[sl], Wm], [pa.b])
            cp("dve", Vaug.ap[:, t, :, 0:128], pa.f[:, 0:256].rearrange("p (h d) -> p h d", h=2), [pa.b], [V_b[t]])
            h0 = 0 if local else 2
            nh = 4 - h0
            X = pq.f[:, 0:512].rearrange("p (h d) -> p h d", h=4)
            cosb = rt[sl].ap[:, 0:16].unsqueeze(1).to_broadcast([128, nh, 16])
            sinb = rt[sl].ap[:, 16:32].unsqueeze(1).to_broadcast([128, nh, 16])
            x1 = X[:, h0:4, 0:16]
            x2 = X[:, h0:4, 16:32]
            tt("dve", ra.ap[:, h0:4, :], x1, cosb, ALU.mult, [pq.b, rt[sl]], [ra])
            tt("dve", rb.ap[:, h0:4, :], x2, sinb, ALU.mult, [pq.b, rt[sl]], [rb])
            tt("dve", qkr.ap[:, h0:4, 0:16], ra.ap[:, h0:4, :], rb.ap[:, h0:4, :], ALU.subtract, [ra, rb], [qkr])
            tt("dve", ra.ap[:, h0:4, :], x2, cosb, ALU.mult, [pq.b, rt[sl]], [ra])
            tt("dve", rb.ap[:, h0:4, :], x1, sinb, ALU.mult, [pq.b, rt[sl]], [rb])
            tt("dve", qkr.ap[:, h0:4, 16:32], ra.ap[:, h0:4, :], rb.ap[:, h0:4, :], ALU.add, [ra, rb], [qkr])
            cp("dve", qkr.ap[:, h0:4, 32:128], X[:, h0:4, 32:128], [pq.b], [qkr])
            for j in range(h0, 4):
                tr(pa.bf[:, j * 128:(j + 1) * 128], qkr.ap[:, j, :], [qkr], [pa.b])
            cp("dve", KT.ap[:, :, t * 128:(t + 1) * 128], pa.bf[:, 256:512].rearrange("p (h t) -> p h t", h=2),
               [pa.b], [KT_b[t]])
            if local:
                cp("dve", QT.ap, pa.bf[:, 0:256].rearrange("p (h t) -> p h t", h=2), [pa.b], [QT])
            for h in range(2):
                mm(pq.f[:, h:h + 1], qkr.ap[:, 2 + h, :], ones_b.ap[:, 0:1], True, True, [qkr, ones_b], [pq.b])
            stt("dve", KmF.ap[:, :, nblk], pq.f[:, 0:2], 1.0 / 256, KmF.ap[:, :, nblk], ALU.mult, ALU.add,
                [pq.b, KmF], [KmF])
            cp("dve", KmT.ap[:, :, nblk], KmF.ap[:, :, nblk], [KmF], [KmT])
            if local:
                j_loc = (t - 32) // 2
                for h in range(2):
                    bq_ = bqs[sl][h]
                    mm(pq.f[:, 0:32], QT.ap[:, h, :], KmT.ap[:, h, :], True, True, [QT, KmT], [pq.b])
                    tt("dve", Gp.ap, pq.f[:, 0:32], vb_bc.ap[:, j_loc * 32:(j_loc + 1) * 32], ALU.add, [pq.b, vb_bc], [Gp])
                    vmax(mx8.ap, Gp.ap, [Gp], [mx8])
                    ts("dve", bq_.ap, Gp.ap, mx8.ap[:, 2:3], NEG, ALU.is_lt, ALU.mult, [Gp, mx8], [bq_])
                    tt("dve", bq_.ap, bq_.ap, vb_bc.ap[:, j_loc * 32:(j_loc + 1) * 32], ALU.add, [bq_, vb_bc], [bq_])

        def attention(t):
            sl = t % 2
            QT = QTs[sl]
            J = t // 2
            i = t - 32
            j_loc = i // 2
            po = P[7]
            units = []
            totals = []
            for h in range(2):
                hu = []
                n0 = 0
                while n0 < J:
                    nb_ = 2 if n0 + 1 < J else 1
                    hu.append(("past", n0, nb_, h))
                    n0 += nb_
                nk_own = 1 if (t % 2 == 0) else 2
                hu.append(("own", J, nk_own, h))
                totals.append(sum((u[2] * 2 if u[0] == "past" else u[2]) for u in hu))
                units += hu
            kt_done = [0, 0]

            def stage_S(ui, u):
                h = u[3]
                pb = P[2 + ui % 3]
                pm = Pm[ui % 4]
                if u[0] == "past":
                    n0_, nb2 = u[1], u[2]
                    w = nb2 * 256
                    kbufs = [KT_b[2 * n0_ + x] for x in range(2 * nb2)]
                    mm(pb.f[:, 0:w], QT.ap[:, h, :], KT.ap[:, h, n0_ * 256:n0_ * 256 + w], True, True, [QT] + kbufs, [pb.b])
                    for x in range(nb2):
                        act(pm.ap[:, x * 256:(x + 1) * 256], pb.f[:, x * 256:(x + 1) * 256], AF.Exp, [pb.b, bqs[sl][h]], [pm],
                            bias=bqs[sl][h].ap[:, n0_ + x:n0_ + x + 1], scale=SCALE)
                else:
                    nk = u[2]
                    w = nk * 128
                    kbufs = [KT_b[2 * J + x] for x in range(nk)]
                    mm(pb.f[:, 0:w], QT.ap[:, h, :], KT.ap[:, h, J * 256:J * 256 + w], True, False, [QT] + kbufs, [pb.b])
                    mm(pb.f[:, w - 128:w], ident.ap, triB.ap, False, True, [ident, triB], [pb.b])
                    act(pm.ap[:, 0:w], pb.f[:, 0:w], AF.Exp, [pb.b], [pm], scale=SCALE)

            def stage_T(ui, u):
                pb = P[5 + ui % 2]
                pm = Pm[ui % 4]
                pt = PT[ui % 3]
                nkt = u[2] * 2 if u[0] == "past" else u[2]
                for x in range(nkt):
                    tr(pb.bf[:, x * 128:(x + 1) * 128], pm.ap[:, x * 128:(x + 1) * 128], [pm], [pb.b])
                cp("dve", pt.ap[:, 0:nkt * 128], pb.bf[:, 0:nkt * 128], [pb.b], [pt])

            def stage_PV(ui, u):
                h = u[3]
                pt = PT[ui % 3]
                nkt = u[2] * 2 if u[0] == "past" else u[2]
                kt0 = u[1] * 2
                for x in range(nkt):
                    first = kt_done[h] == 0
                    kt_done[h] += 1
                    last = kt_done[h] == totals[h]
                    mm(po.f[:, 128:258], pt.ap[:, x * 128:(x + 1) * 128], Vaug.ap[:, kt0 + x, h, :], first, last,
                       [pt, V_b[kt0 + x]], [po.b])
                if kt_done[h] == totals[h]:
                    recip(rden.ap[:, h:h + 1], po.f[:, 256:257], [po.b], [rden])
                    act(om.ap[:, h, :], po.f[:, 128:256], AF.Identity, [po.b, rden], [om], scale=rden.ap[:, h:h + 1])

            nu = len(units)
            D1, D2 = 2, 4
            for step in range(nu + D2):
                if step < nu:
                    stage_S(step, units[step])
                if 0 <= step - D1 < nu:
                    stage_T(step - D1, units[step - D1])
                if 0 <= step - D2 < nu:
                    stage_PV(step - D2, units[step - D2])
            for h in range(2):
                tr(po.bf[:, h * 128:(h + 1) * 128], om.ap[:, h, :], [om], [po.b])
            cp("dve", omT[sl].ap, po.bf[:, 0:256].rearrange("p (h t) -> p h t", h=2), [po.b], [omT[sl]])
            r0 = 512 + hp * 256
            dma("sp", mixT[r0:r0 + 256, i * 128:(i + 1) * 128].rearrange("(h p) t -> p h t", p=128), omT[sl].ap,
                [omT[sl]], [mix_b[i]], "mixo%d" % sl)

        for k in range(16):
            S.begin(); front(2 * k, P[0], P[1]); la = S.end()
            S.begin(); front(2 * k + 1, P[2], P[3]); lb_ = S.end()
            S.merged([la, lb_])
        S.begin(); front(32, P[0], P[1]); la = S.end()
        S.merged([la])
        for t in range(32, 64):
            lists = []
            S.begin(); attention(t); lists.append(S.end())
            if t + 1 < 64:
                S.begin(); front(t + 1, P[0], P[1]); lists.append(S.end())
            S.merged(lists)

    S.barrier()
    A.off = A_persist
    A.top = A_top_F
    memKT = A.alloc([8, 256], BF16, "memKT")
    memV = A.alloc([2, 1024], BF16, "memV")
    g_xa = A.alloc([1024], F32, "g_xa")
    g_mlp = A.alloc([1024], F32, "g_mlp")
    g_fin = A.alloc([1024], F32, "g_fin")
    dma("sp", g_xa.ap, norm_xattn.partition_broadcast(128), [], [g_xa], "c_gxa")
    dma("sp", g_mlp.ap, norm_mlp.partition_broadcast(128), [], [g_mlp], "c_gmlp")
    dma("sp", g_fin.ap, norm_final.partition_broadcast(128), [], [g_fin], "c_gfin")
    W1 = [A.alloc([8, 512], BF16, "W1_%d" % i) for i in range(2)]
    W2 = [A.alloc([4, 1024], BF16, "W2_%d" % i) for i in range(2)]
    X = [A.alloc([1024], F32, "X%d" % i) for i in range(4)]
    xs = [A.alloc([1024], BF16, "xsF%d" % i) for i in range(4)]
    junk = A.alloc([1024], BF16, "junkF")
    ssb = [A.alloc([4], F32, "ssbF%d" % i) for i in range(4)]
    hTs = A.alloc([8, 512], BF16, "hTs")
    hTs_b = [Buf("hTs%d" % i) for i in range(4)]

    def norm4(g_bc):
        lists = []
        for j in range(4):
            S.begin()
            norm_tile(X[j], g_bc, junk, ssb[j], xs[j], P[j], hTs.ap[:, :, j * 128:(j + 1) * 128], [hTs_b[j]], "act")
            lists.append(S.end())
        S.merged(lists)
    qxT = A.alloc([8, 512], BF16, "qxT")
    oxT = A.alloc([8, 512], BF16, "oxT")
    PTxs = [A.alloc([2, 512], BF16, "PTx%d" % i) for i in range(2)]
    rds = [A.alloc([512], F32, "rd%d" % i) for i in range(2)]
    qxT_b = [Buf("qxT%d" % i) for i in range(8)]
    oxT_b = [Buf("oxT%d" % i) for i in range(8)]
    rl = [A.alloc([512], F32, "rl%d" % i) for i in range(2)]
    hid = A.alloc([32, 512], BF16, "hid")
    hid_off = A.off - 32 * 512
    Wxk_ap = arena[:, hid_off:hid_off + 8192].rearrange("p (c n) -> p c n", c=8)
    Wxv_ap = arena[:, hid_off + 8192:hid_off + 16384].rearrange("p (c n) -> p c n", c=8)
    for c in range(8):
        dma("pool", Wxk_ap[:, c, :], w_xk.rearrange("(c p) n -> p c n", p=128)[:, c, :], [], [hid], "w_xk")
        dma("pool", Wxv_ap[:, c, :], w_xv.rearrange("(c p) n -> p c n", p=128)[:, c, :], [], [hid], "w_xk")
    g_mem = X[3]
    dma("sp", g_mem.ap, norm_mem.partition_broadcast(128), [], [g_mem], "c_gmem")
    for m in range(2):
        dma("sp", X[m].ap, mem[m * 128:(m + 1) * 128, :], [], [X[m]], "X%d" % m)
        norm_tile(X[m], g_mem, junk, ssb[m], xs[m], P[0], hTs.ap[:, :, m * 128:(m + 1) * 128], [hTs_b[m]], "act")
    for ch in range(8):
        pb = P[1 + ch % 2]
        for c in range(8):
            mm(pb.f[:, 0:256], Wxk_ap[:, c, ch * 128:(ch + 1) * 128], hTs.ap[:, c, 0:256], c == 0, c == 7, [hid] + hTs_b[0:2], [pb.b])
        cp("dve", memKT.ap[:, ch, :], pb.f[:, 0:256], [pb.b], [memKT])
    for m in range(2):
        for hh in range(2):
            pb = P[3 + hh]
            for c in range(8):
                mm(pb.f[:, 0:512], hTs.ap[:, c, m * 128:(m + 1) * 128], Wxv_ap[:, c, hh * 512:(hh + 1) * 512], c == 0, c == 7,
                   [hTs_b[m], hid], [pb.b])
            cp("act", memV.ap[:, m, hh * 512:(hh + 1) * 512], pb.f[:, 0:512], [pb.b], [memV])

    XSCALE = 256 ** -0.5
    mixT_v = mixT.rearrange("(c p) t -> p c t", p=128)
    for s in range(8):
        dma("sp", oxT.ap, mixT_v[:, :, s * 512:(s + 1) * 512], [mix_b[4 * s + j] for j in range(4)], oxT_b, "mixin")
        for j in range(4):
            tok0 = 4096 + (s * 4 + j) * 128
            dma("sp", X[j].ap, xall[tok0:tok0 + 128, :], [], [X[j]], "X%d" % j)
        for j in range(4):
            for hh in range(2):
                pb = P[1 + (2 * j + hh) % 2]
                for c in range(8):
                    mm(pb.f[:, 0:512], oxT.ap[:, c, j * 128:(j + 1) * 128], Wo.ap[:, c, hh * 512:(hh + 1) * 512], c == 0, c == 7,
                       [oxT_b[c], Wo], [pb.b])
                tt("dve", X[j].ap[:, hh * 512:(hh + 1) * 512], X[j].ap[:, hh * 512:(hh + 1) * 512], pb.f[:, 0:512], ALU.add,
                   [X[j], pb.b], [X[j]])
        norm4(g_xa)
        def xattn_lane(L):
            B = P[4 * L:4 * L + 4]
            ptx, rd_ = PTxs[L], rds[L]
            for h in (L, L + 2):
                for dd in range(2):
                    ch = 2 * h + dd
                    for c in range(8):
                        mm(B[0].f[:, 0:512], Wxq.ap[:, c, ch * 128:(ch + 1) * 128], hTs.ap[:, c, :], c == 0, c == 7, [Wxq] + hTs_b, [B[0].b])
                    cp("act", qxT.ap[:, ch, :], B[0].f[:, 0:512], [B[0].b], [qxT_b[ch]])
                for m in range(2):
                    pb = B[1 + m]
                    for dd in range(2):
                        mm(pb.f[:, 0:512], memKT.ap[:, 2 * h + dd, m * 128:(m + 1) * 128], qxT.ap[:, 2 * h + dd, :], dd == 0, dd == 1,
                           [memKT, qxT_b[2 * h + dd]], [pb.b])
                    act(ptx.ap[:, m, :], pb.f[:, 0:512], AF.Exp, [pb.b], [ptx], scale=XSCALE)
                for m in range(2):
                    mm(B[3].f[:, 0:512], ones_b.ap, ptx.ap[:, m, :], m == 0, m == 1, [ones_b, ptx], [B[3].b])
                act(rd_.ap, B[3].f[:, 0:512], AF.Ln, [B[3].b], [rd_])
                act(rd_.ap, rd_.ap, AF.Exp, [rd_], [rd_], scale=-1.0)
                for dd in range(2):
                    pb = B[1 + dd]
                    for m in range(2):
                        mm(pb.f[:, 0:512], memV.ap[:, m, (2 * h + dd) * 128:(2 * h + dd + 1) * 128], ptx.ap[:, m, :], m == 0, m == 1,
                           [memV, ptx], [pb.b])
                    tt("dve", oxT.ap[:, 2 * h + dd, :], pb.f[:, 0:512], rd_.ap, ALU.mult, [pb.b, rd_], [oxT_b[2 * h + dd]])

        lanes = []
        for L in range(2):
            S.begin(); xattn_lane(L); lanes.append(S.end())
        S.merged(lanes)
        for j in range(4):
            for hh in range(2):
                pb = P[1 + (2 * j + hh) % 2]
                for c in range(8):
                    mm(pb.f[:, 0:512], oxT.ap[:, c, j * 128:(j + 1) * 128], Wxo.ap[:, c, hh * 512:(hh + 1) * 512], c == 0, c == 7,
                       [oxT_b[c], Wxo], [pb.b])
                tt("dve", X[j].ap[:, hh * 512:(hh + 1) * 512], X[j].ap[:, hh * 512:(hh + 1) * 512], pb.f[:, 0:512], ALU.add,
                   [X[j], pb.b], [X[j]])
        norm4(g_mlp)
        for fc in range(8):
            w1 = W1[fc % 2]
            dma("sp", w1.ap.rearrange("p c n -> p (c n)"), ff1b[fc], [ff1b_b[fc]], [w1], "W1_%d" % (fc % 2))
            for sub in range(4):
                k_ = fc * 4 + sub
                pb = P[1 + k_ % 2]
                for c in range(8):
                    mm(pb.f[:, 0:512], w1.ap[:, c, sub * 128:(sub + 1) * 128], hTs.ap[:, c, :], c == 0, c == 7, [w1] + hTs_b, [pb.b])
                r_ = rl[k_ % 2]
                act(r_.ap, pb.f[:, 0:512], AF.Relu, [pb.b], [r_])
                tt("pool", hid.ap[:, k_, :], r_.ap, r_.ap, ALU.mult, [r_], [hid])
        for j2 in range(8):
            w2 = W2[j2 % 2]
            dma("sp", w2.ap.rearrange("p a n -> p (a n)"), ff2b[j2], [ff2b_b[j2]], [w2], "W2_%d" % (j2 % 2))
            for j in range(4):
                for hh in range(2):
                    pb = P[j * 2 + hh]
                    for sub in range(4):
                        mm(pb.f[:, 0:512], hid.ap[:, j2 * 4 + sub, j * 128:(j + 1) * 128], w2.ap[:, sub, hh * 512:(hh + 1) * 512],
                           (j2 == 0 and sub == 0), (j2 == 7 and sub == 3), [hid, w2], [pb.b])
        for j in range(4):
            for hh in range(2):
                pb = P[j * 2 + hh]
                tt("dve", X[j].ap[:, hh * 512:(hh + 1) * 512], X[j].ap[:, hh * 512:(hh + 1) * 512], pb.f[:, 0:512], ALU.add,
                   [X[j], pb.b], [X[j]])
        for j in range(4):
            sb_ = ssb[j]
            act(junk.ap, X[j].ap, AF.Square, [X[j]], [junk, sb_], accum=sb_.ap[:, 0:1])
            act(sb_.ap[:, 1:2], sb_.ap[:, 0:1], AF.Ln, [sb_, epsb], [sb_], bias=epsb.ap[:, 0:1], scale=1.0 / 1024)
            act(sb_.ap[:, 2:3], sb_.ap[:, 1:2], AF.Exp, [sb_], [sb_], scale=-0.5)
            stt("dve", X[j].ap, X[j].ap, sb_.ap[:, 2:3], g_fin.ap, ALU.mult, ALU.mult, [X[j], sb_, g_fin], [X[j]])
            r0 = (s * 4 + j) * 128
            dma("sp", out[r0:r0 + 128, :], X[j].ap, [X[j]], [], "X%d" % j, is_out=True)

    S.emit(st)
    st.close()
    return nc


_NC_CACHE = {}


def _host_consts():
    ar = np.arange(128)
    U = (ar[:, None] <= ar[None, :]).astype(np.float32)
    L = (ar[:, None] > ar[None, :]).astype(np.float32)
    mask4 = np.tile(U, (1, 4)).astype(np.float32)
    tri = np.where(ar[None, :] <= ar[:, None], 0.0, NEG).astype(np.float32)
    return {
        "cU": U, "cL": L, "cMask4": mask4,
        "cTriB": tri.astype(ml_dtypes.bfloat16),
        "cIdent": np.eye(128, dtype=np.float32).astype(ml_dtypes.bfloat16),
    }


def _rot_table(positions):
    half = 16
    inv_freq = (np.float32(500000.0) ** (-np.arange(half, dtype=np.float32) * np.float32(2.0) / np.float32(32))).astype(np.float32)
    ang = positions.astype(np.float32)[:, None] * inv_freq[None, :]
    return np.concatenate([np.cos(ang), np.sin(ang)], axis=1).astype(np.float32)


def make_in_maps(inputs):
    x = np.asarray(inputs["x"], dtype=np.float32)
    memv = np.asarray(inputs["mem"], dtype=np.float32)
    consts = _host_consts()
    shared = {
        "w_in": np.ascontiguousarray(inputs["w_in"][0]),
        "w_out": np.ascontiguousarray(inputs["w_out"][0]),
        "w_xq": np.ascontiguousarray(inputs["w_xq"][0]),
        "w_xk": np.ascontiguousarray(inputs["w_xk"][0]),
        "w_xv": np.ascontiguousarray(inputs["w_xv"][0]),
        "w_xo": np.ascontiguousarray(inputs["w_xo"][0]),
        "w_ff1": np.ascontiguousarray(inputs["w_ff1"][0]),
        "w_ff2": np.ascontiguousarray(inputs["w_ff2"][0]),
        "norm_mix": np.ascontiguousarray(inputs["norm_mix"][0:1]),
        "norm_xattn": np.ascontiguousarray(inputs["norm_xattn"][0:1]),
        "norm_mem": np.ascontiguousarray(inputs["norm_mem"][0:1]),
        "norm_mlp": np.ascontiguousarray(inputs["norm_mlp"][0:1]),
        "norm_final": np.ascontiguousarray(np.asarray(inputs["norm_final"]).reshape(1, 1024)),
        "hgrn_norm": np.ascontiguousarray(inputs["hgrn_norm"][0:1]),
        "lb_logits": np.ascontiguousarray(inputs["lb_logits"]),
    }
    shared = {k: np.asarray(v, dtype=np.float32) for k, v in shared.items()}
    shared.update(consts)
    in_maps = []
    for c in range(8):
        b, hf = c // 2, c % 2
        xall = np.zeros((8192, 1024), np.float32)
        if hf == 1:
            xall[:4096] = x[b, :4096]
        xall[4096:] = x[b, hf * 4096:(hf + 1) * 4096]
        pos = np.concatenate([np.arange(4096), hf * 4096 + np.arange(4096)])
        vb = np.full((16, 32), NEG, np.float32)
        for j in range(16):
            lo = 16 if hf == 0 else 0
            vb[j, lo:16 + j] = 0.0
        m = dict(shared)
        m["xall"] = xall
        m["mem"] = np.ascontiguousarray(memv[b])
        m["rot"] = _rot_table(pos)
        m["vbias"] = vb.reshape(1, 512)
        in_maps.append(m)
    return in_maps


def kernel(**inputs):
    if "nc" not in _NC_CACHE:
        _NC_CACHE["nc"] = build_nc()
    nc = _NC_CACHE["nc"]
    in_maps = make_in_maps(inputs)
    res = run_bass_kernel_spmd(nc, in_maps, core_ids=list(range(8)))
    outp = np.zeros((4, 8192, 1024), np.float32)
    for c in range(8):
        b, hf = c // 2, c % 2
        outp[b, hf * 4096:(hf + 1) * 4096] = np.asarray(res.results[c]["out"], dtype=np.float32)
    return outp
```
